# Optimizing a Trainium2 kernel written in Bass

```python
import math
import jax, jax.numpy as jnp
from jax import lax
import numpy as np

D_MODEL = 2048
BATCH = 4
SEQ = 2048
DEPTH = 1
DEC_BATCH = 32
DEC_SEQ = 1
PAST_LEN = 8192
PAGE_SIZE = 128

N_META = 16
HG_WIDTH = D_MODEL // 2
AT_WIDTH = D_MODEL - HG_WIDTH
HG_DK = 128
HG_DV = 128
HG_HEADS = HG_WIDTH // HG_DV
HG_KW = HG_HEADS * HG_DK
AT_DQK = 64
AT_DV = 2 * AT_DQK
AT_HEADS = AT_WIDTH // AT_DV
AT_QKW = AT_HEADS * 2 * AT_DQK
N_BUCKETS = 32
MAX_DISTANCE = 128
MAX_EXACT = N_BUCKETS // 2
CHUNK = 64
Q_BLOCK = 128
RMS_EPS = 1e-6
NEG_INF = -1e30
PROJ_WIDTHS = [HG_KW, HG_KW, HG_WIDTH, HG_WIDTH, AT_QKW, AT_QKW, AT_WIDTH, AT_WIDTH]
D_IN = sum(PROJ_WIDTHS)
SPLIT_OFFSETS = [int(v) for v in np.cumsum(PROJ_WIDTHS)[:-1]]

kernel_name = "hymba_hgrn2_diffattn_decode_step"


def rmsnorm(x, g):
    xf = x.astype(jnp.float32)
    y = xf * lax.rsqrt(jnp.mean(xf * xf, axis=-1, keepdims=True) + RMS_EPS) * g.astype(jnp.float32)
    return y.astype(x.dtype)


def lambda_init(layer):
    return 0.8 - 0.6 * math.exp(-0.3 * layer)


def project(h, norm_g, w_in, lb):
    B, T = h.shape[:2]
    z = rmsnorm(h, norm_g) @ w_in
    hq, hf, hi, hg, aq, ak, av, ag = jnp.split(z, SPLIT_OFFSETS, axis=-1)
    f = lb + (1.0 - lb) * jax.nn.sigmoid(hf.astype(jnp.float32))
    log_f = jnp.log(f).reshape(B, T, HG_HEADS, HG_DK)
    hk = (1.0 - f).reshape(B, T, HG_HEADS, HG_DK)
    hq = hq.reshape(B, T, HG_HEADS, HG_DK)
    hv = hi.reshape(B, T, HG_HEADS, HG_DV)
    aq = aq.reshape(B, T, AT_HEADS, 2 * AT_DQK)
    ak = ak.reshape(B, T, AT_HEADS, 2 * AT_DQK)
    av = av.reshape(B, T, AT_HEADS, AT_DV)
    return hq, hk, hv, log_f, hg, aq, ak, av, ag


def hgrn2_chunked(q, k, v, log_f, s0, chunk):
    B, L, H, DK = q.shape
    n = L // chunk
    def blocks(a):
        a = a.astype(jnp.float32)
        return jnp.moveaxis(a.reshape(B, n, chunk, H, a.shape[-1]), 1, 0)
    mask = jnp.tril(jnp.ones((chunk, chunk), dtype=bool))
    def step(S, inp):
        qc, kc, vc, gc = inp
        b = jnp.cumsum(gc, axis=1)
        qe = qc * jnp.exp(b)
        ke = kc * jnp.exp(-b)
        A = jnp.einsum('bthk,bshk->bhts', qe, ke)
        A = jnp.where(mask, A, 0.0)
        o = jnp.einsum('bthk,bhkv->bthv', qe, S) + jnp.einsum('bhts,bshv->bthv', A, vc)
        b_last = b[:, -1]
        S_new = jnp.exp(b_last)[..., None] * S + jnp.einsum(
            'bshk,bshv->bhkv', kc * jnp.exp(b_last[:, None] - b), vc)
        return S_new, o
    S, o = lax.scan(step, s0.astype(jnp.float32), (blocks(q), blocks(k), blocks(v), blocks(log_f)))
    o = jnp.moveaxis(o, 0, 1).reshape(B, L, H, v.shape[-1])
    return S, o


def rel_bucket(dist):
    n = jnp.maximum(dist, 0)
    nl = jnp.maximum(n, MAX_EXACT).astype(jnp.float32)
    large = MAX_EXACT + (jnp.log(nl / MAX_EXACT) / math.log(MAX_DISTANCE / MAX_EXACT)
                         * (N_BUCKETS - MAX_EXACT)).astype(jnp.int32)
    large = jnp.minimum(large, N_BUCKETS - 1)
    return jnp.where(n < MAX_EXACT, n, large)


def diff_attend(q, k, v, q_pos, k_pos, rel_bias, lam):
    B, Tq, H, _ = q.shape
    Tk = k.shape[1]
    qf = q.astype(jnp.float32).reshape(B, Tq, H, 2, AT_DQK)
    kf = k.astype(jnp.float32).reshape(B, Tk, H, 2, AT_DQK)
    s = jnp.einsum('bqhcd,bkhcd->bchqk', qf, kf) * (AT_DQK ** -0.5)
    dist = q_pos[:, None] - k_pos[None, :]
    bias = jnp.transpose(rel_bias[rel_bucket(dist)].astype(jnp.float32), (2, 0, 1))
    s = jnp.where(dist >= 0, s + bias, NEG_INF)
    p = jax.nn.softmax(s, axis=-1)
    w = p[:, 0] - lam * p[:, 1]
    return jnp.einsum('bhqk,bkhv->bqhv', w, v.astype(jnp.float32))


def merge(h, o_hg, g_hg, o_at, g_at, hg_norm_g, at_norm_g, lam_init, w_out):
    B, T = h.shape[:2]
    o_hg = rmsnorm(o_hg, hg_norm_g).reshape(B, T, HG_WIDTH) * jax.nn.silu(g_hg.astype(jnp.float32))
    o_at = (rmsnorm(o_at, at_norm_g) * (1.0 - lam_init)).reshape(B, T, AT_WIDTH) * jax.nn.silu(g_at.astype(jnp.float32))
    mixed = jnp.concatenate([o_hg, o_at], axis=-1).astype(h.dtype)
    return h + mixed @ w_out


def setup_inputs(seed: int = 0) -> dict:
    key = jax.random.key(seed)
    ks = jax.random.split(key, 16)
    n_pages = PAST_LEN // PAGE_SIZE
    n_phys = (5 * DEC_BATCH * n_pages + 3) // 4
    perm = jax.random.permutation(ks[5], n_phys)[:DEC_BATCH * n_pages]
    page_table = perm.reshape(DEC_BATCH, n_pages).astype(jnp.int32)
    f32 = jnp.float32
    return {
        "x_prompt": jax.random.normal(ks[0], (BATCH, SEQ, D_MODEL), f32),
        "x_sample": jax.random.normal(ks[1], (DEC_BATCH, DEC_SEQ, D_MODEL), f32),
        "cache_k": jax.random.normal(ks[2], (DEPTH, n_phys, PAGE_SIZE, AT_HEADS, 2 * AT_DQK), f32),
        "cache_v": jax.random.normal(ks[3], (DEPTH, n_phys, PAGE_SIZE, AT_HEADS, AT_DV), f32),
        "state_hgrn": 0.3 * jax.random.normal(ks[4], (DEPTH, DEC_BATCH, HG_HEADS, HG_DK, HG_DV), f32),
        "page_table": page_table,
        "meta_tokens": jax.random.normal(ks[6], (N_META, D_MODEL), f32),
        "rel_bias": 0.1 * jax.random.normal(ks[7], (N_BUCKETS, AT_HEADS), f32),
        "hgrn_lb": 0.1 * jax.random.normal(ks[8], (DEPTH + 1, HG_KW), f32),
        "norm_g": 1.0 + 0.02 * jax.random.normal(ks[9], (DEPTH, D_MODEL), f32),
        "w_in": jax.random.normal(ks[10], (DEPTH, D_MODEL, D_IN), f32) * D_MODEL ** -0.5,
        "hgrn_norm_g": 1.0 + 0.02 * jax.random.normal(ks[11], (DEPTH, HG_DV), f32),
        "diff_norm_g": 1.0 + 0.02 * jax.random.normal(ks[12], (DEPTH, AT_DV), f32),
        "diff_lambda": 0.1 * jax.random.normal(ks[13], (DEPTH, 4, AT_DQK), f32),
        "w_out": jax.random.normal(ks[14], (DEPTH, D_MODEL, D_MODEL), f32) * D_MODEL ** -0.5,
        "final_norm_g": 1.0 + 0.02 * jax.random.normal(ks[15], (D_MODEL,), f32),
    }


def reference(x_prompt, x_sample, cache_k, cache_v, state_hgrn, page_table, meta_tokens, rel_bias,
              hgrn_lb, norm_g, w_in, hgrn_norm_g, diff_norm_g, diff_lambda, w_out, final_norm_g):
    B = x_prompt.shape[0]
    DB = x_sample.shape[0]
    T = N_META + SEQ
    n_blk = SEQ // Q_BLOCK
    lb_table = jnp.cumsum(jax.nn.softmax(hgrn_lb.astype(jnp.float32), axis=0), axis=0)
    pos_p = jnp.arange(T, dtype=jnp.int32)
    pos_q_blocks = pos_p[N_META:].reshape(n_blk, Q_BLOCK)
    pos_s_q = PAST_LEN + jnp.arange(DEC_SEQ, dtype=jnp.int32)
    pos_s_k = jnp.arange(PAST_LEN + DEC_SEQ, dtype=jnp.int32)

    h_p = jnp.concatenate([jnp.broadcast_to(meta_tokens[None].astype(x_prompt.dtype), (B, N_META, D_MODEL)),
                           x_prompt], axis=1)
    h_s = x_sample
    kp_l, vp_l, sp_l, ks_l, vs_l, ss_l = [], [], [], [], [], []
    for l in range(DEPTH):
        lam_i = lambda_init(l)
        lp = diff_lambda[l].astype(jnp.float32)
        lam = jnp.exp(jnp.sum(lp[0] * lp[1])) - jnp.exp(jnp.sum(lp[2] * lp[3])) + lam_i
        lb = lb_table[l]

        hq, hk, hv, lf, hg, aq, ak, av, ag = project(h_p, norm_g[l], w_in[l], lb)
        s0 = jnp.zeros((B, HG_HEADS, HG_DK, HG_DV), jnp.float32)
        s_meta, o_meta = hgrn2_chunked(hq[:, :N_META], hk[:, :N_META], hv[:, :N_META], lf[:, :N_META], s0, N_META)
        s_p, o_real = hgrn2_chunked(hq[:, N_META:], hk[:, N_META:], hv[:, N_META:], lf[:, N_META:], s_meta, CHUNK)
        o_hg_p = jnp.concatenate([o_meta, o_real], axis=1)
        o_at_meta = diff_attend(aq[:, :N_META], ak[:, :N_META], av[:, :N_META],
                                pos_p[:N_META], pos_p[:N_META], rel_bias, lam)
        qb = jnp.moveaxis(aq[:, N_META:].reshape(B, n_blk, Q_BLOCK, AT_HEADS, 2 * AT_DQK), 1, 0)
        ob = lax.map(lambda a: diff_attend(a[0], ak, av, a[1], pos_p, rel_bias, lam), (qb, pos_q_blocks))
        o_at_real = jnp.moveaxis(ob, 0, 1).reshape(B, SEQ, AT_HEADS, AT_DV)
        o_at_p = jnp.concatenate([o_at_meta, o_at_real], axis=1)
        h_p = merge(h_p, o_hg_p, hg, o_at_p, ag, hgrn_norm_g[l], diff_norm_g[l], lam_i, w_out[l])
        kp_l.append(ak)
        vp_l.append(av)
        sp_l.append(s_p)

        sq, sk, sv, slf, sg, bq, bk, bv, bg = project(h_s, norm_g[l], w_in[l], lb)
        s_s, o_hg_s = hgrn2_chunked(sq, sk, sv, slf, state_hgrn[l], DEC_SEQ)
        k_past = cache_k[l][page_table].reshape(DB, PAST_LEN, AT_HEADS, 2 * AT_DQK)
        v_past = cache_v[l][page_table].reshape(DB, PAST_LEN, AT_HEADS, AT_DV)
        k_all = jnp.concatenate([k_past, bk.astype(k_past.dtype)], axis=1)
        v_all = jnp.concatenate([v_past, bv.astype(v_past.dtype)], axis=1)
        o_at_s = diff_attend(bq, k_all, v_all, pos_s_q, pos_s_k, rel_bias, lam)
        h_s = merge(h_s, o_hg_s, sg, o_at_s, bg, hgrn_norm_g[l], diff_norm_g[l], lam_i, w_out[l])
        ks_l.append(bk)
        vs_l.append(bv)
        ss_l.append(s_s)

    y_prompt = rmsnorm(h_p, final_norm_g)[:, N_META:]
    y_sample = rmsnorm(h_s, final_norm_g)
    k_prompt = jnp.stack(kp_l)
    v_prompt = jnp.stack(vp_l)
    s_prompt = jnp.stack(sp_l)
    k_sample = jnp.stack(ks_l)
    v_sample = jnp.stack(vs_l)
    s_sample = jnp.stack(ss_l)
    return (y_prompt, y_sample, k_prompt, v_prompt, s_prompt, k_sample, v_sample, s_sample)
```

```python
import math
from contextlib import ExitStack
import numpy as np
import concourse.bass as bass
import concourse.mybir as mybir
from concourse.bass_utils import run_bass_kernel_spmd

F32 = mybir.dt.float32
BF16 = mybir.dt.bfloat16
I32 = mybir.dt.int32
AF = mybir.ActivationFunctionType
ALU = mybir.AluOpType
AX = mybir.AxisListType

D = 2048
KC = 16
NR = 2048
NMETA = 16
NDEC = 8
NT = NR + NMETA + NDEC
C_META = NR
C_DEC = NR + NMETA
NSPEC = NMETA + NDEC
RANGES = [(0, 512), (512, 512), (1024, 512), (1536, 512), (2048, NSPEC)]
NPAGE = 64
NPHYS = 2560
EPS = 1e-6
NEG = -30000.0
LAM_INIT = 0.8 - 0.6 * math.exp(-0.3 * 0)
ENGS = ["pe", "act", "dve", "pool", "sp"]


class Prog:
    def __init__(self, nc):
        self.nc = nc
        self.h = {"pe": nc.tensor, "act": nc.scalar, "dve": nc.vector, "pool": nc.gpsimd, "sp": nc.sync}
        self.sem = {e: nc.alloc_semaphore(name=f"eng_{e}") for e in ENGS}
        self.pending = {e: [] for e in ENGS}
        self.base = {e: 0 for e in ENGS}
        self.sigcount = {e: 0 for e in ENGS}
        self.sigpos = {e: {} for e in ENGS}
        self.res = {}
        self.chan = {}
        self.waited = {e: {} for e in ENGS}

    def _deps(self, eng, r, w):
        def same(t):
            return False
        deps = set()
        for k in r:
            st = self.res.get(k)
            if st and st[0] is not None:
                deps.add(st[0])
        for k in w:
            st = self.res.get(k)
            if st:
                if st[0] is not None and not same(st[0]):
                    deps.add(st[0])
                for t in st[1].values():
                    if not same(t):
                        deps.add(t)
        return deps

    def _update(self, tok, rkey, r, w):
        for k in r:
            st = self.res.setdefault(k, [None, {}])
            st[1][rkey] = tok
        for k in w:
            self.res[k] = [tok, {}]

    def op(self, eng, fn, r=(), w=()):
        w = list(w) + [k for k in r if isinstance(k, tuple) and k[0] == "ps" and k not in w]
        deps = self._deps(eng, r, w)
        gidx = self.base[eng] + len(self.pending[eng])
        tok = ("e", eng, gidx)
        if eng == "pe":
            deps = {d for d in deps if not (d[0] == "e" and d[1] == "pe")}
        self.pending[eng].append(dict(fn=fn, deps=deps, sig=False, chan=None))
        self._update(tok, ("e", eng), r, w)

    def dma(self, eng, fn, r=(), w=(), chan=None, wait_all=False):
        deps = self._deps(eng, r, w)
        if chan not in self.chan:
            self.chan[chan] = [self.nc.alloc_semaphore(name=f"ch_{chan}"), 0]
        c = self.chan[chan]
        c[1] += 1
        tok = ("dall", chan) if wait_all else ("d", chan, 16 * c[1])
        deps = {d for d in deps if not (d[0] == "dall" and d[1] == chan)}
        self.pending[eng].append(dict(fn=fn, deps=deps, sig=False, chan=chan))
        self._update(tok, ("d", chan), r, w)

    def flush(self, barrier=False):
        if barrier:
            for e in ENGS:
                for o in reversed(self.pending[e]):
                    if o["chan"] is None:
                        o["sig"] = True
                        break
        def mark(tok):
            if tok[0] == "e":
                loc = tok[2] - self.base[tok[1]]
                if loc >= 0:
                    self.pending[tok[1]][loc]["sig"] = True
        for e in ENGS:
            for o in self.pending[e]:
                for d in o["deps"]:
                    mark(d)
        for st in self.res.values():
            if st[0] is not None:
                mark(st[0])
            for t in st[1].values():
                mark(t)
        for e in ENGS:
            cnt = self.sigcount[e]
            for i, o in enumerate(self.pending[e]):
                if o["sig"] and o["chan"] is None:
                    cnt += 1
                    self.sigpos[e][self.base[e] + i] = cnt
            self.sigcount[e] = cnt

        def resolve(tok):
            if tok[0] == "e":
                return self.sem[tok[1]], self.sigpos[tok[1]][tok[2]]
            if tok[0] == "d":
                return self.chan[tok[1]][0], tok[2]
            return self.chan[tok[1]][0], 16 * self.chan[tok[1]][1]

        finals = {e: self.sigcount[e] for e in ENGS}
        if barrier:
            self.sigcount["sp"] += 1
        sp_final = self.sigcount["sp"]

        def emit(e, eh):
            wd = self.waited[e]
            for o in self.pending[e]:
                need = {}
                for d in o["deps"]:
                    s, v = resolve(d)
                    key = id(s)
                    if wd.get(key, (None, 0))[1] >= v:
                        continue
                    if key not in need or need[key][1] < v:
                        need[key] = (s, v)
                for key, (s, v) in need.items():
                    eh.wait_ge(s, v)
                    wd[key] = (s, v)
                ins = o["fn"](eh)
                if o["chan"] is not None:
                    ins.then_inc(self.chan[o["chan"]][0], 16)
                elif o["sig"]:
                    ins.then_inc(self.sem[e], 1)
            if barrier:
                if e == "sp":
                    for ch, (sm, cnt) in self.chan.items():
                        if cnt > 0:
                            eh.wait_ge(sm, 16 * cnt)
                    for e2 in ENGS:
                        if e2 != "sp" and finals[e2] > 0:
                            eh.wait_ge(self.sem[e2], finals[e2])
                    eh.nop().then_inc(self.sem["sp"], 1)
                else:
                    eh.wait_ge(self.sem["sp"], sp_final)

        with self.nc.Block() as block:
            @block.tensor
            def _(eh):
                emit("pe", eh)

            @block.scalar
            def _(eh):
                emit("act", eh)

            @block.vector
            def _(eh):
                emit("dve", eh)

            @block.gpsimd
            def _(eh):
                emit("pool", eh)

            @block.sync
            def _(eh):
                emit("sp", eh)
        for e in ENGS:
            self.base[e] += len(self.pending[e])
            self.pending[e] = []

    def final_wait(self, keys):
        toks = set()
        for k in keys:
            st = self.res.get(k)
            if st and st[0] is not None:
                toks.add(st[0])
        self.pending["sp"].append(dict(fn=lambda eh: eh.nop(), deps=toks, sig=False, chan=None))


class _Stop(Exception):
    pass


def build_program(nphys=NPHYS, stop=None):
    nc = bass.Bass("TRN2", target_bir_lowering=False)
    P = Prog(nc)
    stacks = []
    try:
        _build_body(nc, P, stacks, nphys, stop)
    except _Stop:
        P.flush(barrier=True)
    for es in reversed(stacks):
        es.close()
    return nc


def _build_body(nc, P, stacks, nphys, stop):
    def ExitStackT():
        es = ExitStack()
        stacks.append(es)
        return es

    def chk(tag):
        if stop == tag:
            raise _Stop()

    def din(name, shape, dt=F32):
        return nc.dram_tensor(name, list(shape), dt, kind="ExternalInput")

    def dout(name, shape, dt=F32):
        return nc.dram_tensor(name, list(shape), dt, kind="ExternalOutput")

    X = din("X", [NT, D]).ap()
    Win = din("Win", [D, 4096]).ap()
    Wout = din("Wout", [D, D]).ap()
    Xres = din("Xres", [1028, D]).ap()
    normg_d = din("normg", [1, D]).ap()
    lbraw_d = din("lbraw", [128, 8]).ap()
    hgng_d = din("hgng", [128, 1]).ap()
    dfng_d = din("dfng", [128, 1]).ap()
    fng_d = din("fng", [1, D]).ap()
    dlam_d = din("dlam", [1, 256]).ap()
    rbx_d = din("rbx", [33, 4]).ap()
    pt_d = din("pt", [1, 512], I32).ap()
    ck_d = din("ck", [nphys * 128, 512]).ap()
    cv_d = din("cv", [nphys * 128, 512]).ap()
    sh_d = din("sh", [NDEC, 4, 128, 128]).ap()
    c_sq_d = din("c_sq", [128, 3, 128]).ap()
    c_j16_d = din("c_j16", [16, 16]).ap()
    c_maskA_d = din("c_maskA", [128, 2, 64]).ap()
    c_maskAm_d = din("c_maskAm", [NSPEC, 16]).ap()
    c_mcol_d = din("c_mcol", [128, 4]).ap()
    c_resetm_d = din("c_resetm", [1, NT]).ap()
    c_oh2_d = din("c_oh2", [33, 384]).ap()
    c_ohd_d = din("c_ohd", [33, 3, 128]).ap()
    c_selv_d = din("c_selv", [NSPEC, NDEC, 128]).ap()
    c_p01_d = din("c_p01", [8, 8]).ap()
    c_idxm_d = din("c_idxm", [128, 2], I32).ap()

    yp_d = dout("yp", [1024, D]).ap()
    ys_d = dout("ys", [4, D]).ap()
    kp_d = dout("kp", [NR + NMETA, 512]).ap()
    vp_d = dout("vp", [NR + NMETA, 512]).ap()
    sp_d = dout("sp", [4, 128, 128]).ap()
    ks_d = dout("ks", [NDEC, 512]).ap()
    vs_d = dout("vs", [NDEC, 512]).ap()
    ss_d = dout("ss", [NDEC, 4, 128, 128]).ap()

    u_scr = nc.dram_tensor("u_scr", [4, 384], F32)
    snd_d = nc.dram_tensor("snd", [1024, 2056], BF16).ap()
    rcv_ds = [nc.dram_tensor(f"rcv{i}", [256, 2056], BF16).ap() for i in range(8)]

    def mk(es):
        return lambda name, shape, dt: es.enter_context(nc.sbuf_tensor("s_" + name, list(shape), dt))

    es_p = ExitStackT()
    sb = mk(es_p)
    es_0 = ExitStackT()
    sb0 = mk(es_0)
    identb = sb("identb", [128, 128], BF16)
    Jb = sb("Jb", [128, 128], BF16)
    J16b = sb("J16b", [16, 16], BF16)
    onesb = sb("onesb", [128, 128], BF16)
    onesf = sb("onesf", [128, 128], F32)
    maskA = sb("maskA", [128, 2, 64], F32)
    maskAm = sb("maskAm", [NSPEC, 16], F32)
    mcol = sb("mcol", [128, 4], F32)
    resetm = sb("resetm", [128, NT], BF16)
    lbt = sb("lbt", [128, 4], F32)
    omlt = sb("omlt", [128, 4], F32)
    hgng = sb("hgng", [128, 1], F32)
    dfng = sb("dfng", [128, 1], F32)
    cn = sb("cn", [128, 1], F32)
    lamt = sb("lamt", [128, 2], F32)
    rbx = sb("rbx", [33, 4], F32)
    selv = sb("selv", [NSPEC, NDEC, 128], F32)
    comb = sb("comb", [8, 4], F32)
    Hm = sb("Hm", [128, 4, 2, 128], BF16)
    b31t = sb("b31t", [128, 4], F32)
    bsm = sb("bsm", [128, 3, 4], F32)
    idxg = sb("idxg", [128, 512], I32)
    idxm = sb("idxm", [128, 2], I32)
    kspec = sb("kspec", [NSPEC, 4, 128], F32)
    vspec = sb("vspec", [NSPEC, 4, 128], F32)
    qspec = sb("qspec", [NSPEC, 4, 128], F32)
    gdec = sb("gdec", [128, 4, NDEC], F32)
    odec = sb("odec", [128, 4, NDEC], F32)

    csq = sb0("csq", [128, 3, 128], F32)
    cj16 = sb0("cj16", [16, 16], F32)
    lbraw = sb0("lbraw", [128, 8], F32)
    dlam = sb0("dlam", [1, 256], F32)
    lsm = sb0("lsm", [1, 136], F32)
    oh2 = sb0("oh2", [33, 384], F32)
    ohd = sb0("ohd", [33, 3, 128], F32)
    p01 = sb0("p01", [8, 8], F32)
    ut = sb0("ut", [4, 384], F32)
    ptb = sb0("ptb", [128, 512], I32)
    ptf = sb0("ptf", [128, 512], F32)

    PS = [es_p.enter_context(nc.psum_tensor(f"ps{k}", [128, 512], F32)) for k in range(8)]

    def psk(k):
        return ("ps", k)

    def cload(dst, src, name, eng="sp"):
        P.dma(eng, lambda e, d=dst, s=src: e.dma_start(out=d, in_=s), w=[name], chan="const" if eng == "sp" else "constp", wait_all=True)

    cload(csq[:], c_sq_d, "csq")
    cload(cj16[:], c_j16_d, "cj16")
    cload(maskA[:], c_maskA_d, "maskA")
    cload(maskAm[:], c_maskAm_d, "maskAm")
    cload(mcol[:], c_mcol_d, "mcol")
    cload(resetm[:], c_resetm_d.partition_broadcast(128), "resetm", eng="pool")
    cload(lbraw[:], lbraw_d, "lbraw")
    cload(hgng[:], hgng_d, "hgng")
    cload(dfng[:], dfng_d, "dfng")
    cload(dlam[:], dlam_d, "dlam")
    cload(rbx[:], rbx_d, "rbx")
    cload(oh2[:], c_oh2_d, "oh2")
    cload(ohd[:], c_ohd_d, "ohd")
    cload(selv[:], c_selv_d, "selv")
    cload(p01[:], c_p01_d, "p01")
    cload(b31t[:], rbx_d[31:32, :].partition_broadcast(128), "b31t")
    cload(ptb[:], pt_d.partition_broadcast(128), "ptb")
    cload(idxm[:], c_idxm_d, "idxm")

    P.op("dve", lambda e: e.tensor_copy(out=identb[:], in_=csq[:, 0, :]), r=["csq"], w=["identb"])
    P.op("dve", lambda e: e.tensor_copy(out=Jb[:], in_=csq[:, 1, :]), r=["csq"], w=["Jb"])
    P.op("dve", lambda e: e.tensor_copy(out=J16b[:], in_=cj16[:]), r=["cj16"], w=["J16b"])
    P.op("pool", lambda e: e.memset(onesb[:], 1.0), w=["onesb"])
    P.op("pool", lambda e: e.memset(onesf[:], 1.0), w=["onesf"])
    P.op("dve", lambda e: e.tensor_tensor(out=lbt[:], in0=lbraw[:, 0:4], in1=lbraw[:, 4:8], op=ALU.subtract),
         r=["lbraw"], w=["lbt"])
    P.op("act", lambda e: e.activation(out=lbt[:], in_=lbt[:], func=AF.Sigmoid), r=["lbt"], w=["lbt"])
    P.op("dve", lambda e: e.tensor_scalar(out=omlt[:], in0=lbt[:], scalar1=-1.0, scalar2=1.0, op0=ALU.mult, op1=ALU.add),
         r=["lbt"], w=["omlt"])
    P.op("dve", lambda e: e.tensor_scalar(out=cn[:], in0=dfng[:], scalar1=float(1.0 - LAM_INIT), scalar2=None, op0=ALU.mult),
         r=["dfng"], w=["cn"])
    dl4 = dlam[:].rearrange("o (a b d) -> o a b d", a=2, b=2)
    P.op("dve", lambda e: e.tensor_tensor(out=lsm[:, 0:128].rearrange("o (a d) -> o a d", a=2), in0=dl4[:, :, 0, :],
                                          in1=dl4[:, :, 1, :], op=ALU.mult), r=["dlam"], w=["lsm"])
    P.op("dve", lambda e: e.tensor_reduce(out=lsm[:, 128:130], in_=lsm[:, 0:128].rearrange("o (a d) -> o a d", a=2),
                                          axis=AX.X, op=ALU.add), r=["lsm"], w=["lsm"])
    P.op("act", lambda e: e.activation(out=lsm[:, 130:132], in_=lsm[:, 128:130], func=AF.Exp), r=["lsm"], w=["lsm"])
    P.op("dve", lambda e: e.tensor_tensor(out=lsm[:, 132:133], in0=lsm[:, 130:131], in1=lsm[:, 131:132], op=ALU.subtract),
         r=["lsm"], w=["lsm"])
    P.op("dve", lambda e: e.tensor_scalar(out=lsm[:, 133:134], in0=lsm[:, 132:133], scalar1=float(LAM_INIT), scalar2=None,
                                          op0=ALU.add), r=["lsm"], w=["lsm"])
    P.op("pe", lambda e: e.matmul(PS[0][:, 0:1], lhsT=onesf[0:1, :], rhs=lsm[:, 133:134], start=True, stop=True),
         r=["onesf", "lsm"], w=[psk(0)])
    P.op("dve", lambda e: e.tensor_copy(out=lamt[:, 0:1], in_=PS[0][:, 0:1]), r=[psk(0)], w=["lamt"])
    P.op("dve", lambda e: e.tensor_scalar(out=lamt[:, 1:2], in0=lamt[:, 0:1], scalar1=-1.0, scalar2=None, op0=ALU.mult),
         r=["lamt"], w=["lamt"])
    P.op("dve", lambda e: e.scalar_tensor_tensor(out=comb[:], in0=p01[:, 4:8], scalar=lamt[0:8, 1:2], in1=p01[:, 0:4],
                                                 op0=ALU.mult, op1=ALU.add), r=["p01", "lamt"], w=["comb"])
    P.op("pe", lambda e: e.matmul(PS[1][0:4, 0:384], lhsT=rbx[:], rhs=oh2[:], start=True, stop=True),
         r=["rbx", "oh2"], w=[psk(1)])
    P.op("dve", lambda e: e.tensor_copy(out=ut[:], in_=PS[1][0:4, 0:384]), r=[psk(1)], w=["ut"])
    P.dma("sp", lambda e: e.dma_start(out=u_scr.ap(), in_=ut[:]), r=["ut"], w=["u_scr"], chan="uscr")
    for h in range(4):
        for oi in range(2):
            P.dma("pool", lambda e, h=h, oi=oi: e.dma_start(
                out=Hm[:, h, oi, :], in_=bass.AP(u_scr, h * 384 + 128 * oi, [[1, 128], [1, 128]])),
                r=["u_scr"], w=["Hm"], chan="hm", wait_all=True)
    for kind in range(3):
        P.op("pe", lambda e, kind=kind: e.matmul(PS[2][:, 4 * kind:4 * kind + 4], lhsT=ohd[:, kind, :], rhs=rbx[:],
                                                 start=True, stop=True), r=["ohd", "rbx"], w=[psk(2)])
    P.op("dve", lambda e: e.tensor_copy(out=bsm[:].rearrange("p a b -> p (a b)"), in_=PS[2][:, 0:12]), r=[psk(2)], w=["bsm"])
    P.op("dve", lambda e: e.tensor_copy(out=ptf[:], in_=ptb[:]), r=["ptb"], w=["ptf"])
    P.op("dve", lambda e: e.tensor_scalar(out=ptf[:], in0=ptf[:], scalar1=128.0, scalar2=mcol[:, 3:4], op0=ALU.mult, op1=ALU.add),
         r=["ptf", "mcol"], w=["ptf"])
    P.op("dve", lambda e: e.tensor_copy(out=idxg[:], in_=ptf[:]), r=["ptf"], w=["idxg"])
    P.flush(barrier=True)
    es_0.close()
    chk("p0")
    es_m = ExitStackT()
    mixedT = mk(es_m)("mixedT", [128, 8, 2056], BF16)
    es_a = ExitStackT()
    sb = mk(es_a)

    xnT = sb("xnT", [128, KC, NT], BF16)
    wblk = [sb(f"wblk{s}", [128, KC, 128], BF16) for s in range(3)]
    wctr = [0]
    es_a1 = ExitStackT()
    sb = mk(es_a1)
    normg_bc = sb("normg_bc", [128, D], F32)
    xin = [sb(f"xin{s}", [128, D], F32) for s in range(2)]
    xnb = [sb(f"xnb{s}", [128, D], BF16) for s in range(2)]
    junk = sb("junk", [128, D], BF16)
    stat = sb("stat", [128, 17, 4], F32)

    P.dma("sp", lambda e: e.dma_start(out=normg_bc[:], in_=normg_d.partition_broadcast(128)), w=["normg_bc"], chan="ngbc")

    for i in range(17):
        s = i % 2
        rows = 128 if i < 16 else NSPEC
        r0 = 128 * i
        P.dma("sp", lambda e, s=s, rows=rows, r0=r0: e.dma_start(out=xin[s][0:rows, :], in_=X[r0:r0 + rows, :]),
              w=[f"xin{s}"], chan=f"xin{s}")
        P.op("act", lambda e, s=s, rows=rows, i=i: e.activation(out=junk[0:rows, :], in_=xin[s][0:rows, :], func=AF.Square,
                                                                accum_out=stat[0:rows, i, 0:1]),
             r=[f"xin{s}"], w=["junk", ("stat", i)])
        P.op("dve", lambda e, rows=rows, i=i: e.tensor_scalar(out=stat[0:rows, i, 1:2], in0=stat[0:rows, i, 0:1],
                                                              scalar1=1.0 / D, scalar2=EPS, op0=ALU.mult, op1=ALU.add),
             r=[("stat", i)], w=[("stat", i)])
        P.op("act", lambda e, rows=rows, i=i: e.activation(out=stat[0:rows, i, 2:3], in_=stat[0:rows, i, 1:2], func=AF.Sqrt),
             r=[("stat", i)], w=[("stat", i)])
        P.op("dve", lambda e, rows=rows, i=i: e.reciprocal(out=stat[0:rows, i, 3:4], in_=stat[0:rows, i, 2:3]),
             r=[("stat", i)], w=[("stat", i)])
        P.op("dve", lambda e, s=s, rows=rows, i=i: e.scalar_tensor_tensor(
            out=xnb[s][0:rows, :], in0=xin[s][0:rows, :], scalar=stat[0:rows, i, 3:4], in1=normg_bc[0:rows, :],
            op0=ALU.mult, op1=ALU.mult), r=[f"xin{s}", ("stat", i), "normg_bc"], w=[f"xnb{s}"])
        for g4 in range(4):
            pk = g4 % 2
            pT = PS[pk][:].bitcast(BF16)
            for q in range(4):
                kc = 4 * g4 + q
                P.op("pe", lambda e, s=s, rows=rows, kc=kc, q=q, pT=pT: e.transpose(
                    out=pT[:, q * 128:q * 128 + rows], in_=xnb[s][0:rows, kc * 128:(kc + 1) * 128],
                    identity=identb[0:rows, 0:rows]), r=[f"xnb{s}", "identb"], w=[psk(pk)])
            eng = "act" if g4 % 2 == 0 else "dve"
            src = pT[:, 0:512].rearrange("p (q t) -> p q t", q=4)[:, :, 0:rows]
            dst = xnT[:, 4 * g4:4 * g4 + 4, r0:r0 + rows]
            if eng == "act":
                P.op("act", lambda e, src=src, dst=dst: e.copy(out=dst, in_=src), r=[psk(pk)], w=[("xnT", i)])
            else:
                P.op("dve", lambda e, src=src, dst=dst: e.tensor_copy(out=dst, in_=src), r=[psk(pk)], w=[("xnT", i)])

    P.flush(barrier=True)
    es_a1.close()
    chk("a1")
    es_a2 = ExitStackT()
    sb = mk(es_a2)

    def tiles_of(t0, n):
        return [("xnT", i) for i in range(17) if not (128 * i >= t0 + n or (128 * i + (128 if i < 16 else NSPEC)) <= t0)]

    def load_w(col0):
        s = wctr[0] % 3
        wctr[0] += 1
        src = Win[:, col0:col0 + 128].rearrange("(kc p) c -> p kc c", p=128)
        P.dma("pool", lambda e, s=s, src=src: e.dma_start(out=wblk[s][:], in_=src), w=[f"wblk{s}"], chan=f"wblk{s}")
        return s

    pctr = [0]

    def proj_F(ws, consume, ranges=RANGES):
        for (t0, n) in ranges:
            pk = 2 + (pctr[0] % 2)
            pctr[0] += 1
            for kc in range(KC):
                P.op("pe", lambda e, pk=pk, ws=ws, kc=kc, t0=t0, n=n: e.matmul(
                    PS[pk][:, 0:n], lhsT=wblk[ws][:, kc, :], rhs=xnT[:, kc, t0:t0 + n], start=(kc == 0), stop=(kc == KC - 1)),
                    r=[f"wblk{ws}"] + tiles_of(t0, n), w=[psk(pk)])
            consume(PS[pk], pk, t0, n)

    def proj_T(ws, consume, tiles=range(17)):
        for i in tiles:
            rows = 128 if i < 16 else NSPEC
            pk = 2 + (pctr[0] % 2)
            pctr[0] += 1
            for kc in range(KC):
                P.op("pe", lambda e, pk=pk, ws=ws, kc=kc, i=i, rows=rows: e.matmul(
                    PS[pk][0:rows, 0:128], lhsT=xnT[:, kc, 128 * i:128 * i + rows], rhs=wblk[ws][:, kc, :],
                    start=(kc == 0), stop=(kc == KC - 1)), r=[f"wblk{ws}", ("xnT", i)], w=[psk(pk)])
            consume(PS[pk], pk, i, rows)

    fbuf = sb("fbuf", [128, NT], F32)
    L0 = sb("L0", [128, NT], F32)
    L1 = sb("L1", [128, NT], F32)
    nkeT = sb("nkeT", [128, NT], BF16)
    qeT = sb("qeT", [128, NT], BF16)
    vtok = sb("vtok", [128, 17, 128], BF16)
    vspf = sb("vspf", [NSPEC, 128], F32)
    qdec = sb("qdec", [128, NDEC], F32)
    kdec = sb("kdec", [128, NDEC], F32)
    S32 = sb("S32", [128, 128], F32)
    Sbf = sb("Sbf", [128, 128], BF16)
    Stmp = sb("Stmp", [128, 128], F32)
    ATm = [sb(f"ATm{s}", [128, 64], BF16) for s in range(2)]
    ktok = [sb(f"ktok{s}", [128, 128], BF16) for s in range(4)]
    S0t = [sb(f"S0t{s}", [128, 128], F32) for s in range(2)]
    Snt = [sb(f"Snt{s}", [128, 128], F32) for s in range(2)]
    sqb = sb("sqb", [128, 512], F32)
    rsb = sb("rsb", [128, 512], F32)
    gtmp = [sb(f"gtmp{s}", [128, 512], F32) for s in range(2)]

    def norm_gate_cols(obuf_ap, okey, n, gain_ap, gain_key, dst_ap, dst_key, pk):
        P.op("act", lambda e: e.activation(out=sqb[:, 0:n], in_=obuf_ap, func=AF.Square), r=[okey], w=["sqb"])
        P.op("pe", lambda e: e.matmul(PS[pk][:, 0:n], lhsT=onesf[:], rhs=sqb[:, 0:n], start=True, stop=True),
             r=["onesf", "sqb"], w=[psk(pk)])
        P.op("dve", lambda e: e.tensor_scalar(out=rsb[:, 0:n], in0=PS[pk][:, 0:n], scalar1=1.0 / 128, scalar2=EPS,
                                              op0=ALU.mult, op1=ALU.add), r=[psk(pk)], w=["rsb"])
        P.op("act", lambda e: e.activation(out=rsb[:, 0:n], in_=rsb[:, 0:n], func=AF.Sqrt), r=["rsb"], w=["rsb"])
        P.op("dve", lambda e: e.reciprocal(out=rsb[:, 0:n], in_=rsb[:, 0:n]), r=["rsb"], w=["rsb"])
        P.op("dve", lambda e: e.scalar_tensor_tensor(out=dst_ap, in0=obuf_ap, scalar=gain_ap, in1=rsb[:, 0:n],
                                                     op0=ALU.mult, op1=ALU.mult), r=[okey, "rsb", gain_key], w=[dst_key])

    for h in range(4):
        ws = load_w(512 * 1 + 128 * h)

        def cons_f(ps, pk, t0, n):
            P.op("act", lambda e: e.activation(out=fbuf[:, t0:t0 + n], in_=ps[:, 0:n], func=AF.Sigmoid), r=[psk(pk)], w=["fbuf"])
        proj_F(ws, cons_f)
        chk("h0")
        P.op("dve", lambda e, h=h: e.tensor_scalar(out=fbuf[:], in0=fbuf[:], scalar1=omlt[:, h:h + 1], scalar2=lbt[:, h:h + 1],
                                                   op0=ALU.mult, op1=ALU.add), r=["fbuf", "omlt", "lbt"], w=["fbuf"])
        P.op("act", lambda e: e.activation(out=L0[:], in_=fbuf[:], func=AF.Ln), r=["fbuf"], w=["L0"])
        chk("h0b")
        P.op("dve", lambda e: e.tensor_tensor_scan(out=L1[:], data0=resetm[:], data1=L0[:], initial=0.0, op0=ALU.mult, op1=ALU.add),
             r=["resetm", "L0"], w=["L1"])
        chk("h0c")
        P.op("act", lambda e: e.activation(out=L0[:], in_=L1[:], func=AF.Exp, scale=-1.0), r=["L1"], w=["L0"])
        P.op("dve", lambda e: e.scalar_tensor_tensor(out=nkeT[:], in0=fbuf[:], scalar=onesf[:, 0:1], in1=L0[:], op0=ALU.subtract, op1=ALU.mult),
             r=["fbuf", "L0", "onesf"], w=["nkeT"])
        P.op("dve", lambda e: e.tensor_scalar(out=kdec[:], in0=fbuf[:, C_DEC:C_DEC + NDEC], scalar1=-1.0, scalar2=1.0,
                                              op0=ALU.mult, op1=ALU.add), r=["fbuf"], w=["kdec"])
        P.op("act", lambda e: e.activation(out=fbuf[:], in_=L1[:], func=AF.Exp), r=["L1", "kdec"], w=["fbuf"])
        chk("h1")
        ws = load_w(512 * 0 + 128 * h)

        def cons_q(ps, pk, t0, n):
            if t0 == C_META:
                P.op("act", lambda e: e.copy(out=qdec[:], in_=ps[:, NMETA:NSPEC]), r=[psk(pk)], w=["qdec"])
            P.op("dve", lambda e: e.tensor_tensor(out=qeT[:, t0:t0 + n], in0=ps[:, 0:n], in1=fbuf[:, t0:t0 + n], op=ALU.mult),
                 r=[psk(pk), "fbuf"], w=["qeT"])
        proj_F(ws, cons_q)
        chk("h2")
        ws = load_w(512 * 2 + 128 * h)

        def cons_v(ps, pk, i, rows):
            P.op("act", lambda e: e.copy(out=vtok[0:rows, i, :], in_=ps[0:rows, 0:128]), r=[psk(pk)], w=["vtok"])
            if i == 16:
                P.op("dve", lambda e: e.tensor_copy(out=vspf[:], in_=ps[0:NSPEC, 0:128]), r=[psk(pk)], w=["vspf"])
        proj_T(ws, cons_v)
        chk("h3")
        P.op("pool", lambda e: e.memset(S32[:], 0.0), w=["S32"])
        P.op("pool", lambda e: e.memset(Sbf[:], 0.0), w=["Sbf"])
        chunks = [("m", 0)] + [("r", c) for c in range(32)]
        kslot = 0
        for ci, (kind, c) in enumerate(chunks):
            if kind == "m":
                c0, n, tile, rows, par = C_META, 16, 16, NSPEC, 2
            else:
                c0, n, tile, rows, par = 64 * c, 64, c // 2, 128, c % 2
            tc0 = 128 * tile
            a = ci % 2
            P.op("pe", lambda e, c0=c0, n=n, tc0=tc0, rows=rows: e.matmul(
                PS[4][0:rows, 0:n], lhsT=nkeT[:, tc0:tc0 + rows], rhs=qeT[:, c0:c0 + n], start=True, stop=True),
                r=["nkeT", "qeT"], w=[psk(4)])
            if kind == "m":
                P.op("dve", lambda e, a=a: e.tensor_tensor(out=ATm[a][0:NSPEC, 0:16], in0=PS[4][0:NSPEC, 0:16], in1=maskAm[:], op=ALU.mult),
                     r=[psk(4), "maskAm"], w=[f"ATm{a}"])
            else:
                P.op("dve", lambda e, a=a, par=par: e.tensor_tensor(out=ATm[a][:, 0:64], in0=PS[4][:, 0:64], in1=maskA[:, par, :], op=ALU.mult),
                     r=[psk(4), "maskA"], w=[f"ATm{a}"])
            if kind == "m" or par == 0:
                pTk = PS[5][:].bitcast(BF16)
                P.op("pe", lambda e, tc0=tc0, rows=rows, pTk=pTk: e.transpose(out=pTk[0:rows, 0:128], in_=nkeT[:, tc0:tc0 + rows],
                                                                              identity=identb[:]), r=["nkeT", "identb"], w=[psk(5)])
                if kind == "m":
                    ks_m = kslot % 4
                    kslot += 1
                    P.op("act", lambda e, ks_m=ks_m, pTk=pTk: e.activation(out=ktok[ks_m][0:NSPEC, :], in_=pTk[0:NSPEC, 0:128], func=AF.Copy,
                                                                           scale=mcol[0:NSPEC, 2:3]), r=[psk(5), "mcol"], w=[f"ktok{ks_m}"])
                    kt_cur = {2: ks_m}
                else:
                    kt_cur = {}
                    for pp in range(2):
                        ks_p = kslot % 4
                        kslot += 1
                        P.op("act", lambda e, ks_p=ks_p, pp=pp, pTk=pTk: e.activation(out=ktok[ks_p][:], in_=pTk[:, 0:128], func=AF.Copy,
                                                                                      scale=mcol[:, pp:pp + 1]), r=[psk(5), "mcol"], w=[f"ktok{ks_p}"])
                        kt_cur[pp] = ks_p
            ksl = kt_cur[par]
            P.op("pe", lambda e, c0=c0, n=n: e.matmul(PS[6][:, 0:n], lhsT=Sbf[:], rhs=qeT[:, c0:c0 + n], start=True, stop=False),
                 r=["Sbf", "qeT"], w=[psk(6)])
            P.op("pe", lambda e, a=a, n=n, tile=tile, rows=rows: e.matmul(PS[6][:, 0:n], lhsT=vtok[0:rows, tile, :], rhs=ATm[a][0:rows, 0:n],
                                                                          start=False, stop=True), r=["vtok", f"ATm{a}"], w=[psk(6)])
            P.op("act", lambda e, c0=c0, n=n: e.copy(out=L1[:, c0:c0 + n], in_=PS[6][:, 0:n]), r=[psk(6)], w=["L1o"])
            P.op("pe", lambda e, ksl=ksl, tile=tile, rows=rows: e.matmul(PS[7][:, 0:128], lhsT=ktok[ksl][0:rows, :], rhs=vtok[0:rows, tile, :],
                                                                         start=True, stop=True), r=[f"ktok{ksl}", "vtok"], w=[psk(7)])
            ebc = fbuf[:, c0 + n - 1:c0 + n]
            P.op("dve", lambda e, ebc=ebc: e.tensor_scalar(out=Stmp[:], in0=PS[7][:, 0:128], scalar1=ebc, scalar2=None, op0=ALU.mult),
                 r=[psk(7), "fbuf"], w=["Stmp"])
            P.op("dve", lambda e, ebc=ebc: e.scalar_tensor_tensor(out=S32[:], in0=S32[:], scalar=ebc, in1=Stmp[:], op0=ALU.mult, op1=ALU.add),
                 r=["S32", "Stmp", "fbuf"], w=["S32"])
            P.op("act", lambda e: e.copy(out=Sbf[:], in_=S32[:]), r=["S32"], w=["Sbf"])
        P.dma("sp", lambda e, h=h: e.dma_start(out=sp_d[h], in_=S32[:]), r=["S32"], w=["sp_out"], chan="spout")
        chk("h4")
        for s in range(NDEC):
            sl = s % 2
            P.dma("sp", lambda e, s=s, h=h, sl=sl: e.dma_start(out=S0t[sl][:], in_=sh_d[s, h]), w=[f"S0t{sl}"], chan=f"s0t{sl}")
            P.op("pe", lambda e, s=s: e.matmul(PS[4][:, 0:128], lhsT=selv[:, s, :], rhs=vspf[:], start=True, stop=True),
                 r=["selv", "vspf"], w=[psk(4)])
            P.op("dve", lambda e, s=s, sl=sl: e.tensor_scalar(out=Stmp[:], in0=S0t[sl][:], scalar1=fbuf[:, C_DEC + s:C_DEC + s + 1], scalar2=None,
                                                              op0=ALU.mult), r=[f"S0t{sl}", "fbuf"], w=["Stmp"])
            P.op("dve", lambda e, s=s, sl=sl: e.scalar_tensor_tensor(out=Snt[sl][:], in0=PS[4][:, 0:128], scalar=kdec[:, s:s + 1], in1=Stmp[:],
                                                                     op0=ALU.mult, op1=ALU.add), r=[psk(4), "kdec", "Stmp"], w=[f"Snt{sl}"])
            P.dma("sp", lambda e, s=s, h=h, sl=sl: e.dma_start(out=ss_d[s, h], in_=Snt[sl][:]), r=[f"Snt{sl}"], w=["ss_out"], chan=f"snt{sl}")
            P.op("pe", lambda e, s=s, sl=sl: e.matmul(PS[5][:, 0:1], lhsT=Snt[sl][:], rhs=qdec[:, s:s + 1], start=True, stop=True),
                 r=[f"Snt{sl}", "qdec"], w=[psk(5)])
            P.op("act", lambda e, s=s: e.copy(out=L1[:, C_DEC + s:C_DEC + s + 1], in_=PS[5][:, 0:1]), r=[psk(5)], w=["L1o"])
        chk("h5")
        for (t0, n) in RANGES[:4]:
            norm_gate_cols(L1[:, t0:t0 + n], "L1o", n, hgng[:, 0:1], "hgng", mixedT[:, h, t0:t0 + n], ("mixedT", h), 4)
        norm_gate_cols(L1[:, C_DEC:C_DEC + NDEC], "L1o", NDEC, hgng[:, 0:1], "hgng", mixedT[:, h, NR:NR + NDEC], ("mixedT", h), 4)
        chk("h6")
        ws = load_w(512 * 3 + 128 * h)

        def cons_g(ps, pk, t0, n, h=h):
            gs = (t0 // 512) % 2
            if t0 == C_META:
                P.op("act", lambda e: e.activation(out=gtmp[gs][:, 0:NDEC], in_=ps[:, NMETA:NSPEC], func=AF.Silu), r=[psk(pk)], w=[f"gtmp{gs}"])
                P.op("dve", lambda e: e.tensor_tensor(out=mixedT[:, h, NR:NR + NDEC], in0=mixedT[:, h, NR:NR + NDEC], in1=gtmp[gs][:, 0:NDEC],
                                                      op=ALU.mult), r=[f"gtmp{gs}", ("mixedT", h)], w=[("mixedT", h)])
            else:
                P.op("act", lambda e: e.activation(out=gtmp[gs][:, 0:n], in_=ps[:, 0:n], func=AF.Silu), r=[psk(pk)], w=[f"gtmp{gs}"])
                P.op("dve", lambda e: e.tensor_tensor(out=mixedT[:, h, t0:t0 + n], in0=mixedT[:, h, t0:t0 + n], in1=gtmp[gs][:, 0:n],
                                                      op=ALU.mult), r=[f"gtmp{gs}", ("mixedT", h)], w=[("mixedT", h)])
        proj_F(ws, cons_g)
    P.flush(barrier=True)
    es_a2.close()
    chk("a2")
    es_a3 = ExitStackT()
    sb = mk(es_a3)

    QT = sb("QT", [128, NR], BF16)
    K0T = sb("K0T", [128, NR + NMETA], BF16)
    K1T = sb("K1T", [128, NR + NMETA], BF16)
    Vtk = sb("Vtk", [128, 17, 128], BF16)
    gate = sb("gate", [128, NR], F32)
    kstage = [sb(f"kstage{s}", [128, 128], F32) for s in range(2)]
    vstage = [sb(f"vstage{s}", [128, 128], F32) for s in range(2)]
    Et = [[sb(f"Et{c}{s}", [128, 512], BF16) for s in range(2)] for c in range(2)]
    rz = [sb(f"rz{c}", [128, 512], F32) for c in range(2)]
    tat = [sb(f"tat{c}", [128, 512], F32) for c in range(2)]
    oat = sb("oat", [128, 512], F32)

    P.op("pool", lambda e: e.memset(K0T[:], 0.0), w=["K0T"])
    P.op("pool", lambda e: e.memset(K1T[:], 0.0), w=["K1T"])

    def out_rows(i):
        return (NMETA + 128 * i, 128)

    for h in range(4):
        ws = load_w(512 * 5 + 128 * h)

        def cons_kF(ps, pk, t0, n):
            nn = n if t0 != C_META else NMETA
            P.op("act", lambda e: e.copy(out=K0T[0:64, t0:t0 + nn], in_=ps[0:64, 0:nn]), r=[psk(pk)], w=["K0T"])
            P.op("dve", lambda e: e.tensor_copy(out=K1T[64:128, t0:t0 + nn], in_=ps[64:128, 0:nn]), r=[psk(pk)], w=["K1T"])
        proj_F(ws, cons_kF)
        sctr = [0]

        def cons_kT(ps, pk, i, rows, h=h):
            s = sctr[0] % 2
            sctr[0] += 1
            P.op("act", lambda e: e.copy(out=kstage[s][0:rows, :], in_=ps[0:rows, 0:128]), r=[psk(pk)], w=[f"kstage{s}"])
            if i < 16:
                r0, _ = out_rows(i)
                P.dma("sp", lambda e: e.dma_start(out=kp_d[r0:r0 + 128, 128 * h:128 * h + 128], in_=kstage[s][:]),
                      r=[f"kstage{s}"], w=["kp_out"], chan=f"kstage{s}")
            else:
                P.dma("sp", lambda e: e.dma_start(out=kp_d[0:NMETA, 128 * h:128 * h + 128], in_=kstage[s][0:NMETA, :]),
                      r=[f"kstage{s}"], w=["kp_out"], chan=f"kstage{s}")
                P.dma("sp", lambda e: e.dma_start(out=ks_d[:, 128 * h:128 * h + 128], in_=kstage[s][NMETA:NSPEC, :]),
                      r=[f"kstage{s}"], w=["ks_out"], chan=f"kstageB{s}")
                P.op("dve", lambda e: e.tensor_copy(out=kspec[:, h, :], in_=ps[0:NSPEC, 0:128]), r=[psk(pk)], w=["kspec"])
        proj_T(ws, cons_kT)
        ws = load_w(512 * 6 + 128 * h)
        sctr2 = [0]

        def cons_vT(ps, pk, i, rows, h=h):
            s = sctr2[0] % 2
            sctr2[0] += 1
            P.op("act", lambda e: e.copy(out=vstage[s][0:rows, :], in_=ps[0:rows, 0:128]), r=[psk(pk)], w=[f"vstage{s}"])
            P.op("dve", lambda e: e.tensor_copy(out=Vtk[0:rows, i, :], in_=ps[0:rows, 0:128]), r=[psk(pk)], w=["Vtk"])
            if i < 16:
                r0, _ = out_rows(i)
                P.dma("sp", lambda e: e.dma_start(out=vp_d[r0:r0 + 128, 128 * h:128 * h + 128], in_=vstage[s][:]),
                      r=[f"vstage{s}"], w=["vp_out"], chan=f"vstage{s}")
            else:
                P.dma("sp", lambda e: e.dma_start(out=vp_d[0:NMETA, 128 * h:128 * h + 128], in_=vstage[s][0:NMETA, :]),
                      r=[f"vstage{s}"], w=["vp_out"], chan=f"vstage{s}")
                P.dma("sp", lambda e: e.dma_start(out=vs_d[:, 128 * h:128 * h + 128], in_=vstage[s][NMETA:NSPEC, :]),
                      r=[f"vstage{s}"], w=["vs_out"], chan=f"vstageB{s}")
                P.op("dve", lambda e: e.tensor_copy(out=vspec[:, h, :], in_=ps[0:NSPEC, 0:128]), r=[psk(pk)], w=["vspec"])
        proj_T(ws, cons_vT)
        ws = load_w(512 * 4 + 128 * h)

        def cons_qF(ps, pk, t0, n):
            P.op("act", lambda e: e.mul(out=QT[:, t0:t0 + n], in_=ps[:, 0:n], mul=0.125), r=[psk(pk)], w=["QT"])
        proj_F(ws, cons_qF, ranges=RANGES[:4])

        def cons_qT(ps, pk, i, rows, h=h):
            P.op("act", lambda e: e.mul(out=qspec[:, h, :], in_=ps[0:NSPEC, 0:128], mul=0.125), r=[psk(pk)], w=["qspec"])
        proj_T(ws, cons_qT, tiles=[16])
        ws = load_w(512 * 7 + 128 * h)

        def cons_gF(ps, pk, t0, n, h=h):
            if t0 == C_META:
                P.op("act", lambda e: e.activation(out=gdec[:, h, :], in_=ps[:, NMETA:NSPEC], func=AF.Silu), r=[psk(pk)], w=["gdec"])
            else:
                P.op("act", lambda e: e.activation(out=gate[:, t0:t0 + n], in_=ps[:, 0:n], func=AF.Silu), r=[psk(pk)], w=["gate"])
        proj_F(ws, cons_gF)

        ectr = 0
        for r in range(4):
            q0 = 512 * r
            ktiles = [("m", 0)] + [("r", j) for j in range(4 * r + 4)]
            for ti, (kind, j) in enumerate(ktiles):
                if kind == "m":
                    kc0, kk, vt, lo = C_META, NMETA, 16, 0
                else:
                    kc0, kk, vt = 128 * j, 128, j
                    lo = 128 * max(0, j - 4 * r)
                n = 512 - lo
                sl = ectr % 2
                ectr += 1
                first = (ti == 0)
                last = (ti == len(ktiles) - 1)
                for c in range(2):
                    KcT = K0T if c == 0 else K1T
                    kname = "K0T" if c == 0 else "K1T"
                    pk = 2 * c + sl
                    extra = []
                    if kind == "m":
                        if r == 0:
                            extra.append((0, 1))
                    else:
                        if j >= 4 * r:
                            extra.append((0, 0))
                            if n >= 256:
                                extra.append((128, 1))
                        elif j == 4 * r - 1:
                            extra.append((0, 1))
                    P.op("pe", lambda e, pk=pk, KcT=KcT, kc0=kc0, kk=kk, q0=q0, lo=lo, n=n, ne=len(extra): e.matmul(
                        PS[pk][0:kk, 0:n], lhsT=KcT[:, kc0:kc0 + kk], rhs=QT[:, q0 + lo:q0 + 512], start=True, stop=(ne == 0)),
                        r=[kname, "QT"], w=[psk(pk)])
                    for xi, (co, oi) in enumerate(extra):
                        if kind == "m":
                            P.op("pe", lambda e, pk=pk, co=co, oi=oi, xi=xi, ne=len(extra), h=h: e.matmul(
                                PS[pk][0:NMETA, co:co + 128], lhsT=J16b[:], rhs=Hm[0:NMETA, h, oi, :], start=False, stop=(xi == ne - 1)),
                                r=["J16b", "Hm"], w=[psk(pk)])
                        else:
                            P.op("pe", lambda e, pk=pk, co=co, oi=oi, xi=xi, ne=len(extra), h=h: e.matmul(
                                PS[pk][:, co:co + 128], lhsT=Jb[:], rhs=Hm[:, h, oi, :], start=False, stop=(xi == ne - 1)),
                                r=["Jb", "Hm"], w=[psk(pk)])
                    P.op("act", lambda e, pk=pk, c=c, sl=sl, kk=kk, n=n, h=h: e.activation(
                        out=Et[c][sl][0:kk, 0:n], in_=PS[pk][0:kk, 0:n], func=AF.Exp, bias=b31t[0:kk, h:h + 1]),
                        r=[psk(pk), "b31t"], w=[f"Et{c}{sl}"])
                for c in range(2):
                    P.op("pe", lambda e, c=c, sl=sl, kk=kk, n=n, lo=lo, vt=vt, first=first, last=last: e.matmul(
                        PS[4 + c][:, lo:512], lhsT=Vtk[0:kk, vt, :], rhs=Et[c][sl][0:kk, 0:n], start=first, stop=last),
                        r=["Vtk", f"Et{c}{sl}"], w=[psk(4 + c)])
                    P.op("pe", lambda e, c=c, sl=sl, kk=kk, n=n, lo=lo, first=first, last=last: e.matmul(
                        PS[6 + c][:, lo:512], lhsT=onesb[0:kk, :], rhs=Et[c][sl][0:kk, 0:n], start=first, stop=last),
                        r=["onesb", f"Et{c}{sl}"], w=[psk(6 + c)])
            for c in range(2):
                P.op("dve", lambda e, c=c: e.reciprocal(out=rz[c][:], in_=PS[6 + c][:]), r=[psk(6 + c)], w=[f"rz{c}"])
                P.op("dve", lambda e, c=c: e.tensor_tensor(out=tat[c][:], in0=PS[4 + c][:], in1=rz[c][:], op=ALU.mult),
                     r=[psk(4 + c), f"rz{c}"], w=[f"tat{c}"])
            P.op("dve", lambda e: e.scalar_tensor_tensor(out=oat[:], in0=tat[1][:], scalar=lamt[:, 1:2], in1=tat[0][:], op0=ALU.mult, op1=ALU.add),
                 r=["tat0", "tat1", "lamt"], w=["oat"])
            norm_gate_cols(oat[:], "oat", 512, cn[:, 0:1], "cn", tat[0][:], "tat0", 0)
            P.op("dve", lambda e, q0=q0, h=h: e.tensor_tensor(out=mixedT[:, 4 + h, q0:q0 + 512], in0=tat[0][:], in1=gate[:, q0:q0 + 512], op=ALU.mult),
                 r=["tat0", "gate"], w=[("mixedT", 4 + h)])
    P.flush(barrier=True)
    es_a3.close()
    es_a.close()
    chk("a3")
    es_b = ExitStackT()
    sb = mk(es_b)

    NKB = 4
    kbuf = [sb(f"kbuf{s}", [128, 512], F32) for s in range(NKB)]
    vbuf = [sb(f"vbuf{s}", [128, 512], F32) for s in range(NKB)]
    k64 = sb("k64", [128, 512], F32)
    v64 = sb("v64", [128, 512], F32)
    qbc = sb("qbc", [128, 512], F32)
    prod = sb("prod", [128, 512], F32)
    scall = sb("scall", [128, 65, 8], F32)
    ball = sb("ball", [128, 65, 8], F32)
    eall = sb("eall", [128, 65, 8], F32)
    Rn = sb("Rn", [8, 512], F32)
    rzd = sb("rzd", [8, 1], F32)

    P.op("pool", lambda e: e.memset(k64[:], 0.0), w=["k64"])
    P.op("pool", lambda e: e.memset(v64[:], 0.0), w=["v64"])
    ball4 = ball[:].rearrange("p g (h c) -> p g h c", c=2)
    P.op("dve", lambda e: e.tensor_copy(out=ball4[:, 0:63, :, :], in_=bsm[:, 0:1, :].unsqueeze(3).to_broadcast([128, 63, 4, 2])),
         r=["bsm"], w=["ball"])
    P.op("dve", lambda e: e.tensor_copy(out=ball4[:, 63:65, :, :], in_=bsm[:, 1:3, :].unsqueeze(3).to_broadcast([128, 2, 4, 2])),
         r=["bsm"], w=["ball"])
    gctr = 0
    for s in range(NDEC):
        P.op("pe", lambda e, s=s: e.matmul(PS[0][:, 0:512], lhsT=selv[:, s, :], rhs=qspec[:].rearrange("p h d -> p (h d)"),
                                           start=True, stop=True), r=["selv", "qspec"], w=[psk(0)])
        P.op("act", lambda e: e.copy(out=qbc[:], in_=PS[0][:, 0:512]), r=[psk(0)], w=["qbc"])
        P.dma("sp", lambda e, s=s: e.dma_start(out=k64[0:1, :], in_=kspec[NMETA + s:NMETA + s + 1, :, :].rearrange("p h d -> p (h d)")),
              r=["kspec"], w=["k64"], chan="k64")
        P.dma("sp", lambda e, s=s: e.dma_start(out=v64[0:1, :], in_=vspec[NMETA + s:NMETA + s + 1, :, :].rearrange("p h d -> p (h d)")),
              r=["vspec"], w=["v64"], chan="v64")
        for pg in range(65):
            if pg < 64:
                sl = gctr % NKB
                gctr += 1
                col = s * 64 + pg
                P.dma("pool", lambda e, sl=sl, col=col: e.indirect_dma_start(
                    out=kbuf[sl][:, :], out_offset=None, in_=ck_d,
                    in_offset=bass.IndirectOffsetOnAxis(ap=idxg[:, col:col + 1], axis=0)), r=["idxg"], w=[f"kbuf{sl}"], chan=f"kbuf{sl}")
                src, skey = kbuf[sl], f"kbuf{sl}"
            else:
                src, skey = k64, "k64"
            P.op("dve", lambda e, src=src: e.tensor_tensor(out=prod[:], in0=src[:], in1=qbc[:], op=ALU.mult), r=[skey, "qbc"], w=["prod"])
            P.op("dve", lambda e, pg=pg: e.tensor_reduce(out=scall[:, pg, :], in_=prod[:].rearrange("p (g d) -> p g d", d=64), axis=AX.X, op=ALU.add),
                 r=["prod"], w=["scall"])
        P.op("dve", lambda e: e.tensor_tensor(out=scall[:], in0=scall[:], in1=ball[:], op=ALU.add), r=["scall", "ball"], w=["scall"])
        P.op("act", lambda e: e.activation(out=eall[:], in_=scall[:], func=AF.Exp), r=["scall"], w=["eall"])
        for pg in range(65):
            if pg < 64:
                sl = gctr % NKB
                gctr += 1
                col = s * 64 + pg
                P.dma("pool", lambda e, sl=sl, col=col: e.indirect_dma_start(
                    out=vbuf[sl][:, :], out_offset=None, in_=cv_d,
                    in_offset=bass.IndirectOffsetOnAxis(ap=idxg[:, col:col + 1], axis=0)), r=["idxg"], w=[f"vbuf{sl}"], chan=f"vbuf{sl}")
                src, skey = vbuf[sl], f"vbuf{sl}"
            else:
                src, skey = v64, "v64"
            P.op("pe", lambda e, pg=pg, src=src: e.matmul(PS[1][0:8, 0:512], lhsT=eall[:, pg, :], rhs=src[:], start=(pg == 0), stop=(pg == 64)),
                 r=["eall", skey], w=[psk(1)])
            P.op("pe", lambda e, pg=pg: e.matmul(PS[2][0:8, 0:1], lhsT=eall[:, pg, :], rhs=onesf[:, 0:1], start=(pg == 0), stop=(pg == 64)),
                 r=["eall", "onesf"], w=[psk(2)])
        P.op("dve", lambda e: e.reciprocal(out=rzd[:], in_=PS[2][0:8, 0:1]), r=[psk(2)], w=["rzd"])
        P.op("dve", lambda e: e.tensor_scalar(out=Rn[:], in0=PS[1][0:8, 0:512], scalar1=rzd[:, 0:1], scalar2=None, op0=ALU.mult),
             r=[psk(1), "rzd"], w=["Rn"])
        for h in range(4):
            P.op("pe", lambda e, h=h: e.matmul(PS[3][:, h:h + 1], lhsT=Rn[:, 128 * h:128 * h + 128], rhs=comb[:, h:h + 1], start=True, stop=True),
                 r=["Rn", "comb"], w=[psk(3)])
        P.op("act", lambda e, s=s: e.copy(out=odec[:, :, s], in_=PS[3][:, 0:4]), r=[psk(3)], w=["odec"])
    od2 = odec[:].rearrange("p h s -> p (h s)")
    norm_gate_cols(od2, "odec", 32, cn[:, 0:1], "cn", prod[:, 0:32], "prod", 4)
    P.op("dve", lambda e: e.tensor_tensor(out=mixedT[:, 4:8, NR:NR + NDEC], in0=prod[:, 0:32].rearrange("p (h s) -> p h s", h=4),
                                          in1=gdec[:], op=ALU.mult), r=["prod", "gdec"], w=[("mixedT", k) for k in range(4, 8)])
    P.flush(barrier=True)
    es_b.close()
    chk("b")

    snd3 = snd_d.rearrange("(c p) n -> p c n", p=128)
    MK = [("mixedT", k) for k in range(8)]
    for hf in range(2):
        P.dma("sp", lambda e, hf=hf: e.dma_start(out=snd3[:, :, 1028 * hf:1028 * hf + 1024], in_=mixedT[:, :, 1024 * hf:1024 * hf + 1024]),
              r=MK, w=["snd"], chan="snd", wait_all=True)
        P.dma("sp", lambda e, hf=hf: e.dma_start(out=snd3[:, :, 1028 * hf + 1024:1028 * hf + 1028], in_=mixedT[:, :, NR + 4 * hf:NR + 4 * hf + 4]),
              r=MK, w=["snd"], chan="snd", wait_all=True)
    P.flush(barrier=True)
    chk("c1")
    es_m.close()
    es_c = ExitStackT()
    sb = mk(es_c)
    mxin = sb("mxin", [128, 16, 1028], BF16)
    ypre = sb("ypre", [128, 9, D], F32)
    fng_bc = sb("fng_bc", [128, D], F32)
    wob = [sb(f"wob{s}", [128, 16, 512], BF16) for s in range(2)]
    st2 = sb("st2", [128, 9, 4], F32)
    junk2 = sb("junk2", [128, D], BF16)
    ccsem = nc.alloc_semaphore(name="ccsem")
    with nc.Block() as block:
        @block.gpsimd
        def _(g):
            for i in range(8):
                g.collective_compute("AllGather", ALU.bypass, replica_groups=[[0, 1], [2, 3], [4, 5], [6, 7]],
                                     ins=[snd_d[128 * i:128 * i + 128, :]], outs=[rcv_ds[i]]).then_inc(ccsem, 1)
            g.wait_ge(ccsem, 8)
    chk("c2")
    P.dma("sp", lambda e: e.dma_start(out=fng_bc[:], in_=fng_d.partition_broadcast(128)), w=["fng_bc"], chan="fng")
    for cc in range(16):
        rk, ch = cc // 8, cc % 8
        rcv_rows = rcv_ds[ch].rearrange("r (t n) -> (r t) n", t=2)
        P.dma("pool", lambda e, cc=cc, rk=rk, rcv_rows=rcv_rows: e.indirect_dma_start(
            out=mxin[:, cc, :], out_offset=None, in_=rcv_rows,
            in_offset=bass.IndirectOffsetOnAxis(ap=idxm[:, rk:rk + 1], axis=0)), r=["idxm"], w=["mxin"], chan="mxin", wait_all=True)
    for tt in range(9):
        rows = 128 if tt < 8 else 4
        P.dma("sp", lambda e, tt=tt, rows=rows: e.dma_start(out=ypre[0:rows, tt, :], in_=Xres[128 * tt:128 * tt + rows, :]),
              w=[("ypre", tt)], chan="xres", wait_all=True)
    chk("c3")
    for g in range(4):
        s = g % 2
        src = Wout[:, 512 * g:512 * g + 512].rearrange("(cc p) n -> p cc n", p=128)
        P.dma("pool", lambda e, s=s, src=src: e.dma_start(out=wob[s][:], in_=src), w=[f"wob{s}"], chan=f"wob{s}")
        for tt in range(9):
            rows = 128 if tt < 8 else 4
            t0 = 128 * tt
            pk = tt % 2
            for cc in range(16):
                P.op("pe", lambda e, pk=pk, cc=cc, t0=t0, rows=rows, s=s: e.matmul(
                    PS[pk][0:rows, 0:512], lhsT=mxin[:, cc, t0:t0 + rows], rhs=wob[s][:, cc, :], start=(cc == 0), stop=(cc == 15)),
                    r=["mxin", f"wob{s}"], w=[psk(pk)])
            P.op("dve", lambda e, pk=pk, tt=tt, rows=rows, g=g: e.tensor_tensor(
                out=ypre[0:rows, tt, 512 * g:512 * g + 512], in0=ypre[0:rows, tt, 512 * g:512 * g + 512], in1=PS[pk][0:rows, 0:512], op=ALU.add),
                r=[psk(pk), ("ypre", tt)], w=[("ypre", tt)])
    for tt in range(9):
        rows = 128 if tt < 8 else 4
        P.op("act", lambda e, tt=tt, rows=rows: e.activation(out=junk2[0:rows, :], in_=ypre[0:rows, tt, :], func=AF.Square,
                                                             accum_out=st2[0:rows, tt, 0:1]), r=[("ypre", tt)], w=["junk2", ("st2", tt)])
        P.op("dve", lambda e, tt=tt, rows=rows: e.tensor_scalar(out=st2[0:rows, tt, 1:2], in0=st2[0:rows, tt, 0:1], scalar1=1.0 / D, scalar2=EPS,
                                                                op0=ALU.mult, op1=ALU.add), r=[("st2", tt)], w=[("st2", tt)])
        P.op("act", lambda e, tt=tt, rows=rows: e.activation(out=st2[0:rows, tt, 2:3], in_=st2[0:rows, tt, 1:2], func=AF.Sqrt),
             r=[("st2", tt)], w=[("st2", tt)])
        P.op("dve", lambda e, tt=tt, rows=rows: e.reciprocal(out=st2[0:rows, tt, 3:4], in_=st2[0:rows, tt, 2:3]), r=[("st2", tt)], w=[("st2", tt)])
        P.op("dve", lambda e, tt=tt, rows=rows: e.scalar_tensor_tensor(out=ypre[0:rows, tt, :], in0=ypre[0:rows, tt, :], scalar=st2[0:rows, tt, 3:4],
                                                                       in1=fng_bc[0:rows, :], op0=ALU.mult, op1=ALU.mult),
             r=[("ypre", tt), ("st2", tt), "fng_bc"], w=[("ypre", tt)])
        if tt < 8:
            P.dma("sp", lambda e, tt=tt: e.dma_start(out=yp_d[128 * tt:128 * tt + 128, :], in_=ypre[:, tt, :]), r=[("ypre", tt)], w=["yp_out"],
                  chan="yout", wait_all=True)
        else:
            P.dma("sp", lambda e, tt=tt: e.dma_start(out=ys_d[:, :], in_=ypre[0:4, tt, :]), r=[("ypre", tt)], w=["yp_out"],
                  chan="yout", wait_all=True)
    P.flush(barrier=True)
    es_c.close()
    es_p.close()


def _rel_bucket(n):
    n = np.asarray(n, dtype=np.int64)
    nl = np.maximum(n, 16).astype(np.float32)
    large = 16 + (np.log(nl / np.float32(16)) / np.float32(math.log(128 / 16)) * np.float32(16)).astype(np.int32)
    large = np.minimum(large, 31)
    return np.where(n < 16, n, large)


def _constants():
    c = {}
    sq = np.zeros((128, 3, 128), np.float32)
    sq[:, 0, :] = np.eye(128)
    sq[:, 1, :] = np.eye(128)[::-1]
    c["c_sq"] = sq
    c["c_j16"] = np.eye(16, dtype=np.float32)[::-1].copy()
    tri = (np.arange(64)[None, :] >= np.arange(64)[:, None]).astype(np.float32)
    mA = np.zeros((128, 2, 64), np.float32)
    mA[0:64, 0, :] = -tri
    mA[64:128, 1, :] = -tri
    c["c_maskA"] = mA
    mAm = np.zeros((NSPEC, 16), np.float32)
    mAm[0:16, :] = -(np.arange(16)[None, :] >= np.arange(16)[:, None]).astype(np.float32)
    c["c_maskAm"] = mAm
    mc = np.zeros((128, 4), np.float32)
    mc[0:64, 0] = -1.0
    mc[64:128, 1] = -1.0
    mc[0:16, 2] = -1.0
    mc[:, 3] = np.arange(128)
    c["c_mcol"] = mc
    rm = np.ones((1, NT), np.float32)
    rm[0, 0:NR:64] = 0.0
    rm[0, C_META] = 0.0
    rm[0, C_DEC:] = 0.0
    c["c_resetm"] = rm
    oh2 = np.zeros((33, 384), np.float32)
    for j in range(384):
        dist = j - 127
        if dist < 0:
            oh2[32, j] = NEG
        else:
            oh2[int(_rel_bucket(dist)), j] += 1.0
            oh2[31, j] -= 1.0
    c["c_oh2"] = oh2
    ohd = np.zeros((33, 3, 128), np.float32)
    ohd[31, 0, :] = 1.0
    for tok in range(128):
        ohd[int(_rel_bucket(128 - tok)), 1, tok] = 1.0
    ohd[0, 2, 0] = 1.0
    ohd[32, 2, 1:] = NEG
    c["c_ohd"] = ohd
    selv = np.zeros((NSPEC, NDEC, 128), np.float32)
    for s in range(NDEC):
        selv[NMETA + s, s, :] = 1.0
    c["c_selv"] = selv
    p01 = np.zeros((8, 8), np.float32)
    for h in range(4):
        p01[2 * h, h] = 1.0
        p01[2 * h + 1, 4 + h] = 1.0
    c["c_p01"] = p01
    return c


_CACHE = {}


def kernel(x_prompt, x_sample, cache_k, cache_v, state_hgrn, page_table, meta_tokens, rel_bias,
           hgrn_lb, norm_g, w_in, hgrn_norm_g, diff_norm_g, diff_lambda, w_out, final_norm_g):
    f32 = np.float32
    x_prompt = np.asarray(x_prompt, f32)
    x_sample = np.asarray(x_sample, f32)
    cache_k = np.asarray(cache_k, f32)
    cache_v = np.asarray(cache_v, f32)
    state_hgrn = np.asarray(state_hgrn, f32)
    page_table = np.asarray(page_table, np.int32)
    meta_tokens = np.asarray(meta_tokens, f32)
    rel_bias = np.asarray(rel_bias, f32)
    hgrn_lb = np.asarray(hgrn_lb, f32)
    norm_g = np.asarray(norm_g, f32)
    w_in = np.asarray(w_in, f32)
    w_out = np.asarray(w_out, f32)
    hgrn_norm_g = np.asarray(hgrn_norm_g, f32)
    diff_norm_g = np.asarray(diff_norm_g, f32)
    diff_lambda = np.asarray(diff_lambda, f32)
    final_norm_g = np.asarray(final_norm_g, f32)

    if "nc" not in _CACHE:
        _CACHE["nc"] = build_program()
    nc = _CACHE["nc"]
    consts = _constants()
    perm = np.concatenate([np.arange(0, 512), np.arange(1024, 1536), np.arange(512, 1024), np.arange(1536, 2048)])
    wout_p = np.ascontiguousarray(w_out[0][perm, :])
    ck_half = [np.ascontiguousarray(cache_k[0][:, :, 4 * j:4 * j + 4, :]).reshape(NPHYS * 128, 512) for j in range(2)]
    cv_half = [np.ascontiguousarray(cache_v[0][:, :, 4 * j:4 * j + 4, :]).reshape(NPHYS * 128, 512) for j in range(2)]
    win3 = w_in[0].reshape(D, 8, 8, 128)
    win_half = [np.ascontiguousarray(win3[:, :, 4 * j:4 * j + 4, :]).reshape(D, 4096) for j in range(2)]
    in_maps = []
    for c in range(8):
        b, j = c // 2, c % 2
        m = dict(consts)
        m["X"] = np.ascontiguousarray(np.concatenate([x_prompt[b], meta_tokens, x_sample[8 * b:8 * b + 8, 0, :]], 0))
        m["Win"] = win_half[j]
        m["Wout"] = wout_p
        m["Xres"] = np.ascontiguousarray(np.concatenate([x_prompt[b, 1024 * j:1024 * j + 1024],
                                                        x_sample[8 * b + 4 * j:8 * b + 4 * j + 4, 0, :]], 0))
        m["normg"] = norm_g[0].reshape(1, D).copy()
        lb2 = hgrn_lb[:, 512 * j:512 * j + 512].reshape(2, 4, 128)
        m["lbraw"] = np.ascontiguousarray(lb2.transpose(2, 0, 1)).reshape(128, 8)
        m["hgng"] = hgrn_norm_g[0].reshape(128, 1).copy()
        m["dfng"] = diff_norm_g[0].reshape(128, 1).copy()
        m["fng"] = final_norm_g.reshape(1, D).copy()
        m["dlam"] = diff_lambda[0].reshape(1, 256).copy()
        m["rbx"] = np.ascontiguousarray(np.concatenate([rel_bias[:, 4 * j:4 * j + 4], np.ones((1, 4), f32)], 0))
        m["pt"] = np.ascontiguousarray(page_table[8 * b:8 * b + 8].reshape(1, 512))
        m["ck"] = ck_half[j]
        m["cv"] = cv_half[j]
        m["sh"] = np.ascontiguousarray(state_hgrn[0, 8 * b:8 * b + 8, 4 * j:4 * j + 4])
        m["c_idxm"] = ((np.arange(2)[None, :] * 128 + np.arange(128)[:, None]) * 2 + j).astype(np.int32)
        in_maps.append(m)
    res = run_bass_kernel_spmd(nc, in_maps, core_ids=list(range(8)))
    R = res.results
    y_prompt = np.zeros((4, NR, D), f32)
    y_sample = np.zeros((32, 1, D), f32)
    k_prompt = np.zeros((1, 4, NR + NMETA, 8, 128), f32)
    v_prompt = np.zeros((1, 4, NR + NMETA, 8, 128), f32)
    s_prompt = np.zeros((1, 4, 8, 128, 128), f32)
    k_sample = np.zeros((1, 32, 1, 8, 128), f32)
    v_sample = np.zeros((1, 32, 1, 8, 128), f32)
    s_sample = np.zeros((1, 32, 8, 128, 128), f32)
    for c in range(8):
        b, j = c // 2, c % 2
        r = R[c]
        y_prompt[b, 1024 * j:1024 * j + 1024] = r["yp"]
        y_sample[8 * b + 4 * j:8 * b + 4 * j + 4, 0] = r["ys"]
        k_prompt[0, b, :, 4 * j:4 * j + 4, :] = r["kp"].reshape(NR + NMETA, 4, 128)
        v_prompt[0, b, :, 4 * j:4 * j + 4, :] = r["vp"].reshape(NR + NMETA, 4, 128)
        s_prompt[0, b, 4 * j:4 * j + 4] = r["sp"]
        k_sample[0, 8 * b:8 * b + 8, 0, 4 * j:4 * j + 4, :] = r["ks"].reshape(8, 4, 128)
        v_sample[0, 8 * b:8 * b + 8, 0, 4 * j:4 * j + 4, :] = r["vs"].reshape(8, 4, 128)
        s_sample[0, 8 * b:8 * b + 8, 4 * j:4 * j + 4] = r["ss"]
    return (y_prompt, y_sample, k_prompt, v_prompt, s_prompt, k_sample, v_sample, s_sample)
```

```python
import math
from contextlib import ExitStack
import numpy as np
import concourse.bass as bass
import concourse.mybir as mybir
from concourse.bass_utils import run_bass_kernel_spmd

F32 = mybir.dt.float32
BF16 = mybir.dt.bfloat16
I32 = mybir.dt.int32
AF = mybir.ActivationFunctionType
ALU = mybir.AluOpType
AX = mybir.AxisListType

D = 2048
KC = 16
NR = 2048
NMETA = 16
NDEC = 8
NT = NR + NMETA + NDEC
C_META = NR
C_DEC = NR + NMETA
NSPEC = NMETA + NDEC
RANGES = [(0, 512), (512, 512), (1024, 512), (1536, 512), (2048, NSPEC)]
NPAGE = 64
NPHYS = 2560
EPS = 1e-6
NEG = -30000.0
LAM_INIT = 0.8 - 0.6 * math.exp(-0.3 * 0)
ENGS = ["pe", "act", "dve", "pool", "sp"]


class Prog:
    def __init__(self, nc):
        self.nc = nc
        self.h = {"pe": nc.tensor, "act": nc.scalar, "dve": nc.vector, "pool": nc.gpsimd, "sp": nc.sync}
        self.sem = {e: nc.alloc_semaphore(name=f"eng_{e}") for e in ENGS}
        self.pending = {e: [] for e in ENGS}
        self.base = {e: 0 for e in ENGS}
        self.sigcount = {e: 0 for e in ENGS}
        self.sigpos = {e: {} for e in ENGS}
        self.res = {}
        self.chan = {}
        self.waited = {e: {} for e in ENGS}

    def _deps(self, eng, r, w):
        def same(t):
            return False
        deps = set()
        for k in r:
            st = self.res.get(k)
            if st and st[0] is not None:
                deps.add(st[0])
        for k in w:
            st = self.res.get(k)
            if st:
                if st[0] is not None and not same(st[0]):
                    deps.add(st[0])
                for t in st[1].values():
                    if not same(t):
                        deps.add(t)
        return deps

    def _update(self, tok, rkey, r, w):
        for k in r:
            st = self.res.setdefault(k, [None, {}])
            st[1][rkey] = tok
        for k in w:
            self.res[k] = [tok, {}]

    def op(self, eng, fn, r=(), w=()):
        w = list(w) + [k for k in r if isinstance(k, tuple) and k[0] == "ps" and k not in w]
        deps = self._deps(eng, r, w)
        gidx = self.base[eng] + len(self.pending[eng])
        tok = ("e", eng, gidx)
        if eng == "pe":
            deps = {d for d in deps if not (d[0] == "e" and d[1] == "pe")}
        self.pending[eng].append(dict(fn=fn, deps=deps, sig=False, chan=None))
        self._update(tok, ("e", eng), r, w)

    def dma(self, eng, fn, r=(), w=(), chan=None, wait_all=False):
        deps = self._deps(eng, r, w)
        if chan not in self.chan:
            self.chan[chan] = [self.nc.alloc_semaphore(name=f"ch_{chan}"), 0]
        c = self.chan[chan]
        c[1] += 1
        tok = ("dall", chan) if wait_all else ("d", chan, 16 * c[1])
        deps = {d for d in deps if not (d[0] == "dall" and d[1] == chan)}
        self.pending[eng].append(dict(fn=fn, deps=deps, sig=False, chan=chan))
        self._update(tok, ("d", chan), r, w)

    def flush(self, barrier=False):
        if barrier:
            for e in ENGS:
                for o in reversed(self.pending[e]):
                    if o["chan"] is None:
                        o["sig"] = True
                        break
        def mark(tok):
            if tok[0] == "e":
                loc = tok[2] - self.base[tok[1]]
                if loc >= 0:
                    self.pending[tok[1]][loc]["sig"] = True
        for e in ENGS:
            for o in self.pending[e]:
                for d in o["deps"]:
                    mark(d)
        for st in self.res.values():
            if st[0] is not None:
                mark(st[0])
            for t in st[1].values():
                mark(t)
        for e in ENGS:
            cnt = self.sigcount[e]
            for i, o in enumerate(self.pending[e]):
                if o["sig"] and o["chan"] is None:
                    cnt += 1
                    self.sigpos[e][self.base[e] + i] = cnt
            self.sigcount[e] = cnt

        def resolve(tok):
            if tok[0] == "e":
                return self.sem[tok[1]], self.sigpos[tok[1]][tok[2]]
            if tok[0] == "d":
                return self.chan[tok[1]][0], tok[2]
            return self.chan[tok[1]][0], 16 * self.chan[tok[1]][1]

        finals = {e: self.sigcount[e] for e in ENGS}
        if barrier:
            self.sigcount["sp"] += 1
        sp_final = self.sigcount["sp"]

        def emit(e, eh):
            wd = self.waited[e]
            for o in self.pending[e]:
                need = {}
                for d in o["deps"]:
                    s, v = resolve(d)
                    key = id(s)
                    if wd.get(key, (None, 0))[1] >= v:
                        continue
                    if key not in need or need[key][1] < v:
                        need[key] = (s, v)
                for key, (s, v) in need.items():
                    eh.wait_ge(s, v)
                    wd[key] = (s, v)
                ins = o["fn"](eh)
                if o["chan"] is not None:
                    ins.then_inc(self.chan[o["chan"]][0], 16)
                elif o["sig"]:
                    ins.then_inc(self.sem[e], 1)
            if barrier:
                if e == "sp":
                    for ch, (sm, cnt) in self.chan.items():
                        if cnt > 0:
                            eh.wait_ge(sm, 16 * cnt)
                    for e2 in ENGS:
                        if e2 != "sp" and finals[e2] > 0:
                            eh.wait_ge(self.sem[e2], finals[e2])
                    eh.nop().then_inc(self.sem["sp"], 1)
                else:
                    eh.wait_ge(self.sem["sp"], sp_final)

        with self.nc.Block() as block:
            @block.tensor
            def _(eh):
                emit("pe", eh)

            @block.scalar
            def _(eh):
                emit("act", eh)

            @block.vector
            def _(eh):
                emit("dve", eh)

            @block.gpsimd
            def _(eh):
                emit("pool", eh)

            @block.sync
            def _(eh):
                emit("sp", eh)
        for e in ENGS:
            self.base[e] += len(self.pending[e])
            self.pending[e] = []

    def final_wait(self, keys):
        toks = set()
        for k in keys:
            st = self.res.get(k)
            if st and st[0] is not None:
                toks.add(st[0])
        self.pending["sp"].append(dict(fn=lambda eh: eh.nop(), deps=toks, sig=False, chan=None))


class _Stop(Exception):
    pass


def build_program(nphys=NPHYS, stop=None):
    nc = bass.Bass("TRN2", target_bir_lowering=False)
    P = Prog(nc)
    stacks = []
    try:
        _build_body(nc, P, stacks, nphys, stop)
    except _Stop:
        P.flush(barrier=True)
    for es in reversed(stacks):
        es.close()
    return nc


def _build_body(nc, P, stacks, nphys, stop):
    def ExitStackT():
        es = ExitStack()
        stacks.append(es)
        return es

    def chk(tag):
        if stop == tag:
            raise _Stop()

    def din(name, shape, dt=F32):
        return nc.dram_tensor(name, list(shape), dt, kind="ExternalInput")

    def dout(name, shape, dt=F32):
        return nc.dram_tensor(name, list(shape), dt, kind="ExternalOutput")

    X = din("X", [NT, D]).ap()
    Win = din("Win", [D, 4096]).ap()
    Wout = din("Wout", [D, D]).ap()
    Xres = din("Xres", [1028, D]).ap()
    normg_d = din("normg", [1, D]).ap()
    lbraw_d = din("lbraw", [128, 8]).ap()
    hgng_d = din("hgng", [128, 1]).ap()
    dfng_d = din("dfng", [128, 1]).ap()
    fng_d = din("fng", [1, D]).ap()
    dlam_d = din("dlam", [1, 256]).ap()
    rbx_d = din("rbx", [33, 4]).ap()
    pt_d = din("pt", [1, 512], I32).ap()
    ck_d = din("ck", [nphys * 128, 512]).ap()
    cv_d = din("cv", [nphys * 128, 512]).ap()
    sh_d = din("sh", [NDEC, 4, 128, 128]).ap()
    c_sq_d = din("c_sq", [128, 3, 128]).ap()
    c_j16_d = din("c_j16", [16, 16]).ap()
    c_maskA_d = din("c_maskA", [128, 2, 64]).ap()
    c_maskAm_d = din("c_maskAm", [NSPEC, 16]).ap()
    c_mcol_d = din("c_mcol", [128, 4]).ap()
    c_resetm_d = din("c_resetm", [1, NT]).ap()
    c_oh2_d = din("c_oh2", [33, 384]).ap()
    c_ohd_d = din("c_ohd", [33, 3, 128]).ap()
    c_selv_d = din("c_selv", [NSPEC, NDEC, 128]).ap()
    c_p01_d = din("c_p01", [8, 8]).ap()
    c_idxm_d = din("c_idxm", [128, 2], I32).ap()

    yp_d = dout("yp", [1024, D]).ap()
    ys_d = dout("ys", [4, D]).ap()
    kp_d = dout("kp", [NR + NMETA, 512]).ap()
    vp_d = dout("vp", [NR + NMETA, 512]).ap()
    sp_d = dout("sp", [4, 128, 128]).ap()
    ks_d = dout("ks", [NDEC, 512]).ap()
    vs_d = dout("vs", [NDEC, 512]).ap()
    ss_d = dout("ss", [NDEC, 4, 128, 128]).ap()

    u_scr = nc.dram_tensor("u_scr", [4, 384], F32)
    snd_d = nc.dram_tensor("snd", [1024, 2056], BF16).ap()
    rcv_ds = [nc.dram_tensor(f"rcv{i}", [256, 2056], BF16).ap() for i in range(8)]

    def mk(es):
        return lambda name, shape, dt: es.enter_context(nc.sbuf_tensor("s_" + name, list(shape), dt))

    es_p = ExitStackT()
    sb = mk(es_p)
    es_0 = ExitStackT()
    sb0 = mk(es_0)
    identb = sb("identb", [128, 128], BF16)
    Jb = sb("Jb", [128, 128], BF16)
    J16b = sb("J16b", [16, 16], BF16)
    onesb = sb("onesb", [128, 128], BF16)
    onesf = sb("onesf", [128, 128], F32)
    maskA = sb("maskA", [128, 2, 64], F32)
    maskAm = sb("maskAm", [NSPEC, 16], F32)
    mcol = sb("mcol", [128, 4], F32)
    resetm = sb("resetm", [128, NT], BF16)
    lbt = sb("lbt", [128, 4], F32)
    omlt = sb("omlt", [128, 4], F32)
    hgng = sb("hgng", [128, 1], F32)
    dfng = sb("dfng", [128, 1], F32)
    cn = sb("cn", [128, 1], F32)
    lamt = sb("lamt", [128, 2], F32)
    rbx = sb("rbx", [33, 4], F32)
    selv = sb("selv", [NSPEC, NDEC, 128], F32)
    comb = sb("comb", [8, 4], F32)
    Hm = sb("Hm", [128, 4, 2, 128], BF16)
    b31t = sb("b31t", [128, 4], F32)
    bsm = sb("bsm", [128, 3, 4], F32)
    idxg = sb("idxg", [128, 512], I32)
    idxm = sb("idxm", [128, 2], I32)
    kspec = sb("kspec", [NSPEC, 4, 128], F32)
    vspec = sb("vspec", [NSPEC, 4, 128], F32)
    qspec = sb("qspec", [NSPEC, 4, 128], F32)
    gdec = sb("gdec", [128, 4, NDEC], F32)
    odec = sb("odec", [128, 4, NDEC], F32)

    csq = sb0("csq", [128, 3, 128], F32)
    cj16 = sb0("cj16", [16, 16], F32)
    lbraw = sb0("lbraw", [128, 8], F32)
    dlam = sb0("dlam", [1, 256], F32)
    lsm = sb0("lsm", [1, 136], F32)
    oh2 = sb0("oh2", [33, 384], F32)
    ohd = sb0("ohd", [33, 3, 128], F32)
    p01 = sb0("p01", [8, 8], F32)
    ut = sb0("ut", [4, 384], F32)
    ptb = sb0("ptb", [128, 512], I32)
    ptf = sb0("ptf", [128, 512], F32)

    PS = [es_p.enter_context(nc.psum_tensor(f"ps{k}", [128, 512], F32)) for k in range(8)]

    def psk(k):
        return ("ps", k)

    def cload(dst, src, name, eng="sp"):
        P.dma(eng, lambda e, d=dst, s=src: e.dma_start(out=d, in_=s), w=[name], chan="const" if eng == "sp" else "constp", wait_all=True)

    cload(csq[:], c_sq_d, "csq")
    cload(cj16[:], c_j16_d, "cj16")
    cload(maskA[:], c_maskA_d, "maskA")
    cload(maskAm[:], c_maskAm_d, "maskAm")
    cload(mcol[:], c_mcol_d, "mcol")
    cload(resetm[:], c_resetm_d.partition_broadcast(128), "resetm", eng="pool")
    cload(lbraw[:], lbraw_d, "lbraw")
    cload(hgng[:], hgng_d, "hgng")
    cload(dfng[:], dfng_d, "dfng")
    cload(dlam[:], dlam_d, "dlam")
    cload(rbx[:], rbx_d, "rbx")
    cload(oh2[:], c_oh2_d, "oh2")
    cload(ohd[:], c_ohd_d, "ohd")
    cload(selv[:], c_selv_d, "selv")
    cload(p01[:], c_p01_d, "p01")
    cload(b31t[:], rbx_d[31:32, :].partition_broadcast(128), "b31t")
    cload(ptb[:], pt_d.partition_broadcast(128), "ptb")
    cload(idxm[:], c_idxm_d, "idxm")

    P.op("dve", lambda e: e.tensor_copy(out=identb[:], in_=csq[:, 0, :]), r=["csq"], w=["identb"])
    P.op("dve", lambda e: e.tensor_copy(out=Jb[:], in_=csq[:, 1, :]), r=["csq"], w=["Jb"])
    P.op("dve", lambda e: e.tensor_copy(out=J16b[:], in_=cj16[:]), r=["cj16"], w=["J16b"])
    P.op("pool", lambda e: e.memset(onesb[:], 1.0), w=["onesb"])
    P.op("pool", lambda e: e.memset(onesf[:], 1.0), w=["onesf"])
    P.op("dve", lambda e: e.tensor_tensor(out=lbt[:], in0=lbraw[:, 0:4], in1=lbraw[:, 4:8], op=ALU.subtract),
         r=["lbraw"], w=["lbt"])
    P.op("act", lambda e: e.activation(out=lbt[:], in_=lbt[:], func=AF.Sigmoid), r=["lbt"], w=["lbt"])
    P.op("dve", lambda e: e.tensor_scalar(out=omlt[:], in0=lbt[:], scalar1=-1.0, scalar2=1.0, op0=ALU.mult, op1=ALU.add),
         r=["lbt"], w=["omlt"])
    P.op("dve", lambda e: e.tensor_scalar(out=cn[:], in0=dfng[:], scalar1=float(1.0 - LAM_INIT), scalar2=None, op0=ALU.mult),
         r=["dfng"], w=["cn"])
    dl4 = dlam[:].rearrange("o (a b d) -> o a b d", a=2, b=2)
    P.op("dve", lambda e: e.tensor_tensor(out=lsm[:, 0:128].rearrange("o (a d) -> o a d", a=2), in0=dl4[:, :, 0, :],
                                          in1=dl4[:, :, 1, :], op=ALU.mult), r=["dlam"], w=["lsm"])
    P.op("dve", lambda e: e.tensor_reduce(out=lsm[:, 128:130], in_=lsm[:, 0:128].rearrange("o (a d) -> o a d", a=2),
                                          axis=AX.X, op=ALU.add), r=["lsm"], w=["lsm"])
    P.op("act", lambda e: e.activation(out=lsm[:, 130:132], in_=lsm[:, 128:130], func=AF.Exp), r=["lsm"], w=["lsm"])
    P.op("dve", lambda e: e.tensor_tensor(out=lsm[:, 132:133], in0=lsm[:, 130:131], in1=lsm[:, 131:132], op=ALU.subtract),
         r=["lsm"], w=["lsm"])
    P.op("dve", lambda e: e.tensor_scalar(out=lsm[:, 133:134], in0=lsm[:, 132:133], scalar1=float(LAM_INIT), scalar2=None,
                                          op0=ALU.add), r=["lsm"], w=["lsm"])
    P.op("pe", lambda e: e.matmul(PS[0][:, 0:1], lhsT=onesf[0:1, :], rhs=lsm[:, 133:134], start=True, stop=True),
         r=["onesf", "lsm"], w=[psk(0)])
    P.op("dve", lambda e: e.tensor_copy(out=lamt[:, 0:1], in_=PS[0][:, 0:1]), r=[psk(0)], w=["lamt"])
    P.op("dve", lambda e: e.tensor_scalar(out=lamt[:, 1:2], in0=lamt[:, 0:1], scalar1=-1.0, scalar2=None, op0=ALU.mult),
         r=["lamt"], w=["lamt"])
    P.op("dve", lambda e: e.scalar_tensor_tensor(out=comb[:], in0=p01[:, 4:8], scalar=lamt[0:8, 1:2], in1=p01[:, 0:4],
                                                 op0=ALU.mult, op1=ALU.add), r=["p01", "lamt"], w=["comb"])
    P.op("pe", lambda e: e.matmul(PS[1][0:4, 0:384], lhsT=rbx[:], rhs=oh2[:], start=True, stop=True),
         r=["rbx", "oh2"], w=[psk(1)])
    P.op("dve", lambda e: e.tensor_copy(out=ut[:], in_=PS[1][0:4, 0:384]), r=[psk(1)], w=["ut"])
    P.dma("sp", lambda e: e.dma_start(out=u_scr.ap(), in_=ut[:]), r=["ut"], w=["u_scr"], chan="uscr")
    for h in range(4):
        for oi in range(2):
            P.dma("pool", lambda e, h=h, oi=oi: e.dma_start(
                out=Hm[:, h, oi, :], in_=bass.AP(u_scr, h * 384 + 128 * oi, [[1, 128], [1, 128]])),
                r=["u_scr"], w=["Hm"], chan="hm", wait_all=True)
    for kind in range(3):
        P.op("pe", lambda e, kind=kind: e.matmul(PS[2][:, 4 * kind:4 * kind + 4], lhsT=ohd[:, kind, :], rhs=rbx[:],
                                                 start=True, stop=True), r=["ohd", "rbx"], w=[psk(2)])
    P.op("dve", lambda e: e.tensor_copy(out=bsm[:].rearrange("p a b -> p (a b)"), in_=PS[2][:, 0:12]), r=[psk(2)], w=["bsm"])
    P.op("dve", lambda e: e.tensor_copy(out=ptf[:], in_=ptb[:]), r=["ptb"], w=["ptf"])
    P.op("dve", lambda e: e.tensor_scalar(out=ptf[:], in0=ptf[:], scalar1=128.0, scalar2=mcol[:, 3:4], op0=ALU.mult, op1=ALU.add),
         r=["ptf", "mcol"], w=["ptf"])
    P.op("dve", lambda e: e.tensor_copy(out=idxg[:], in_=ptf[:]), r=["ptf"], w=["idxg"])
    P.flush(barrier=True)
    es_0.close()
    chk("p0")
    es_m = ExitStackT()
    mixedT = mk(es_m)("mixedT", [128, 8, 2056], BF16)
    es_b = ExitStackT()
    sbb = mk(es_b)
    NKB = 6
    kvbuf = [sbb(f"kvbuf{s}", [128, 512], F32) for s in range(NKB)]
    k64 = sbb("k64", [128, 512], F32)
    v64 = sbb("v64", [128, 512], F32)
    qbc = sbb("qbc", [128, 512], F32)
    prod = sbb("prod", [128, 512], F32)
    scall = sbb("scall", [128, 65, 8], F32)
    ball = sbb("ball", [128, 65, 8], F32)
    eall = sbb("eall", [128, 65, 8], F32)
    esum = sbb("esum", [128, 8], F32)
    Rn = sbb("Rn", [8, 512], F32)
    rzd = sbb("rzd", [8, 1], F32)
    sqb = sbb("sqb", [128, 512], F32)
    rsb = sbb("rsb", [128, 512], F32)
    gtmp = [sbb(f"gtmp{s}", [128, 512], F32) for s in range(1)]

    def decode_gen():
        P.op("pool", lambda e: e.memset(k64[:], 0.0), w=["k64"])
        P.op("pool", lambda e: e.memset(v64[:], 0.0), w=["v64"])
        ball4 = ball[:].rearrange("p g (h c) -> p g h c", c=2)
        P.op("dve", lambda e: e.tensor_copy(out=ball4[:, 0:63, :, :], in_=bsm[:, 0:1, :].unsqueeze(3).to_broadcast([128, 63, 4, 2])),
             r=["bsm"], w=["ball"])
        P.op("dve", lambda e: e.tensor_copy(out=ball4[:, 63:65, :, :], in_=bsm[:, 1:3, :].unsqueeze(3).to_broadcast([128, 2, 4, 2])),
             r=["bsm"], w=["ball"])
        gctr = 0
        for s in range(NDEC):
            P.op("pe", lambda e, s=s: e.matmul(PS[1][:, 0:512], lhsT=selv[:, s, :], rhs=qspec[:].rearrange("p h d -> p (h d)"),
                                               start=True, stop=True), r=["selv", ("spec", id(qspec))], w=[psk(1)])
            P.op("act", lambda e: e.copy(out=qbc[:], in_=PS[1][:, 0:512]), r=[psk(1)], w=["qbc"])
            P.dma("sp", lambda e, s=s: e.dma_start(out=k64[0:1, :], in_=kspec[NMETA + s:NMETA + s + 1, :, :].rearrange("p h d -> p (h d)")),
                  r=[("spec", id(kspec))], w=["k64"], chan="k64")
            P.dma("sp", lambda e, s=s: e.dma_start(out=v64[0:1, :], in_=vspec[NMETA + s:NMETA + s + 1, :, :].rearrange("p h d -> p (h d)")),
                  r=[("spec", id(vspec))], w=["v64"], chan="v64")
            for pg in range(65):
                if pg < 64:
                    sl = gctr % NKB
                    gctr += 1
                    col = s * 64 + pg
                    P.dma("pool", lambda e, sl=sl, col=col: e.indirect_dma_start(
                        out=kvbuf[sl][:, :], out_offset=None, in_=ck_d,
                        in_offset=bass.IndirectOffsetOnAxis(ap=idxg[:, col:col + 1], axis=0)), r=["idxg"], w=[f"kvbuf{sl}"], chan=f"kvbuf{sl}")
                    src, skey = kvbuf[sl], f"kvbuf{sl}"
                else:
                    src, skey = k64, "k64"
                P.op("dve", lambda e, src=src: e.tensor_tensor(out=prod[:], in0=src[:], in1=qbc[:], op=ALU.mult), r=[skey, "qbc"], w=["prod"])
                P.op("dve", lambda e, pg=pg: e.tensor_reduce(out=scall[:, pg, :], in_=prod[:].rearrange("p (g d) -> p g d", d=64), axis=AX.X, op=ALU.add),
                     r=["prod"], w=["scall"])
                yield
            P.op("dve", lambda e: e.tensor_tensor(out=scall[:], in0=scall[:], in1=ball[:], op=ALU.add), r=["scall", "ball"], w=["scall"])
            P.op("act", lambda e: e.activation(out=eall[:], in_=scall[:], func=AF.Exp), r=["scall"], w=["eall"])
            P.op("dve", lambda e: e.tensor_reduce(out=esum[:], in_=eall[:].rearrange("p g k -> p k g"), axis=AX.X, op=ALU.add),
                 r=["eall"], w=["esum"])
            for pg in range(65):
                if pg < 64:
                    sl = gctr % NKB
                    gctr += 1
                    col = s * 64 + pg
                    P.dma("pool", lambda e, sl=sl, col=col: e.indirect_dma_start(
                        out=kvbuf[sl][:, :], out_offset=None, in_=cv_d,
                        in_offset=bass.IndirectOffsetOnAxis(ap=idxg[:, col:col + 1], axis=0)), r=["idxg"], w=[f"kvbuf{sl}"], chan=f"kvbuf{sl}")
                    src, skey = kvbuf[sl], f"kvbuf{sl}"
                else:
                    src, skey = v64, "v64"
                P.op("pe", lambda e, pg=pg, src=src: e.matmul(PS[0][0:8, 0:512], lhsT=eall[:, pg, :], rhs=src[:], start=(pg == 0), stop=(pg == 64)),
                     r=["eall", skey], w=[psk(0)])
                yield
            P.op("pe", lambda e: e.matmul(PS[1][0:8, 0:1], lhsT=esum[:], rhs=onesf[:, 0:1], start=True, stop=True),
                 r=["esum", "onesf"], w=[psk(1)])
            P.op("dve", lambda e: e.reciprocal(out=rzd[:], in_=PS[1][0:8, 0:1]), r=[psk(1)], w=["rzd"])
            P.op("dve", lambda e: e.tensor_scalar(out=Rn[:], in0=PS[0][0:8, 0:512], scalar1=rzd[:, 0:1], scalar2=None, op0=ALU.mult),
                 r=[psk(0), "rzd"], w=["Rn"])
            for h in range(4):
                P.op("pe", lambda e, h=h: e.matmul(PS[1][:, h:h + 1], lhsT=Rn[:, 128 * h:128 * h + 128], rhs=comb[:, h:h + 1], start=True, stop=True),
                     r=["Rn", "comb"], w=[psk(1)])
            P.op("act", lambda e, s=s: e.copy(out=odec[:, :, s], in_=PS[1][:, 0:4]), r=[psk(1)], w=["odec"])
            yield

    dgen = [None]

    def pump(n):
        if dgen[0] is None:
            return
        for _ in range(n):
            try:
                next(dgen[0])
            except StopIteration:
                dgen[0] = None
                return

    es_a = ExitStackT()
    sb = mk(es_a)

    xnT = sb("xnT", [128, KC, NT], BF16)
    wblk = [sb(f"wblk{s}", [128, KC, 128], BF16) for s in range(3)]
    wctr = [0]
    es_a1 = ExitStackT()
    sb = mk(es_a1)
    normg_bc = sb("normg_bc", [128, D], F32)
    xin = [sb(f"xin{s}", [128, D], F32) for s in range(2)]
    xnb = [sb(f"xnb{s}", [128, D], BF16) for s in range(2)]
    junk = sb("junk", [128, D], BF16)
    stat = sb("stat", [128, 17, 4], F32)

    P.dma("sp", lambda e: e.dma_start(out=normg_bc[:], in_=normg_d.partition_broadcast(128)), w=["normg_bc"], chan="ngbc")

    for i in range(17):
        s = i % 2
        rows = 128 if i < 16 else NSPEC
        r0 = 128 * i
        P.dma("sp", lambda e, s=s, rows=rows, r0=r0: e.dma_start(out=xin[s][0:rows, :], in_=X[r0:r0 + rows, :]),
              w=[f"xin{s}"], chan=f"xin{s}")
        P.op("act", lambda e, s=s, rows=rows, i=i: e.activation(out=junk[0:rows, :], in_=xin[s][0:rows, :], func=AF.Square,
                                                                accum_out=stat[0:rows, i, 0:1]),
             r=[f"xin{s}"], w=["junk", ("stat", i)])
        P.op("dve", lambda e, rows=rows, i=i: e.tensor_scalar(out=stat[0:rows, i, 1:2], in0=stat[0:rows, i, 0:1],
                                                              scalar1=1.0 / D, scalar2=EPS, op0=ALU.mult, op1=ALU.add),
             r=[("stat", i)], w=[("stat", i)])
        P.op("act", lambda e, rows=rows, i=i: e.activation(out=stat[0:rows, i, 2:3], in_=stat[0:rows, i, 1:2], func=AF.Sqrt),
             r=[("stat", i)], w=[("stat", i)])
        P.op("dve", lambda e, rows=rows, i=i: e.reciprocal(out=stat[0:rows, i, 3:4], in_=stat[0:rows, i, 2:3]),
             r=[("stat", i)], w=[("stat", i)])
        P.op("dve", lambda e, s=s, rows=rows, i=i: e.scalar_tensor_tensor(
            out=xnb[s][0:rows, :], in0=xin[s][0:rows, :], scalar=stat[0:rows, i, 3:4], in1=normg_bc[0:rows, :],
            op0=ALU.mult, op1=ALU.mult), r=[f"xin{s}", ("stat", i), "normg_bc"], w=[f"xnb{s}"])
        for g4 in range(4):
            pk = g4 % 2
            pT = PS[pk][:].bitcast(BF16)
            for q in range(4):
                kc = 4 * g4 + q
                P.op("pe", lambda e, s=s, rows=rows, kc=kc, q=q, pT=pT: e.transpose(
                    out=pT[:, q * 128:q * 128 + rows], in_=xnb[s][0:rows, kc * 128:(kc + 1) * 128],
                    identity=identb[0:rows, 0:rows]), r=[f"xnb{s}", "identb"], w=[psk(pk)])
            eng = "act" if g4 % 2 == 0 else "dve"
            src = pT[:, 0:512].rearrange("p (q t) -> p q t", q=4)[:, :, 0:rows]
            dst = xnT[:, 4 * g4:4 * g4 + 4, r0:r0 + rows]
            if eng == "act":
                P.op("act", lambda e, src=src, dst=dst: e.copy(out=dst, in_=src), r=[psk(pk)], w=[("xnT", i)])
            else:
                P.op("dve", lambda e, src=src, dst=dst: e.tensor_copy(out=dst, in_=src), r=[psk(pk)], w=[("xnT", i)])

    P.flush(barrier=True)
    es_a1.close()
    chk("a1")
    es_a2 = ExitStackT()
    sb = mk(es_a2)

    def tiles_of(t0, n):
        return [("xnT", i) for i in range(17) if not (128 * i >= t0 + n or (128 * i + (128 if i < 16 else NSPEC)) <= t0)]

    wlist = []
    for hh in range(4):
        for g8 in (4, 5, 6):
            wlist.append(512 * g8 + 128 * hh)
    for hh in range(4):
        for g8 in (1, 0, 2, 3):
            wlist.append(512 * g8 + 128 * hh)
    for hh in range(4):
        for g8 in (5, 6, 4, 7):
            wlist.append(512 * g8 + 128 * hh)
    wiss = [0]

    def issue_next_w():
        k = wiss[0]
        if k >= len(wlist):
            return
        s = k % 3
        src = Win[:, wlist[k]:wlist[k] + 128].rearrange("(kc p) c -> p kc c", p=128)
        P.dma("pool", lambda e, s=s, src=src: e.dma_start(out=wblk[s][:], in_=src), w=[f"wblk{s}"], chan=f"wblk{s}")
        wiss[0] += 1

    def load_w(col0):
        k = wctr[0]
        assert wlist[k] == col0, (k, wlist[k], col0)
        while wiss[0] <= k + 2 and wiss[0] < len(wlist):
            issue_next_w()
        wctr[0] += 1
        return k % 3

    pctr = [0]

    def proj_F(ws, consume, ranges=RANGES):
        for (t0, n) in ranges:
            pk = 2 + (pctr[0] % 2)
            pctr[0] += 1
            for kc in range(KC):
                P.op("pe", lambda e, pk=pk, ws=ws, kc=kc, t0=t0, n=n: e.matmul(
                    PS[pk][:, 0:n], lhsT=wblk[ws][:, kc, :], rhs=xnT[:, kc, t0:t0 + n], start=(kc == 0), stop=(kc == KC - 1)),
                    r=[f"wblk{ws}"] + tiles_of(t0, n), w=[psk(pk)])
            consume(PS[pk], pk, t0, n)
            pump(2)

    def proj_T(ws, consume, tiles=range(17)):
        for i in tiles:
            rows = 128 if i < 16 else NSPEC
            pk = 2 + (pctr[0] % 2)
            pctr[0] += 1
            for kc in range(KC):
                P.op("pe", lambda e, pk=pk, ws=ws, kc=kc, i=i, rows=rows: e.matmul(
                    PS[pk][0:rows, 0:128], lhsT=xnT[:, kc, 128 * i:128 * i + rows], rhs=wblk[ws][:, kc, :],
                    start=(kc == 0), stop=(kc == KC - 1)), r=[f"wblk{ws}", ("xnT", i)], w=[psk(pk)])
            consume(PS[pk], pk, i, rows)
            pump(1)

    for h in range(4):
        for (g8, dstt, scl) in ((4, qspec, 0.125), (5, kspec, 1.0), (6, vspec, 1.0)):
            ws = load_w(512 * g8 + 128 * h)

            def cons_sp(ps, pk, i, rows, h=h, dstt=dstt, scl=scl):
                P.op("act", lambda e: e.mul(out=dstt[:, h, :], in_=ps[0:NSPEC, 0:128], mul=scl), r=[psk(pk)], w=[("spec", id(dstt))])
            proj_T(ws, cons_sp, tiles=[16])
    P.dma("sp", lambda e: e.dma_start(out=ks_d[:, :], in_=kspec[NMETA:NSPEC, :, :].rearrange("p h d -> p (h d)")),
          r=[("spec", id(kspec))], w=["ks_out"], chan="ksout")
    P.dma("sp", lambda e: e.dma_start(out=vs_d[:, :], in_=vspec[NMETA:NSPEC, :, :].rearrange("p h d -> p (h d)")),
          r=[("spec", id(vspec))], w=["vs_out"], chan="vsout")
    dgen[0] = decode_gen()

    fbuf = sb("fbuf", [128, NT], F32)
    L0 = sb("L0", [128, NT], F32)
    L1 = sb("L1", [128, NT], F32)
    nkeT = sb("nkeT", [128, NT], BF16)
    qeT = sb("qeT", [128, NT], BF16)
    vtok = sb("vtok", [128, 17, 128], BF16)
    vspf = sb("vspf", [NSPEC, 128], F32)
    qdec = sb("qdec", [128, NDEC], F32)
    kdec = sb("kdec", [128, NDEC], F32)
    S32 = sb("S32", [128, 128], F32)
    Sbf = sb("Sbf", [128, 128], BF16)
    Stmp = sb("Stmp", [128, 128], F32)
    ATm = [sb(f"ATm{s}", [128, 64], BF16) for s in range(2)]
    ktok = [sb(f"ktok{s}", [128, 128], BF16) for s in range(4)]
    S0t = [sb(f"S0t{s}", [128, 128], F32) for s in range(2)]
    Snt = [sb(f"Snt{s}", [128, 128], F32) for s in range(2)]

    def norm_gate_cols(obuf_ap, okey, n, gain_ap, gain_key, dst_ap, dst_key, pk):
        P.op("act", lambda e: e.activation(out=sqb[:, 0:n], in_=obuf_ap, func=AF.Square), r=[okey], w=["sqb"])
        P.op("pe", lambda e: e.matmul(PS[pk][:, 0:n], lhsT=onesf[:], rhs=sqb[:, 0:n], start=True, stop=True),
             r=["onesf", "sqb"], w=[psk(pk)])
        P.op("dve", lambda e: e.tensor_scalar(out=rsb[:, 0:n], in0=PS[pk][:, 0:n], scalar1=1.0 / 128, scalar2=EPS,
                                              op0=ALU.mult, op1=ALU.add), r=[psk(pk)], w=["rsb"])
        P.op("act", lambda e: e.activation(out=rsb[:, 0:n], in_=rsb[:, 0:n], func=AF.Sqrt), r=["rsb"], w=["rsb"])
        P.op("dve", lambda e: e.reciprocal(out=rsb[:, 0:n], in_=rsb[:, 0:n]), r=["rsb"], w=["rsb"])
        P.op("dve", lambda e: e.scalar_tensor_tensor(out=dst_ap, in0=obuf_ap, scalar=gain_ap, in1=rsb[:, 0:n],
                                                     op0=ALU.mult, op1=ALU.mult), r=[okey, "rsb", gain_key], w=[dst_key])

    for h in range(4):
        ws = load_w(512 * 1 + 128 * h)

        def cons_f(ps, pk, t0, n):
            P.op("act", lambda e: e.activation(out=fbuf[:, t0:t0 + n], in_=ps[:, 0:n], func=AF.Sigmoid), r=[psk(pk)], w=["fbuf"])
        proj_F(ws, cons_f)
        chk("h0")
        P.op("dve", lambda e, h=h: e.tensor_scalar(out=fbuf[:], in0=fbuf[:], scalar1=omlt[:, h:h + 1], scalar2=lbt[:, h:h + 1],
                                                   op0=ALU.mult, op1=ALU.add), r=["fbuf", "omlt", "lbt"], w=["fbuf"])
        P.op("act", lambda e: e.activation(out=L0[:], in_=fbuf[:], func=AF.Ln), r=["fbuf"], w=["L0"])
        chk("h0b")
        P.op("dve", lambda e: e.tensor_tensor_scan(out=L1[:], data0=resetm[:], data1=L0[:], initial=0.0, op0=ALU.mult, op1=ALU.add),
             r=["resetm", "L0"], w=["L1"])
        chk("h0c")
        P.op("act", lambda e: e.activation(out=L0[:], in_=L1[:], func=AF.Exp, scale=-1.0), r=["L1"], w=["L0"])
        P.op("dve", lambda e: e.scalar_tensor_tensor(out=nkeT[:], in0=fbuf[:], scalar=onesf[:, 0:1], in1=L0[:], op0=ALU.subtract, op1=ALU.mult),
             r=["fbuf", "L0", "onesf"], w=["nkeT"])
        P.op("dve", lambda e: e.tensor_scalar(out=kdec[:], in0=fbuf[:, C_DEC:C_DEC + NDEC], scalar1=-1.0, scalar2=1.0,
                                              op0=ALU.mult, op1=ALU.add), r=["fbuf"], w=["kdec"])
        P.op("act", lambda e: e.activation(out=fbuf[:], in_=L1[:], func=AF.Exp), r=["L1", "kdec"], w=["fbuf"])
        chk("h1")
        ws = load_w(512 * 0 + 128 * h)

        def cons_q(ps, pk, t0, n):
            if t0 == C_META:
                P.op("act", lambda e: e.copy(out=qdec[:], in_=ps[:, NMETA:NSPEC]), r=[psk(pk)], w=["qdec"])
            P.op("dve", lambda e: e.tensor_tensor(out=qeT[:, t0:t0 + n], in0=ps[:, 0:n], in1=fbuf[:, t0:t0 + n], op=ALU.mult),
                 r=[psk(pk), "fbuf"], w=["qeT"])
        proj_F(ws, cons_q)
        chk("h2")
        ws = load_w(512 * 2 + 128 * h)

        def cons_v(ps, pk, i, rows):
            P.op("act", lambda e: e.copy(out=vtok[0:rows, i, :], in_=ps[0:rows, 0:128]), r=[psk(pk)], w=["vtok"])
            if i == 16:
                P.op("dve", lambda e: e.tensor_copy(out=vspf[:], in_=ps[0:NSPEC, 0:128]), r=[psk(pk)], w=["vspf"])
        proj_T(ws, cons_v)
        chk("h3")
        P.op("pool", lambda e: e.memset(S32[:], 0.0), w=["S32"])
        P.op("pool", lambda e: e.memset(Sbf[:], 0.0), w=["Sbf"])
        chunks = [("m", 0)] + [("r", c) for c in range(32)]
        kslot = 0
        for ci, (kind, c) in enumerate(chunks):
            if kind == "m":
                c0, n, tile, rows, par = C_META, 16, 16, NSPEC, 2
            else:
                c0, n, tile, rows, par = 64 * c, 64, c // 2, 128, c % 2
            tc0 = 128 * tile
            a = ci % 2
            P.op("pe", lambda e, c0=c0, n=n, tc0=tc0, rows=rows: e.matmul(
                PS[4][0:rows, 0:n], lhsT=nkeT[:, tc0:tc0 + rows], rhs=qeT[:, c0:c0 + n], start=True, stop=True),
                r=["nkeT", "qeT"], w=[psk(4)])
            if kind == "m":
                P.op("dve", lambda e, a=a: e.tensor_tensor(out=ATm[a][0:NSPEC, 0:16], in0=PS[4][0:NSPEC, 0:16], in1=maskAm[:], op=ALU.mult),
                     r=[psk(4), "maskAm"], w=[f"ATm{a}"])
            else:
                P.op("dve", lambda e, a=a, par=par: e.tensor_tensor(out=ATm[a][:, 0:64], in0=PS[4][:, 0:64], in1=maskA[:, par, :], op=ALU.mult),
                     r=[psk(4), "maskA"], w=[f"ATm{a}"])
            if kind == "m" or par == 0:
                pTk = PS[5][:].bitcast(BF16)
                P.op("pe", lambda e, tc0=tc0, rows=rows, pTk=pTk: e.transpose(out=pTk[0:rows, 0:128], in_=nkeT[:, tc0:tc0 + rows],
                                                                              identity=identb[:]), r=["nkeT", "identb"], w=[psk(5)])
                if kind == "m":
                    ks_m = kslot % 4
                    kslot += 1
                    P.op("act", lambda e, ks_m=ks_m, pTk=pTk: e.activation(out=ktok[ks_m][0:NSPEC, :], in_=pTk[0:NSPEC, 0:128], func=AF.Copy,
                                                                           scale=mcol[0:NSPEC, 2:3]), r=[psk(5), "mcol"], w=[f"ktok{ks_m}"])
                    kt_cur = {2: ks_m}
                else:
                    kt_cur = {}
                    for pp in range(2):
                        ks_p = kslot % 4
                        kslot += 1
                        P.op("act", lambda e, ks_p=ks_p, pp=pp, pTk=pTk: e.activation(out=ktok[ks_p][:], in_=pTk[:, 0:128], func=AF.Copy,
                                                                                      scale=mcol[:, pp:pp + 1]), r=[psk(5), "mcol"], w=[f"ktok{ks_p}"])
                        kt_cur[pp] = ks_p
            ksl = kt_cur[par]
            P.op("pe", lambda e, c0=c0, n=n: e.matmul(PS[6][:, 0:n], lhsT=Sbf[:], rhs=qeT[:, c0:c0 + n], start=True, stop=False),
                 r=["Sbf", "qeT"], w=[psk(6)])
            P.op("pe", lambda e, a=a, n=n, tile=tile, rows=rows: e.matmul(PS[6][:, 0:n], lhsT=vtok[0:rows, tile, :], rhs=ATm[a][0:rows, 0:n],
                                                                          start=False, stop=True), r=["vtok", f"ATm{a}"], w=[psk(6)])
            P.op("act", lambda e, c0=c0, n=n: e.copy(out=L1[:, c0:c0 + n], in_=PS[6][:, 0:n]), r=[psk(6)], w=["L1o"])
            P.op("pe", lambda e, ksl=ksl, tile=tile, rows=rows: e.matmul(PS[7][:, 0:128], lhsT=ktok[ksl][0:rows, :], rhs=vtok[0:rows, tile, :],
                                                                         start=True, stop=True), r=[f"ktok{ksl}", "vtok"], w=[psk(7)])
            ebc = fbuf[:, c0 + n - 1:c0 + n]
            P.op("dve", lambda e, ebc=ebc: e.tensor_scalar(out=Stmp[:], in0=PS[7][:, 0:128], scalar1=ebc, scalar2=None, op0=ALU.mult),
                 r=[psk(7), "fbuf"], w=["Stmp"])
            P.op("dve", lambda e, ebc=ebc: e.scalar_tensor_tensor(out=S32[:], in0=S32[:], scalar=ebc, in1=Stmp[:], op0=ALU.mult, op1=ALU.add),
                 r=["S32", "Stmp", "fbuf"], w=["S32"])
            P.op("act", lambda e: e.copy(out=Sbf[:], in_=S32[:]), r=["S32"], w=["Sbf"])
            pump(2)
        P.dma("sp", lambda e, h=h: e.dma_start(out=sp_d[h], in_=S32[:]), r=["S32"], w=["sp_out"], chan="spout")
        chk("h4")
        for s in range(NDEC):
            sl = s % 2
            P.dma("sp", lambda e, s=s, h=h, sl=sl: e.dma_start(out=S0t[sl][:], in_=sh_d[s, h]), w=[f"S0t{sl}"], chan=f"s0t{sl}")
            P.op("pe", lambda e, s=s: e.matmul(PS[4][:, 0:128], lhsT=selv[:, s, :], rhs=vspf[:], start=True, stop=True),
                 r=["selv", "vspf"], w=[psk(4)])
            P.op("dve", lambda e, s=s, sl=sl: e.tensor_scalar(out=Stmp[:], in0=S0t[sl][:], scalar1=fbuf[:, C_DEC + s:C_DEC + s + 1], scalar2=None,
                                                              op0=ALU.mult), r=[f"S0t{sl}", "fbuf"], w=["Stmp"])
            P.op("dve", lambda e, s=s, sl=sl: e.scalar_tensor_tensor(out=Snt[sl][:], in0=PS[4][:, 0:128], scalar=kdec[:, s:s + 1], in1=Stmp[:],
                                                                     op0=ALU.mult, op1=ALU.add), r=[psk(4), "kdec", "Stmp"], w=[f"Snt{sl}"])
            P.dma("sp", lambda e, s=s, h=h, sl=sl: e.dma_start(out=ss_d[s, h], in_=Snt[sl][:]), r=[f"Snt{sl}"], w=["ss_out"], chan=f"snt{sl}")
            P.op("pe", lambda e, s=s, sl=sl: e.matmul(PS[5][:, 0:1], lhsT=Snt[sl][:], rhs=qdec[:, s:s + 1], start=True, stop=True),
                 r=[f"Snt{sl}", "qdec"], w=[psk(5)])
            P.op("act", lambda e, s=s: e.copy(out=L1[:, C_DEC + s:C_DEC + s + 1], in_=PS[5][:, 0:1]), r=[psk(5)], w=["L1o"])
            pump(1)
        chk("h5")
        for (t0, n) in RANGES[:4]:
            norm_gate_cols(L1[:, t0:t0 + n], "L1o", n, hgng[:, 0:1], "hgng", mixedT[:, h, t0:t0 + n], ("mixedT", h), 4)
        norm_gate_cols(L1[:, C_DEC:C_DEC + NDEC], "L1o", NDEC, hgng[:, 0:1], "hgng", mixedT[:, h, NR:NR + NDEC], ("mixedT", h), 4)
        chk("h6")
        ws = load_w(512 * 3 + 128 * h)

        def cons_g(ps, pk, t0, n, h=h):
            gs = 0
            if t0 == C_META:
                P.op("act", lambda e: e.activation(out=gtmp[gs][:, 0:NDEC], in_=ps[:, NMETA:NSPEC], func=AF.Silu), r=[psk(pk)], w=[f"gtmp{gs}"])
                P.op("dve", lambda e: e.tensor_tensor(out=mixedT[:, h, NR:NR + NDEC], in0=mixedT[:, h, NR:NR + NDEC], in1=gtmp[gs][:, 0:NDEC],
                                                      op=ALU.mult), r=[f"gtmp{gs}", ("mixedT", h)], w=[("mixedT", h)])
            else:
                P.op("act", lambda e: e.activation(out=gtmp[gs][:, 0:n], in_=ps[:, 0:n], func=AF.Silu), r=[psk(pk)], w=[f"gtmp{gs}"])
                P.op("dve", lambda e: e.tensor_tensor(out=mixedT[:, h, t0:t0 + n], in0=mixedT[:, h, t0:t0 + n], in1=gtmp[gs][:, 0:n],
                                                      op=ALU.mult), r=[f"gtmp{gs}", ("mixedT", h)], w=[("mixedT", h)])
        proj_F(ws, cons_g)
    P.flush(barrier=True)
    es_a2.close()
    chk("a2")
    es_a3 = ExitStackT()
    sb = mk(es_a3)

    QT = sb("QT", [128, NR], BF16)
    K0T = sb("K0T", [128, NR + NMETA], BF16)
    K1T = sb("K1T", [128, NR + NMETA], BF16)
    Vtk = sb("Vtk", [128, 17, 128], BF16)
    gate = sb("gate", [128, NR], F32)
    kstage = [sb(f"kstage{s}", [128, 128], F32) for s in range(2)]
    vstage = [sb(f"vstage{s}", [128, 128], F32) for s in range(2)]
    Et = [[sb(f"Et{c}{s}", [128, 512], BF16) for s in range(2)] for c in range(2)]
    rz = [sb(f"rz{c}", [128, 512], F32) for c in range(2)]
    tat = [sb(f"tat{c}", [128, 512], F32) for c in range(2)]
    oat = sb("oat", [128, 512], F32)

    P.op("pool", lambda e: e.memset(K0T[:], 0.0), w=["K0T"])
    P.op("pool", lambda e: e.memset(K1T[:], 0.0), w=["K1T"])

    def out_rows(i):
        return (NMETA + 128 * i, 128)

    for h in range(4):
        ws = load_w(512 * 5 + 128 * h)

        def cons_kF(ps, pk, t0, n):
            nn = n if t0 != C_META else NMETA
            P.op("act", lambda e: e.copy(out=K0T[0:64, t0:t0 + nn], in_=ps[0:64, 0:nn]), r=[psk(pk)], w=["K0T"])
            P.op("dve", lambda e: e.tensor_copy(out=K1T[64:128, t0:t0 + nn], in_=ps[64:128, 0:nn]), r=[psk(pk)], w=["K1T"])
        proj_F(ws, cons_kF)
        sctr = [0]

        def cons_kT(ps, pk, i, rows, h=h):
            s = sctr[0] % 2
            sctr[0] += 1
            P.op("act", lambda e: e.copy(out=kstage[s][0:rows, :], in_=ps[0:rows, 0:128]), r=[psk(pk)], w=[f"kstage{s}"])
            if i < 16:
                r0, _ = out_rows(i)
                P.dma("sp", lambda e: e.dma_start(out=kp_d[r0:r0 + 128, 128 * h:128 * h + 128], in_=kstage[s][:]),
                      r=[f"kstage{s}"], w=["kp_out"], chan=f"kstage{s}")
            else:
                P.dma("sp", lambda e: e.dma_start(out=kp_d[0:NMETA, 128 * h:128 * h + 128], in_=kstage[s][0:NMETA, :]),
                      r=[f"kstage{s}"], w=["kp_out"], chan=f"kstage{s}")
        proj_T(ws, cons_kT)
        ws = load_w(512 * 6 + 128 * h)
        sctr2 = [0]

        def cons_vT(ps, pk, i, rows, h=h):
            s = sctr2[0] % 2
            sctr2[0] += 1
            P.op("act", lambda e: e.copy(out=vstage[s][0:rows, :], in_=ps[0:rows, 0:128]), r=[psk(pk)], w=[f"vstage{s}"])
            P.op("dve", lambda e: e.tensor_copy(out=Vtk[0:rows, i, :], in_=ps[0:rows, 0:128]), r=[psk(pk)], w=["Vtk"])
            if i < 16:
                r0, _ = out_rows(i)
                P.dma("sp", lambda e: e.dma_start(out=vp_d[r0:r0 + 128, 128 * h:128 * h + 128], in_=vstage[s][:]),
                      r=[f"vstage{s}"], w=["vp_out"], chan=f"vstage{s}")
            else:
                P.dma("sp", lambda e: e.dma_start(out=vp_d[0:NMETA, 128 * h:128 * h + 128], in_=vstage[s][0:NMETA, :]),
                      r=[f"vstage{s}"], w=["vp_out"], chan=f"vstage{s}")
        proj_T(ws, cons_vT)
        ws = load_w(512 * 4 + 128 * h)

        def cons_qF(ps, pk, t0, n):
            P.op("act", lambda e: e.mul(out=QT[:, t0:t0 + n], in_=ps[:, 0:n], mul=0.125), r=[psk(pk)], w=["QT"])
        proj_F(ws, cons_qF, ranges=RANGES[:4])

        ws = load_w(512 * 7 + 128 * h)

        def cons_gF(ps, pk, t0, n, h=h):
            if t0 == C_META:
                P.op("act", lambda e: e.activation(out=gdec[:, h, :], in_=ps[:, NMETA:NSPEC], func=AF.Silu), r=[psk(pk)], w=["gdec"])
            else:
                P.op("act", lambda e: e.activation(out=gate[:, t0:t0 + n], in_=ps[:, 0:n], func=AF.Silu), r=[psk(pk)], w=["gate"])
        proj_F(ws, cons_gF)

        ectr = 0
        for r in range(4):
            q0 = 512 * r
            ktiles = [("m", 0)] + [("r", j) for j in range(4 * r + 4)]
            for ti, (kind, j) in enumerate(ktiles):
                if kind == "m":
                    kc0, kk, vt, lo = C_META, NMETA, 16, 0
                else:
                    kc0, kk, vt = 128 * j, 128, j
                    lo = 128 * max(0, j - 4 * r)
                n = 512 - lo
                sl = ectr % 2
                ectr += 1
                first = (ti == 0)
                last = (ti == len(ktiles) - 1)
                for c in range(2):
                    KcT = K0T if c == 0 else K1T
                    kname = "K0T" if c == 0 else "K1T"
                    pk = 2 + c
                    extra = []
                    if kind == "m":
                        if r == 0:
                            extra.append((0, 1))
                    else:
                        if j >= 4 * r:
                            extra.append((0, 0))
                            if n >= 256:
                                extra.append((128, 1))
                        elif j == 4 * r - 1:
                            extra.append((0, 1))
                    P.op("pe", lambda e, pk=pk, KcT=KcT, kc0=kc0, kk=kk, q0=q0, lo=lo, n=n, ne=len(extra): e.matmul(
                        PS[pk][0:kk, 0:n], lhsT=KcT[:, kc0:kc0 + kk], rhs=QT[:, q0 + lo:q0 + 512], start=True, stop=(ne == 0)),
                        r=[kname, "QT"], w=[psk(pk)])
                    for xi, (co, oi) in enumerate(extra):
                        if kind == "m":
                            P.op("pe", lambda e, pk=pk, co=co, oi=oi, xi=xi, ne=len(extra), h=h: e.matmul(
                                PS[pk][0:NMETA, co:co + 128], lhsT=J16b[:], rhs=Hm[0:NMETA, h, oi, :], start=False, stop=(xi == ne - 1)),
                                r=["J16b", "Hm"], w=[psk(pk)])
                        else:
                            P.op("pe", lambda e, pk=pk, co=co, oi=oi, xi=xi, ne=len(extra), h=h: e.matmul(
                                PS[pk][:, co:co + 128], lhsT=Jb[:], rhs=Hm[:, h, oi, :], start=False, stop=(xi == ne - 1)),
                                r=["Jb", "Hm"], w=[psk(pk)])
                    P.op("act", lambda e, pk=pk, c=c, sl=sl, kk=kk, n=n, h=h: e.activation(
                        out=Et[c][sl][0:kk, 0:n], in_=PS[pk][0:kk, 0:n], func=AF.Exp, bias=b31t[0:kk, h:h + 1]),
                        r=[psk(pk), "b31t"], w=[f"Et{c}{sl}"])
                for c in range(2):
                    P.op("pe", lambda e, c=c, sl=sl, kk=kk, n=n, lo=lo, vt=vt, first=first, last=last: e.matmul(
                        PS[4 + c][:, lo:512], lhsT=Vtk[0:kk, vt, :], rhs=Et[c][sl][0:kk, 0:n], start=first, stop=last),
                        r=["Vtk", f"Et{c}{sl}"], w=[psk(4 + c)])
                    P.op("pe", lambda e, c=c, sl=sl, kk=kk, n=n, lo=lo, first=first, last=last: e.matmul(
                        PS[6 + c][:, lo:512], lhsT=onesb[0:kk, :], rhs=Et[c][sl][0:kk, 0:n], start=first, stop=last),
                        r=["onesb", f"Et{c}{sl}"], w=[psk(6 + c)])
                pump(2)
            for c in range(2):
                P.op("dve", lambda e, c=c: e.reciprocal(out=rz[c][:], in_=PS[6 + c][:]), r=[psk(6 + c)], w=[f"rz{c}"])
                P.op("dve", lambda e, c=c: e.tensor_tensor(out=tat[c][:], in0=PS[4 + c][:], in1=rz[c][:], op=ALU.mult),
                     r=[psk(4 + c), f"rz{c}"], w=[f"tat{c}"])
            P.op("dve", lambda e: e.scalar_tensor_tensor(out=oat[:], in0=tat[1][:], scalar=lamt[:, 1:2], in1=tat[0][:], op0=ALU.mult, op1=ALU.add),
                 r=["tat0", "tat1", "lamt"], w=["oat"])
            norm_gate_cols(oat[:], "oat", 512, cn[:, 0:1], "cn", tat[0][:], "tat0", 2)
            P.op("dve", lambda e, q0=q0, h=h: e.tensor_tensor(out=mixedT[:, 4 + h, q0:q0 + 512], in0=tat[0][:], in1=gate[:, q0:q0 + 512], op=ALU.mult),
                 r=["tat0", "gate"], w=[("mixedT", 4 + h)])
    pump(100000)
    od2 = odec[:].rearrange("p h s -> p (h s)")
    norm_gate_cols(od2, "odec", 32, cn[:, 0:1], "cn", prod[:, 0:32], "prod", 4)
    P.op("dve", lambda e: e.tensor_tensor(out=mixedT[:, 4:8, NR:NR + NDEC], in0=prod[:, 0:32].rearrange("p (h s) -> p h s", h=4),
                                          in1=gdec[:], op=ALU.mult), r=["prod", "gdec"], w=[("mixedT", k) for k in range(4, 8)])
    P.flush(barrier=True)
    es_a3.close()
    es_a.close()
    chk("a3")
    es_b.close()

    snd3 = snd_d.rearrange("(c p) n -> p c n", p=128)
    MK = [("mixedT", k) for k in range(8)]
    for hf in range(2):
        P.dma("sp", lambda e, hf=hf: e.dma_start(out=snd3[:, :, 1028 * hf:1028 * hf + 1024], in_=mixedT[:, :, 1024 * hf:1024 * hf + 1024]),
              r=MK, w=["snd"], chan="snd", wait_all=True)
        P.dma("sp", lambda e, hf=hf: e.dma_start(out=snd3[:, :, 1028 * hf + 1024:1028 * hf + 1028], in_=mixedT[:, :, NR + 4 * hf:NR + 4 * hf + 4]),
              r=MK, w=["snd"], chan="snd", wait_all=True)
    P.flush(barrier=True)
    chk("c1")
    es_m.close()
    es_c = ExitStackT()
    sb = mk(es_c)
    mxin = sb("mxin", [128, 16, 1028], BF16)
    ypre = sb("ypre", [128, 9, D], F32)
    fng_bc = sb("fng_bc", [128, D], F32)
    wob = [sb(f"wob{s}", [128, 16, 512], BF16) for s in range(2)]
    st2 = sb("st2", [128, 9, 4], F32)
    junk2 = sb("junk2", [128, D], BF16)
    ccsem = nc.alloc_semaphore(name="ccsem")
    with nc.Block() as block:
        @block.gpsimd
        def _(g):
            for i in range(8):
                g.collective_compute("AllGather", ALU.bypass, replica_groups=[[0, 1], [2, 3], [4, 5], [6, 7]],
                                     ins=[snd_d[128 * i:128 * i + 128, :]], outs=[rcv_ds[i]]).then_inc(ccsem, 1)
            g.wait_ge(ccsem, 8)
    chk("c2")
    P.dma("sp", lambda e: e.dma_start(out=fng_bc[:], in_=fng_d.partition_broadcast(128)), w=["fng_bc"], chan="fng")
    for cc in range(16):
        rk, ch = cc // 8, cc % 8
        rcv_rows = rcv_ds[ch].rearrange("r (t n) -> (r t) n", t=2)
        P.dma("pool", lambda e, cc=cc, rk=rk, rcv_rows=rcv_rows: e.indirect_dma_start(
            out=mxin[:, cc, :], out_offset=None, in_=rcv_rows,
            in_offset=bass.IndirectOffsetOnAxis(ap=idxm[:, rk:rk + 1], axis=0)), r=["idxm"], w=["mxin"], chan="mxin", wait_all=True)
    for tt in range(9):
        rows = 128 if tt < 8 else 4
        P.dma("sp", lambda e, tt=tt, rows=rows: e.dma_start(out=ypre[0:rows, tt, :], in_=Xres[128 * tt:128 * tt + rows, :]),
              w=[("ypre", tt)], chan="xres", wait_all=True)
    chk("c3")
    for g in range(4):
        s = g % 2
        src = Wout[:, 512 * g:512 * g + 512].rearrange("(cc p) n -> p cc n", p=128)
        P.dma("pool", lambda e, s=s, src=src: e.dma_start(out=wob[s][:], in_=src), w=[f"wob{s}"], chan=f"wob{s}")
        for tt in range(9):
            rows = 128 if tt < 8 else 4
            t0 = 128 * tt
            pk = tt % 2
            for cc in range(16):
                P.op("pe", lambda e, pk=pk, cc=cc, t0=t0, rows=rows, s=s: e.matmul(
                    PS[pk][0:rows, 0:512], lhsT=mxin[:, cc, t0:t0 + rows], rhs=wob[s][:, cc, :], start=(cc == 0), stop=(cc == 15)),
                    r=["mxin", f"wob{s}"], w=[psk(pk)])
            P.op("dve", lambda e, pk=pk, tt=tt, rows=rows, g=g: e.tensor_tensor(
                out=ypre[0:rows, tt, 512 * g:512 * g + 512], in0=ypre[0:rows, tt, 512 * g:512 * g + 512], in1=PS[pk][0:rows, 0:512], op=ALU.add),
                r=[psk(pk), ("ypre", tt)], w=[("ypre", tt)])
    for tt in range(9):
        rows = 128 if tt < 8 else 4
        P.op("act", lambda e, tt=tt, rows=rows: e.activation(out=junk2[0:rows, :], in_=ypre[0:rows, tt, :], func=AF.Square,
                                                             accum_out=st2[0:rows, tt, 0:1]), r=[("ypre", tt)], w=["junk2", ("st2", tt)])
        P.op("dve", lambda e, tt=tt, rows=rows: e.tensor_scalar(out=st2[0:rows, tt, 1:2], in0=st2[0:rows, tt, 0:1], scalar1=1.0 / D, scalar2=EPS,
                                                                op0=ALU.mult, op1=ALU.add), r=[("st2", tt)], w=[("st2", tt)])
        P.op("act", lambda e, tt=tt, rows=rows: e.activation(out=st2[0:rows, tt, 2:3], in_=st2[0:rows, tt, 1:2], func=AF.Sqrt),
             r=[("st2", tt)], w=[("st2", tt)])
        P.op("dve", lambda e, tt=tt, rows=rows: e.reciprocal(out=st2[0:rows, tt, 3:4], in_=st2[0:rows, tt, 2:3]), r=[("st2", tt)], w=[("st2", tt)])
        P.op("dve", lambda e, tt=tt, rows=rows: e.scalar_tensor_tensor(out=ypre[0:rows, tt, :], in0=ypre[0:rows, tt, :], scalar=st2[0:rows, tt, 3:4],
                                                                       in1=fng_bc[0:rows, :], op0=ALU.mult, op1=ALU.mult),
             r=[("ypre", tt), ("st2", tt), "fng_bc"], w=[("ypre", tt)])
        if tt < 8:
            P.dma("sp", lambda e, tt=tt: e.dma_start(out=yp_d[128 * tt:128 * tt + 128, :], in_=ypre[:, tt, :]), r=[("ypre", tt)], w=["yp_out"],
                  chan="yout", wait_all=True)
        else:
            P.dma("sp", lambda e, tt=tt: e.dma_start(out=ys_d[:, :], in_=ypre[0:4, tt, :]), r=[("ypre", tt)], w=["yp_out"],
                  chan="yout", wait_all=True)
    P.flush(barrier=True)
    es_c.close()
    es_p.close()


def _rel_bucket(n):
    n = np.asarray(n, dtype=np.int64)
    nl = np.maximum(n, 16).astype(np.float32)
    large = 16 + (np.log(nl / np.float32(16)) / np.float32(math.log(128 / 16)) * np.float32(16)).astype(np.int32)
    large = np.minimum(large, 31)
    return np.where(n < 16, n, large)


def _constants():
    c = {}
    sq = np.zeros((128, 3, 128), np.float32)
    sq[:, 0, :] = np.eye(128)
    sq[:, 1, :] = np.eye(128)[::-1]
    c["c_sq"] = sq
    c["c_j16"] = np.eye(16, dtype=np.float32)[::-1].copy()
    tri = (np.arange(64)[None, :] >= np.arange(64)[:, None]).astype(np.float32)
    mA = np.zeros((128, 2, 64), np.float32)
    mA[0:64, 0, :] = -tri
    mA[64:128, 1, :] = -tri
    c["c_maskA"] = mA
    mAm = np.zeros((NSPEC, 16), np.float32)
    mAm[0:16, :] = -(np.arange(16)[None, :] >= np.arange(16)[:, None]).astype(np.float32)
    c["c_maskAm"] = mAm
    mc = np.zeros((128, 4), np.float32)
    mc[0:64, 0] = -1.0
    mc[64:128, 1] = -1.0
    mc[0:16, 2] = -1.0
    mc[:, 3] = np.arange(128)
    c["c_mcol"] = mc
    rm = np.ones((1, NT), np.float32)
    rm[0, 0:NR:64] = 0.0
    rm[0, C_META] = 0.0
    rm[0, C_DEC:] = 0.0
    c["c_resetm"] = rm
    oh2 = np.zeros((33, 384), np.float32)
    for j in range(384):
        dist = j - 127
        if dist < 0:
            oh2[32, j] = NEG
        else:
            oh2[int(_rel_bucket(dist)), j] += 1.0
            oh2[31, j] -= 1.0
    c["c_oh2"] = oh2
    ohd = np.zeros((33, 3, 128), np.float32)
    ohd[31, 0, :] = 1.0
    for tok in range(128):
        ohd[int(_rel_bucket(128 - tok)), 1, tok] = 1.0
    ohd[0, 2, 0] = 1.0
    ohd[32, 2, 1:] = NEG
    c["c_ohd"] = ohd
    selv = np.zeros((NSPEC, NDEC, 128), np.float32)
    for s in range(NDEC):
        selv[NMETA + s, s, :] = 1.0
    c["c_selv"] = selv
    p01 = np.zeros((8, 8), np.float32)
    for h in range(4):
        p01[2 * h, h] = 1.0
        p01[2 * h + 1, 4 + h] = 1.0
    c["c_p01"] = p01
    return c


_CACHE = {}


def kernel(x_prompt, x_sample, cache_k, cache_v, state_hgrn, page_table, meta_tokens, rel_bias,
           hgrn_lb, norm_g, w_in, hgrn_norm_g, diff_norm_g, diff_lambda, w_out, final_norm_g):
    f32 = np.float32
    x_prompt = np.asarray(x_prompt, f32)
    x_sample = np.asarray(x_sample, f32)
    cache_k = np.asarray(cache_k, f32)
    cache_v = np.asarray(cache_v, f32)
    state_hgrn = np.asarray(state_hgrn, f32)
    page_table = np.asarray(page_table, np.int32)
    meta_tokens = np.asarray(meta_tokens, f32)
    rel_bias = np.asarray(rel_bias, f32)
    hgrn_lb = np.asarray(hgrn_lb, f32)
    norm_g = np.asarray(norm_g, f32)
    w_in = np.asarray(w_in, f32)
    w_out = np.asarray(w_out, f32)
    hgrn_norm_g = np.asarray(hgrn_norm_g, f32)
    diff_norm_g = np.asarray(diff_norm_g, f32)
    diff_lambda = np.asarray(diff_lambda, f32)
    final_norm_g = np.asarray(final_norm_g, f32)

    if "nc" not in _CACHE:
        _CACHE["nc"] = build_program()
    nc = _CACHE["nc"]
    consts = _constants()
    perm = np.concatenate([np.arange(0, 512), np.arange(1024, 1536), np.arange(512, 1024), np.arange(1536, 2048)])
    wout_p = np.ascontiguousarray(w_out[0][perm, :])
    ck_half = [np.ascontiguousarray(cache_k[0][:, :, 4 * j:4 * j + 4, :]).reshape(NPHYS * 128, 512) for j in range(2)]
    cv_half = [np.ascontiguousarray(cache_v[0][:, :, 4 * j:4 * j + 4, :]).reshape(NPHYS * 128, 512) for j in range(2)]
    win3 = w_in[0].reshape(D, 8, 8, 128)
    win_half = [np.ascontiguousarray(win3[:, :, 4 * j:4 * j + 4, :]).reshape(D, 4096) for j in range(2)]
    in_maps = []
    for c in range(8):
        b, j = c // 2, c % 2
        m = dict(consts)
        m["X"] = np.ascontiguousarray(np.concatenate([x_prompt[b], meta_tokens, x_sample[8 * b:8 * b + 8, 0, :]], 0))
        m["Win"] = win_half[j]
        m["Wout"] = wout_p
        m["Xres"] = np.ascontiguousarray(np.concatenate([x_prompt[b, 1024 * j:1024 * j + 1024],
                                                        x_sample[8 * b + 4 * j:8 * b + 4 * j + 4, 0, :]], 0))
        m["normg"] = norm_g[0].reshape(1, D).copy()
        lb2 = hgrn_lb[:, 512 * j:512 * j + 512].reshape(2, 4, 128)
        m["lbraw"] = np.ascontiguousarray(lb2.transpose(2, 0, 1)).reshape(128, 8)
        m["hgng"] = hgrn_norm_g[0].reshape(128, 1).copy()
        m["dfng"] = diff_norm_g[0].reshape(128, 1).copy()
        m["fng"] = final_norm_g.reshape(1, D).copy()
        m["dlam"] = diff_lambda[0].reshape(1, 256).copy()
        m["rbx"] = np.ascontiguousarray(np.concatenate([rel_bias[:, 4 * j:4 * j + 4], np.ones((1, 4), f32)], 0))
        m["pt"] = np.ascontiguousarray(page_table[8 * b:8 * b + 8].reshape(1, 512))
        m["ck"] = ck_half[j]
        m["cv"] = cv_half[j]
        m["sh"] = np.ascontiguousarray(state_hgrn[0, 8 * b:8 * b + 8, 4 * j:4 * j + 4])
        m["c_idxm"] = ((np.arange(2)[None, :] * 128 + np.arange(128)[:, None]) * 2 + j).astype(np.int32)
        in_maps.append(m)
    res = run_bass_kernel_spmd(nc, in_maps, core_ids=list(range(8)))
    R = res.results
    y_prompt = np.zeros((4, NR, D), f32)
    y_sample = np.zeros((32, 1, D), f32)
    k_prompt = np.zeros((1, 4, NR + NMETA, 8, 128), f32)
    v_prompt = np.zeros((1, 4, NR + NMETA, 8, 128), f32)
    s_prompt = np.zeros((1, 4, 8, 128, 128), f32)
    k_sample = np.zeros((1, 32, 1, 8, 128), f32)
    v_sample = np.zeros((1, 32, 1, 8, 128), f32)
    s_sample = np.zeros((1, 32, 8, 128, 128), f32)
    for c in range(8):
        b, j = c // 2, c % 2
        r = R[c]
        y_prompt[b, 1024 * j:1024 * j + 1024] = r["yp"]
        y_sample[8 * b + 4 * j:8 * b + 4 * j + 4, 0] = r["ys"]
        k_prompt[0, b, :, 4 * j:4 * j + 4, :] = r["kp"].reshape(NR + NMETA, 4, 128)
        v_prompt[0, b, :, 4 * j:4 * j + 4, :] = r["vp"].reshape(NR + NMETA, 4, 128)
        s_prompt[0, b, 4 * j:4 * j + 4] = r["sp"]
        k_sample[0, 8 * b:8 * b + 8, 0, 4 * j:4 * j + 4, :] = r["ks"].reshape(8, 4, 128)
        v_sample[0, 8 * b:8 * b + 8, 0, 4 * j:4 * j + 4, :] = r["vs"].reshape(8, 4, 128)
        s_sample[0, 8 * b:8 * b + 8, 4 * j:4 * j + 4] = r["ss"]
    return (y_prompt, y_sample, k_prompt, v_prompt, s_prompt, k_sample, v_sample, s_sample)
```

```python
import math
from contextlib import ExitStack
import numpy as np
import concourse.bass as bass
import concourse.mybir as mybir
from concourse.bass_utils import run_bass_kernel_spmd

F32 = mybir.dt.float32
BF16 = mybir.dt.bfloat16
I32 = mybir.dt.int32
AF = mybir.ActivationFunctionType
ALU = mybir.AluOpType
AX = mybir.AxisListType

D = 2048
KC = 16
NR = 2048
NMETA = 16
NDEC = 8
NT = NR + NMETA + NDEC
C_META = NR
C_DEC = NR + NMETA
NSPEC = NMETA + NDEC
RANGES = [(0, 512), (512, 512), (1024, 512), (1536, 512), (2048, NSPEC)]
NPAGE = 64
NPHYS = 2560
EPS = 1e-6
NEG = -30000.0
LAM_INIT = 0.8 - 0.6 * math.exp(-0.3 * 0)
ENGS = ["pe", "act", "dve", "pool", "sp"]


class Prog:
    def __init__(self, nc):
        self.nc = nc
        self.h = {"pe": nc.tensor, "act": nc.scalar, "dve": nc.vector, "pool": nc.gpsimd, "sp": nc.sync}
        self.sem = {e: nc.alloc_semaphore(name=f"eng_{e}") for e in ENGS}
        self.pending = {e: [] for e in ENGS}
        self.base = {e: 0 for e in ENGS}
        self.sigcount = {e: 0 for e in ENGS}
        self.sigpos = {e: {} for e in ENGS}
        self.res = {}
        self.chan = {}
        self.waited = {e: {} for e in ENGS}

    def _deps(self, eng, r, w):
        def same(t):
            return False
        deps = set()
        for k in r:
            st = self.res.get(k)
            if st and st[0] is not None:
                deps.add(st[0])
        for k in w:
            st = self.res.get(k)
            if st:
                if st[0] is not None and not same(st[0]):
                    deps.add(st[0])
                for t in st[1].values():
                    if not same(t):
                        deps.add(t)
        return deps

    def _update(self, tok, rkey, r, w):
        for k in r:
            st = self.res.setdefault(k, [None, {}])
            st[1][rkey] = tok
        for k in w:
            self.res[k] = [tok, {}]

    def op(self, eng, fn, r=(), w=()):
        w = list(w) + [k for k in r if isinstance(k, tuple) and k[0] == "ps" and k not in w]
        deps = self._deps(eng, r, w)
        gidx = self.base[eng] + len(self.pending[eng])
        tok = ("e", eng, gidx)
        if eng == "pe":
            deps = {d for d in deps if not (d[0] == "e" and d[1] == "pe")}
        self.pending[eng].append(dict(fn=fn, deps=deps, sig=False, chan=None))
        self._update(tok, ("e", eng), r, w)

    def dma(self, eng, fn, r=(), w=(), chan=None, wait_all=False):
        deps = self._deps(eng, r, w)
        if chan not in self.chan:
            self.chan[chan] = [self.nc.alloc_semaphore(name=f"ch_{chan}"), 0]
        c = self.chan[chan]
        c[1] += 1
        tok = ("dall", chan) if wait_all else ("d", chan, 16 * c[1])
        deps = {d for d in deps if not (d[0] == "dall" and d[1] == chan)}
        self.pending[eng].append(dict(fn=fn, deps=deps, sig=False, chan=chan))
        self._update(tok, ("d", chan), r, w)

    def flush(self, barrier=False):
        if barrier:
            for e in ENGS:
                for o in reversed(self.pending[e]):
                    if o["chan"] is None:
                        o["sig"] = True
                        break
        def mark(tok):
            if tok[0] == "e":
                loc = tok[2] - self.base[tok[1]]
                if loc >= 0:
                    self.pending[tok[1]][loc]["sig"] = True
        for e in ENGS:
            for o in self.pending[e]:
                for d in o["deps"]:
                    mark(d)
        for st in self.res.values():
            if st[0] is not None:
                mark(st[0])
            for t in st[1].values():
                mark(t)
        for e in ENGS:
            cnt = self.sigcount[e]
            for i, o in enumerate(self.pending[e]):
                if o["sig"] and o["chan"] is None:
                    cnt += 1
                    self.sigpos[e][self.base[e] + i] = cnt
            self.sigcount[e] = cnt

        def resolve(tok):
            if tok[0] == "e":
                return self.sem[tok[1]], self.sigpos[tok[1]][tok[2]]
            if tok[0] == "d":
                return self.chan[tok[1]][0], tok[2]
            return self.chan[tok[1]][0], 16 * self.chan[tok[1]][1]

        finals = {e: self.sigcount[e] for e in ENGS}
        if barrier:
            self.sigcount["sp"] += 1
        sp_final = self.sigcount["sp"]

        def emit(e, eh):
            wd = self.waited[e]
            for o in self.pending[e]:
                need = {}
                for d in o["deps"]:
                    s, v = resolve(d)
                    key = id(s)
                    if wd.get(key, (None, 0))[1] >= v:
                        continue
                    if key not in need or need[key][1] < v:
                        need[key] = (s, v)
                for key, (s, v) in need.items():
                    eh.wait_ge(s, v)
                    wd[key] = (s, v)
                ins = o["fn"](eh)
                if o["chan"] is not None:
                    ins.then_inc(self.chan[o["chan"]][0], 16)
                elif o["sig"]:
                    ins.then_inc(self.sem[e], 1)
            if barrier:
                if e == "sp":
                    for ch, (sm, cnt) in self.chan.items():
                        if cnt > 0:
                            eh.wait_ge(sm, 16 * cnt)
                    for e2 in ENGS:
                        if e2 != "sp" and finals[e2] > 0:
                            eh.wait_ge(self.sem[e2], finals[e2])
                    eh.nop().then_inc(self.sem["sp"], 1)
                else:
                    eh.wait_ge(self.sem["sp"], sp_final)

        with self.nc.Block() as block:
            @block.tensor
            def _(eh):
                emit("pe", eh)

            @block.scalar
            def _(eh):
                emit("act", eh)

            @block.vector
            def _(eh):
                emit("dve", eh)

            @block.gpsimd
            def _(eh):
                emit("pool", eh)

            @block.sync
            def _(eh):
                emit("sp", eh)
        for e in ENGS:
            self.base[e] += len(self.pending[e])
            self.pending[e] = []

    def final_wait(self, keys):
        toks = set()
        for k in keys:
            st = self.res.get(k)
            if st and st[0] is not None:
                toks.add(st[0])
        self.pending["sp"].append(dict(fn=lambda eh: eh.nop(), deps=toks, sig=False, chan=None))


class _Stop(Exception):
    pass


def build_program(nphys=NPHYS, stop=None):
    nc = bass.Bass("TRN2", target_bir_lowering=False)
    P = Prog(nc)
    stacks = []
    try:
        _build_body(nc, P, stacks, nphys, stop)
    except _Stop:
        P.flush(barrier=True)
    for es in reversed(stacks):
        es.close()
    return nc


def _build_body(nc, P, stacks, nphys, stop):
    def ExitStackT():
        es = ExitStack()
        stacks.append(es)
        return es

    def chk(tag):
        if stop == tag:
            raise _Stop()

    def din(name, shape, dt=F32):
        return nc.dram_tensor(name, list(shape), dt, kind="ExternalInput")

    def dout(name, shape, dt=F32):
        return nc.dram_tensor(name, list(shape), dt, kind="ExternalOutput")

    X = din("X", [NT, D]).ap()
    Win = din("Win", [D, 4096]).ap()
    Wout = din("Wout", [D, D]).ap()
    Xres = din("Xres", [1028, D]).ap()
    normg_d = din("normg", [1, D]).ap()
    lbraw_d = din("lbraw", [128, 8]).ap()
    hgng_d = din("hgng", [128, 1]).ap()
    dfng_d = din("dfng", [128, 1]).ap()
    fng_d = din("fng", [1, D]).ap()
    dlam_d = din("dlam", [1, 256]).ap()
    rbx_d = din("rbx", [33, 4]).ap()
    pt_d = din("pt", [1, 512], I32).ap()
    ck_d = din("ck", [nphys * 128, 512]).ap()
    cv_d = din("cv", [nphys * 128, 512]).ap()
    sh_d = din("sh", [NDEC, 4, 128, 128]).ap()
    c_sq_d = din("c_sq", [128, 3, 128]).ap()
    c_j16_d = din("c_j16", [16, 16]).ap()
    c_maskA_d = din("c_maskA", [128, 2, 64]).ap()
    c_maskAm_d = din("c_maskAm", [NSPEC, 16]).ap()
    c_mcol_d = din("c_mcol", [128, 4]).ap()
    c_resetm_d = din("c_resetm", [1, NT]).ap()
    c_oh2_d = din("c_oh2", [33, 384]).ap()
    c_ohd_d = din("c_ohd", [33, 3, 128]).ap()
    c_selv_d = din("c_selv", [NSPEC, NDEC, 128]).ap()
    c_p01_d = din("c_p01", [8, 8]).ap()
    c_idxm_d = din("c_idxm", [128, 2], I32).ap()

    yp_d = dout("yp", [1024, D]).ap()
    ys_d = dout("ys", [4, D]).ap()
    kp_d = dout("kp", [NR + NMETA, 512]).ap()
    vp_d = dout("vp", [NR + NMETA, 512]).ap()
    sp_d = dout("sp", [4, 128, 128]).ap()
    ks_d = dout("ks", [NDEC, 512]).ap()
    vs_d = dout("vs", [NDEC, 512]).ap()
    ss_d = dout("ss", [NDEC, 4, 128, 128]).ap()

    u_scr = nc.dram_tensor("u_scr", [4, 384], F32)
    snd_d = nc.dram_tensor("snd", [1024, 2056], BF16).ap()
    rcv_ds = [nc.dram_tensor(f"rcv{i}", [256, 2056], BF16).ap() for i in range(8)]

    def mk(es):
        return lambda name, shape, dt: es.enter_context(nc.sbuf_tensor("s_" + name, list(shape), dt))

    es_p = ExitStackT()
    sb = mk(es_p)
    es_0 = ExitStackT()
    sb0 = mk(es_0)
    identb = sb("identb", [128, 128], BF16)
    Jb = sb("Jb", [128, 128], BF16)
    J16b = sb("J16b", [16, 16], BF16)
    onesb = sb("onesb", [128, 128], BF16)
    onesf = sb("onesf", [128, 128], F32)
    maskA = sb("maskA", [128, 2, 64], F32)
    maskAm = sb("maskAm", [NSPEC, 16], F32)
    mcol = sb("mcol", [128, 4], F32)
    resetm = sb("resetm", [128, NT], BF16)
    lbt = sb("lbt", [128, 4], F32)
    omlt = sb("omlt", [128, 4], F32)
    hgng = sb("hgng", [128, 1], F32)
    dfng = sb("dfng", [128, 1], F32)
    cn = sb("cn", [128, 1], F32)
    lamt = sb("lamt", [128, 2], F32)
    rbx = sb("rbx", [33, 4], F32)
    selv = sb("selv", [NSPEC, NDEC, 128], F32)
    comb = sb("comb", [8, 4], F32)
    Hm = sb("Hm", [128, 4, 2, 128], BF16)
    b31t = sb("b31t", [128, 4], F32)
    bsm = sb("bsm", [128, 3, 4], F32)
    idxg = sb("idxg", [128, 512], I32)
    idxm = sb("idxm", [128, 2], I32)
    kspec = sb("kspec", [NSPEC, 4, 128], F32)
    vspec = sb("vspec", [NSPEC, 4, 128], F32)
    qspec = sb("qspec", [NSPEC, 4, 128], F32)
    gdec = sb("gdec", [128, 4, NDEC], F32)
    odec = sb("odec", [128, 4, NDEC], F32)

    csq = sb0("csq", [128, 3, 128], F32)
    cj16 = sb0("cj16", [16, 16], F32)
    lbraw = sb0("lbraw", [128, 8], F32)
    dlam = sb0("dlam", [1, 256], F32)
    lsm = sb0("lsm", [1, 136], F32)
    oh2 = sb0("oh2", [33, 384], F32)
    ohd = sb0("ohd", [33, 3, 128], F32)
    p01 = sb0("p01", [8, 8], F32)
    ut = sb0("ut", [4, 384], F32)
    ptb = sb0("ptb", [128, 512], I32)
    ptf = sb0("ptf", [128, 512], F32)

    PS = [es_p.enter_context(nc.psum_tensor(f"ps{k}", [128, 512], F32)) for k in range(8)]

    def psk(k):
        return ("ps", k)

    def cload(dst, src, name, eng="sp"):
        P.dma(eng, lambda e, d=dst, s=src: e.dma_start(out=d, in_=s), w=[name], chan="const" if eng == "sp" else "constp", wait_all=True)

    cload(csq[:], c_sq_d, "csq")
    cload(cj16[:], c_j16_d, "cj16")
    cload(maskA[:], c_maskA_d, "maskA")
    cload(maskAm[:], c_maskAm_d, "maskAm")
    cload(mcol[:], c_mcol_d, "mcol")
    cload(resetm[:], c_resetm_d.partition_broadcast(128), "resetm", eng="pool")
    cload(lbraw[:], lbraw_d, "lbraw")
    cload(hgng[:], hgng_d, "hgng")
    cload(dfng[:], dfng_d, "dfng")
    cload(dlam[:], dlam_d, "dlam")
    cload(rbx[:], rbx_d, "rbx")
    cload(oh2[:], c_oh2_d, "oh2")
    cload(ohd[:], c_ohd_d, "ohd")
    cload(selv[:], c_selv_d, "selv")
    cload(p01[:], c_p01_d, "p01")
    cload(b31t[:], rbx_d[31:32, :].partition_broadcast(128), "b31t")
    cload(ptb[:], pt_d.partition_broadcast(128), "ptb")
    cload(idxm[:], c_idxm_d, "idxm")

    P.op("dve", lambda e: e.tensor_copy(out=identb[:], in_=csq[:, 0, :]), r=["csq"], w=["identb"])
    P.op("dve", lambda e: e.tensor_copy(out=Jb[:], in_=csq[:, 1, :]), r=["csq"], w=["Jb"])
    P.op("dve", lambda e: e.tensor_copy(out=J16b[:], in_=cj16[:]), r=["cj16"], w=["J16b"])
    P.op("pool", lambda e: e.memset(onesb[:], 1.0), w=["onesb"])
    P.op("pool", lambda e: e.memset(onesf[:], 1.0), w=["onesf"])
    P.op("dve", lambda e: e.tensor_tensor(out=lbt[:], in0=lbraw[:, 0:4], in1=lbraw[:, 4:8], op=ALU.subtract),
         r=["lbraw"], w=["lbt"])
    P.op("act", lambda e: e.activation(out=lbt[:], in_=lbt[:], func=AF.Sigmoid), r=["lbt"], w=["lbt"])
    P.op("dve", lambda e: e.tensor_scalar(out=omlt[:], in0=lbt[:], scalar1=-1.0, scalar2=1.0, op0=ALU.mult, op1=ALU.add),
         r=["lbt"], w=["omlt"])
    P.op("dve", lambda e: e.tensor_scalar(out=cn[:], in0=dfng[:], scalar1=float(1.0 - LAM_INIT), scalar2=None, op0=ALU.mult),
         r=["dfng"], w=["cn"])
    dl4 = dlam[:].rearrange("o (a b d) -> o a b d", a=2, b=2)
    P.op("dve", lambda e: e.tensor_tensor(out=lsm[:, 0:128].rearrange("o (a d) -> o a d", a=2), in0=dl4[:, :, 0, :],
                                          in1=dl4[:, :, 1, :], op=ALU.mult), r=["dlam"], w=["lsm"])
    P.op("dve", lambda e: e.tensor_reduce(out=lsm[:, 128:130], in_=lsm[:, 0:128].rearrange("o (a d) -> o a d", a=2),
                                          axis=AX.X, op=ALU.add), r=["lsm"], w=["lsm"])
    P.op("act", lambda e: e.activation(out=lsm[:, 130:132], in_=lsm[:, 128:130], func=AF.Exp), r=["lsm"], w=["lsm"])
    P.op("dve", lambda e: e.tensor_tensor(out=lsm[:, 132:133], in0=lsm[:, 130:131], in1=lsm[:, 131:132], op=ALU.subtract),
         r=["lsm"], w=["lsm"])
    P.op("dve", lambda e: e.tensor_scalar(out=lsm[:, 133:134], in0=lsm[:, 132:133], scalar1=float(LAM_INIT), scalar2=None,
                                          op0=ALU.add), r=["lsm"], w=["lsm"])
    P.op("pe", lambda e: e.matmul(PS[0][:, 0:1], lhsT=onesf[0:1, :], rhs=lsm[:, 133:134], start=True, stop=True),
         r=["onesf", "lsm"], w=[psk(0)])
    P.op("dve", lambda e: e.tensor_copy(out=lamt[:, 0:1], in_=PS[0][:, 0:1]), r=[psk(0)], w=["lamt"])
    P.op("dve", lambda e: e.tensor_scalar(out=lamt[:, 1:2], in0=lamt[:, 0:1], scalar1=-1.0, scalar2=None, op0=ALU.mult),
         r=["lamt"], w=["lamt"])
    P.op("dve", lambda e: e.scalar_tensor_tensor(out=comb[:], in0=p01[:, 4:8], scalar=lamt[0:8, 1:2], in1=p01[:, 0:4],
                                                 op0=ALU.mult, op1=ALU.add), r=["p01", "lamt"], w=["comb"])
    P.op("pe", lambda e: e.matmul(PS[1][0:4, 0:384], lhsT=rbx[:], rhs=oh2[:], start=True, stop=True),
         r=["rbx", "oh2"], w=[psk(1)])
    P.op("dve", lambda e: e.tensor_copy(out=ut[:], in_=PS[1][0:4, 0:384]), r=[psk(1)], w=["ut"])
    P.dma("sp", lambda e: e.dma_start(out=u_scr.ap(), in_=ut[:]), r=["ut"], w=["u_scr"], chan="uscr")
    for h in range(4):
        for oi in range(2):
            P.dma("pool", lambda e, h=h, oi=oi: e.dma_start(
                out=Hm[:, h, oi, :], in_=bass.AP(u_scr, h * 384 + 128 * oi, [[1, 128], [1, 128]])),
                r=["u_scr"], w=["Hm"], chan="hm", wait_all=True)
    for kind in range(3):
        P.op("pe", lambda e, kind=kind: e.matmul(PS[2][:, 4 * kind:4 * kind + 4], lhsT=ohd[:, kind, :], rhs=rbx[:],
                                                 start=True, stop=True), r=["ohd", "rbx"], w=[psk(2)])
    P.op("dve", lambda e: e.tensor_copy(out=bsm[:].rearrange("p a b -> p (a b)"), in_=PS[2][:, 0:12]), r=[psk(2)], w=["bsm"])
    P.op("dve", lambda e: e.tensor_copy(out=ptf[:], in_=ptb[:]), r=["ptb"], w=["ptf"])
    P.op("dve", lambda e: e.tensor_scalar(out=ptf[:], in0=ptf[:], scalar1=128.0, scalar2=mcol[:, 3:4], op0=ALU.mult, op1=ALU.add),
         r=["ptf", "mcol"], w=["ptf"])
    P.op("dve", lambda e: e.tensor_copy(out=idxg[:], in_=ptf[:]), r=["ptf"], w=["idxg"])
    P.flush(barrier=True)
    es_0.close()
    chk("p0")
    es_m = ExitStackT()
    mixedT = mk(es_m)("mixedT", [128, 8, 2056], BF16)
    es_b = ExitStackT()
    sbb = mk(es_b)
    NKB = 6
    kvbuf = [sbb(f"kvbuf{s}", [128, 512], F32) for s in range(NKB)]
    k64 = sbb("k64", [128, 512], F32)
    v64 = sbb("v64", [128, 512], F32)
    qbc = sbb("qbc", [128, 512], F32)
    prod = sbb("prod", [128, 512], F32)
    prodb = [prod, sbb("prod1", [128, 512], F32)]
    scall = sbb("scall", [128, 65, 8], F32)
    ball = sbb("ball", [128, 65, 8], F32)
    eall = sbb("eall", [128, 65, 8], F32)
    esum = sbb("esum", [128, 8], F32)
    Rn = sbb("Rn", [8, 512], F32)
    rzd = sbb("rzd", [8, 1], F32)
    sqb = sbb("sqb", [128, 512], F32)
    rsb = sbb("rsb", [128, 512], F32)
    gtmp = [sbb(f"gtmp{s}", [128, 512], F32) for s in range(1)]

    def decode_gen():
        P.op("pool", lambda e: e.memset(k64[:], 0.0), w=["k64"])
        P.op("pool", lambda e: e.memset(v64[:], 0.0), w=["v64"])
        ball4 = ball[:].rearrange("p g (h c) -> p g h c", c=2)
        P.op("dve", lambda e: e.tensor_copy(out=ball4[:, 0:63, :, :], in_=bsm[:, 0:1, :].unsqueeze(3).to_broadcast([128, 63, 4, 2])),
             r=["bsm"], w=["ball"])
        P.op("dve", lambda e: e.tensor_copy(out=ball4[:, 63:65, :, :], in_=bsm[:, 1:3, :].unsqueeze(3).to_broadcast([128, 2, 4, 2])),
             r=["bsm"], w=["ball"])
        gctr = 0
        for s in range(NDEC):
            P.op("pe", lambda e, s=s: e.matmul(PS[1][:, 0:512], lhsT=selv[:, s, :], rhs=qspec[:].rearrange("p h d -> p (h d)"),
                                               start=True, stop=True), r=["selv", ("spec", id(qspec))], w=[psk(1)])
            P.op("act", lambda e: e.copy(out=qbc[:], in_=PS[1][:, 0:512]), r=[psk(1)], w=["qbc"])
            P.dma("sp", lambda e, s=s: e.dma_start(out=k64[0:1, :], in_=kspec[NMETA + s:NMETA + s + 1, :, :].rearrange("p h d -> p (h d)")),
                  r=[("spec", id(kspec))], w=["k64"], chan="k64")
            P.dma("sp", lambda e, s=s: e.dma_start(out=v64[0:1, :], in_=vspec[NMETA + s:NMETA + s + 1, :, :].rearrange("p h d -> p (h d)")),
                  r=[("spec", id(vspec))], w=["v64"], chan="v64")
            for pg in range(65):
                if pg < 64:
                    sl = gctr % NKB
                    gctr += 1
                    col = s * 64 + pg
                    P.dma("pool", lambda e, sl=sl, col=col: e.indirect_dma_start(
                        out=kvbuf[sl][:, :], out_offset=None, in_=ck_d,
                        in_offset=bass.IndirectOffsetOnAxis(ap=idxg[:, col:col + 1], axis=0)), r=["idxg"], w=[f"kvbuf{sl}"], chan=f"kvbuf{sl}")
                    src, skey = kvbuf[sl], f"kvbuf{sl}"
                else:
                    src, skey = k64, "k64"
                pb = pg % 2
                P.op("dve", lambda e, src=src, pb=pb: e.tensor_tensor(out=prodb[pb][:], in0=src[:], in1=qbc[:], op=ALU.mult), r=[skey, "qbc"], w=[f"prodb{pb}"])
                if pg > 0:
                    P.op("dve", lambda e, pg=pg, pb=pb: e.tensor_reduce(out=scall[:, pg - 1, :], in_=prodb[1 - pb][:].rearrange("p (g d) -> p g d", d=64),
                                                                         axis=AX.X, op=ALU.add), r=[f"prodb{1 - pb}"], w=["scall"])
                yield
            P.op("dve", lambda e: e.tensor_reduce(out=scall[:, 64, :], in_=prodb[0][:].rearrange("p (g d) -> p g d", d=64), axis=AX.X, op=ALU.add),
                 r=["prodb0"], w=["scall"])
            P.op("dve", lambda e: e.tensor_tensor(out=scall[:], in0=scall[:], in1=ball[:], op=ALU.add), r=["scall", "ball"], w=["scall"])
            P.op("act", lambda e: e.activation(out=eall[:], in_=scall[:], func=AF.Exp), r=["scall"], w=["eall"])
            P.op("dve", lambda e: e.tensor_reduce(out=esum[:], in_=eall[:].rearrange("p g k -> p k g"), axis=AX.X, op=ALU.add),
                 r=["eall"], w=["esum"])
            for pg in range(65):
                if pg < 64:
                    sl = gctr % NKB
                    gctr += 1
                    col = s * 64 + pg
                    P.dma("pool", lambda e, sl=sl, col=col: e.indirect_dma_start(
                        out=kvbuf[sl][:, :], out_offset=None, in_=cv_d,
                        in_offset=bass.IndirectOffsetOnAxis(ap=idxg[:, col:col + 1], axis=0)), r=["idxg"], w=[f"kvbuf{sl}"], chan=f"kvbuf{sl}")
                    src, skey = kvbuf[sl], f"kvbuf{sl}"
                else:
                    src, skey = v64, "v64"
                P.op("pe", lambda e, pg=pg, src=src: e.matmul(PS[0][0:8, 0:512], lhsT=eall[:, pg, :], rhs=src[:], start=(pg == 0), stop=(pg == 64)),
                     r=["eall", skey], w=[psk(0)])
                yield
            P.op("pe", lambda e: e.matmul(PS[1][0:8, 0:1], lhsT=esum[:], rhs=onesf[:, 0:1], start=True, stop=True),
                 r=["esum", "onesf"], w=[psk(1)])
            P.op("dve", lambda e: e.reciprocal(out=rzd[:], in_=PS[1][0:8, 0:1]), r=[psk(1)], w=["rzd"])
            P.op("dve", lambda e: e.tensor_scalar(out=Rn[:], in0=PS[0][0:8, 0:512], scalar1=rzd[:, 0:1], scalar2=None, op0=ALU.mult),
                 r=[psk(0), "rzd"], w=["Rn"])
            for h in range(4):
                P.op("pe", lambda e, h=h: e.matmul(PS[1][:, h:h + 1], lhsT=Rn[:, 128 * h:128 * h + 128], rhs=comb[:, h:h + 1], start=True, stop=True),
                     r=["Rn", "comb"], w=[psk(1)])
            P.op("act", lambda e, s=s: e.copy(out=odec[:, :, s], in_=PS[1][:, 0:4]), r=[psk(1)], w=["odec"])
            yield

    dgen = [None]

    def pump(n):
        if dgen[0] is None:
            return
        for _ in range(n):
            try:
                next(dgen[0])
            except StopIteration:
                dgen[0] = None
                return

    es_a = ExitStackT()
    sb = mk(es_a)

    xnT = sb("xnT", [128, KC, NT], BF16)
    wblk = [sb(f"wblk{s}", [128, KC, 128], BF16) for s in range(3)]
    wctr = [0]
    es_a1 = ExitStackT()
    sb = mk(es_a1)
    normg_bc = sb("normg_bc", [128, D], F32)
    xin = [sb(f"xin{s}", [128, D], F32) for s in range(2)]
    xnb = [sb(f"xnb{s}", [128, D], BF16) for s in range(2)]
    junk = sb("junk", [128, D], BF16)
    stat = sb("stat", [128, 17, 4], F32)

    P.dma("sp", lambda e: e.dma_start(out=normg_bc[:], in_=normg_d.partition_broadcast(128)), w=["normg_bc"], chan="ngbc")

    for i in range(17):
        s = i % 2
        rows = 128 if i < 16 else NSPEC
        r0 = 128 * i
        P.dma("sp", lambda e, s=s, rows=rows, r0=r0: e.dma_start(out=xin[s][0:rows, :], in_=X[r0:r0 + rows, :]),
              w=[f"xin{s}"], chan=f"xin{s}")
        P.op("act", lambda e, s=s, rows=rows, i=i: e.activation(out=junk[0:rows, :], in_=xin[s][0:rows, :], func=AF.Square,
                                                                accum_out=stat[0:rows, i, 0:1]),
             r=[f"xin{s}"], w=["junk", ("stat", i)])
        P.op("dve", lambda e, rows=rows, i=i: e.tensor_scalar(out=stat[0:rows, i, 1:2], in0=stat[0:rows, i, 0:1],
                                                              scalar1=1.0 / D, scalar2=EPS, op0=ALU.mult, op1=ALU.add),
             r=[("stat", i)], w=[("stat", i)])
        P.op("act", lambda e, rows=rows, i=i: e.activation(out=stat[0:rows, i, 2:3], in_=stat[0:rows, i, 1:2], func=AF.Sqrt),
             r=[("stat", i)], w=[("stat", i)])
        P.op("dve", lambda e, rows=rows, i=i: e.reciprocal(out=stat[0:rows, i, 3:4], in_=stat[0:rows, i, 2:3]),
             r=[("stat", i)], w=[("stat", i)])
        P.op("dve", lambda e, s=s, rows=rows, i=i: e.scalar_tensor_tensor(
            out=xnb[s][0:rows, :], in0=xin[s][0:rows, :], scalar=stat[0:rows, i, 3:4], in1=normg_bc[0:rows, :],
            op0=ALU.mult, op1=ALU.mult), r=[f"xin{s}", ("stat", i), "normg_bc"], w=[f"xnb{s}"])
        for g4 in range(4):
            pk = g4 % 2
            pT = PS[pk][:].bitcast(BF16)
            for q in range(4):
                kc = 4 * g4 + q
                P.op("pe", lambda e, s=s, rows=rows, kc=kc, q=q, pT=pT: e.transpose(
                    out=pT[:, q * 128:q * 128 + rows], in_=xnb[s][0:rows, kc * 128:(kc + 1) * 128],
                    identity=identb[0:rows, 0:rows]), r=[f"xnb{s}", "identb"], w=[psk(pk)])
            eng = "act" if g4 % 2 == 0 else "dve"
            src = pT[:, 0:512].rearrange("p (q t) -> p q t", q=4)[:, :, 0:rows]
            dst = xnT[:, 4 * g4:4 * g4 + 4, r0:r0 + rows]
            if eng == "act":
                P.op("act", lambda e, src=src, dst=dst: e.copy(out=dst, in_=src), r=[psk(pk)], w=[("xnT", i)])
            else:
                P.op("dve", lambda e, src=src, dst=dst: e.tensor_copy(out=dst, in_=src), r=[psk(pk)], w=[("xnT", i)])

    P.flush(barrier=True)
    es_a1.close()
    chk("a1")
    es_a2 = ExitStackT()
    sb = mk(es_a2)

    def tiles_of(t0, n):
        return [("xnT", i) for i in range(17) if not (128 * i >= t0 + n or (128 * i + (128 if i < 16 else NSPEC)) <= t0)]

    wlist = []
    for hh in range(4):
        for g8 in (4, 5, 6):
            wlist.append(512 * g8 + 128 * hh)
    for hh in range(4):
        for g8 in (1, 0, 2, 3):
            wlist.append(512 * g8 + 128 * hh)
    for hh in range(4):
        for g8 in (5, 6, 4, 7):
            wlist.append(512 * g8 + 128 * hh)
    wiss = [0]

    def issue_next_w():
        k = wiss[0]
        if k >= len(wlist):
            return
        s = k % 3
        src = Win[:, wlist[k]:wlist[k] + 128].rearrange("(kc p) c -> p kc c", p=128)
        P.dma("pool", lambda e, s=s, src=src: e.dma_start(out=wblk[s][:], in_=src), w=[f"wblk{s}"], chan=f"wblk{s}")
        wiss[0] += 1

    def load_w(col0):
        k = wctr[0]
        assert wlist[k] == col0, (k, wlist[k], col0)
        while wiss[0] <= k + 2 and wiss[0] < len(wlist):
            issue_next_w()
        wctr[0] += 1
        return k % 3

    pctr = [0]

    def proj_F(ws, consume, ranges=RANGES):
        for (t0, n) in ranges:
            pk = 2 + (pctr[0] % 2)
            pctr[0] += 1
            for kc in range(KC):
                P.op("pe", lambda e, pk=pk, ws=ws, kc=kc, t0=t0, n=n: e.matmul(
                    PS[pk][:, 0:n], lhsT=wblk[ws][:, kc, :], rhs=xnT[:, kc, t0:t0 + n], start=(kc == 0), stop=(kc == KC - 1)),
                    r=[f"wblk{ws}"] + tiles_of(t0, n), w=[psk(pk)])
            consume(PS[pk], pk, t0, n)
            pump(2)

    def proj_T(ws, consume, tiles=range(17)):
        for i in tiles:
            rows = 128 if i < 16 else NSPEC
            pk = 2 + (pctr[0] % 2)
            pctr[0] += 1
            for kc in range(KC):
                P.op("pe", lambda e, pk=pk, ws=ws, kc=kc, i=i, rows=rows: e.matmul(
                    PS[pk][0:rows, 0:128], lhsT=xnT[:, kc, 128 * i:128 * i + rows], rhs=wblk[ws][:, kc, :],
                    start=(kc == 0), stop=(kc == KC - 1)), r=[f"wblk{ws}", ("xnT", i)], w=[psk(pk)])
            consume(PS[pk], pk, i, rows)
            pump(1)

    for h in range(4):
        for (g8, dstt, scl) in ((4, qspec, 0.125), (5, kspec, 1.0), (6, vspec, 1.0)):
            ws = load_w(512 * g8 + 128 * h)

            def cons_sp(ps, pk, i, rows, h=h, dstt=dstt, scl=scl):
                P.op("act", lambda e: e.mul(out=dstt[:, h, :], in_=ps[0:NSPEC, 0:128], mul=scl), r=[psk(pk)], w=[("spec", id(dstt))])
            proj_T(ws, cons_sp, tiles=[16])
    P.dma("sp", lambda e: e.dma_start(out=ks_d[:, :], in_=kspec[NMETA:NSPEC, :, :].rearrange("p h d -> p (h d)")),
          r=[("spec", id(kspec))], w=["ks_out"], chan="ksout")
    P.dma("sp", lambda e: e.dma_start(out=vs_d[:, :], in_=vspec[NMETA:NSPEC, :, :].rearrange("p h d -> p (h d)")),
          r=[("spec", id(vspec))], w=["vs_out"], chan="vsout")
    dgen[0] = decode_gen()

    fbuf = sb("fbuf", [128, NT], F32)
    L0 = sb("L0", [128, NT], F32)
    L1 = sb("L1", [128, NT], F32)
    nkeT = sb("nkeT", [128, NT], BF16)
    qeT = sb("qeT", [128, NT], BF16)
    vtok = sb("vtok", [128, 17, 128], BF16)
    vspf = sb("vspf", [NSPEC, 128], F32)
    qdec = sb("qdec", [128, NDEC], F32)
    kdec = sb("kdec", [128, NDEC], F32)
    S32 = sb("S32", [128, 128], F32)
    Sbf = sb("Sbf", [128, 128], BF16)
    Stmp = sb("Stmp", [128, 128], F32)
    ATm = [sb(f"ATm{s}", [128, 64], BF16) for s in range(2)]
    ktok = [sb(f"ktok{s}", [128, 128], BF16) for s in range(4)]
    S0t = [sb(f"S0t{s}", [128, 128], F32) for s in range(2)]
    Snt = [sb(f"Snt{s}", [128, 128], F32) for s in range(2)]

    def norm_gate_cols(obuf_ap, okey, n, gain_ap, gain_key, dst_ap, dst_key, pk):
        P.op("act", lambda e: e.activation(out=sqb[:, 0:n], in_=obuf_ap, func=AF.Square), r=[okey], w=["sqb"])
        P.op("pe", lambda e: e.matmul(PS[pk][:, 0:n], lhsT=onesf[:], rhs=sqb[:, 0:n], start=True, stop=True),
             r=["onesf", "sqb"], w=[psk(pk)])
        P.op("dve", lambda e: e.tensor_scalar(out=rsb[:, 0:n], in0=PS[pk][:, 0:n], scalar1=1.0 / 128, scalar2=EPS,
                                              op0=ALU.mult, op1=ALU.add), r=[psk(pk)], w=["rsb"])
        P.op("act", lambda e: e.activation(out=rsb[:, 0:n], in_=rsb[:, 0:n], func=AF.Sqrt), r=["rsb"], w=["rsb"])
        P.op("dve", lambda e: e.reciprocal(out=rsb[:, 0:n], in_=rsb[:, 0:n]), r=["rsb"], w=["rsb"])
        P.op("dve", lambda e: e.scalar_tensor_tensor(out=dst_ap, in0=obuf_ap, scalar=gain_ap, in1=rsb[:, 0:n],
                                                     op0=ALU.mult, op1=ALU.mult), r=[okey, "rsb", gain_key], w=[dst_key])

    for h in range(4):
        ws = load_w(512 * 1 + 128 * h)

        def cons_f(ps, pk, t0, n):
            P.op("act", lambda e: e.activation(out=fbuf[:, t0:t0 + n], in_=ps[:, 0:n], func=AF.Sigmoid), r=[psk(pk)], w=["fbuf"])
        proj_F(ws, cons_f)
        chk("h0")
        P.op("dve", lambda e, h=h: e.tensor_scalar(out=fbuf[:], in0=fbuf[:], scalar1=omlt[:, h:h + 1], scalar2=lbt[:, h:h + 1],
                                                   op0=ALU.mult, op1=ALU.add), r=["fbuf", "omlt", "lbt"], w=["fbuf"])
        P.op("act", lambda e: e.activation(out=L0[:], in_=fbuf[:], func=AF.Ln), r=["fbuf"], w=["L0"])
        chk("h0b")
        P.op("dve", lambda e: e.tensor_tensor_scan(out=L1[:], data0=resetm[:], data1=L0[:], initial=0.0, op0=ALU.mult, op1=ALU.add),
             r=["resetm", "L0"], w=["L1"])
        chk("h0c")
        P.op("act", lambda e: e.activation(out=L0[:], in_=L1[:], func=AF.Exp, scale=-1.0), r=["L1"], w=["L0"])
        P.op("dve", lambda e: e.scalar_tensor_tensor(out=nkeT[:], in0=fbuf[:], scalar=onesf[:, 0:1], in1=L0[:], op0=ALU.subtract, op1=ALU.mult),
             r=["fbuf", "L0", "onesf"], w=["nkeT"])
        P.op("dve", lambda e: e.tensor_scalar(out=kdec[:], in0=fbuf[:, C_DEC:C_DEC + NDEC], scalar1=-1.0, scalar2=1.0,
                                              op0=ALU.mult, op1=ALU.add), r=["fbuf"], w=["kdec"])
        P.op("act", lambda e: e.activation(out=fbuf[:], in_=L1[:], func=AF.Exp), r=["L1", "kdec"], w=["fbuf"])
        chk("h1")
        ws = load_w(512 * 0 + 128 * h)

        def cons_q(ps, pk, t0, n):
            if t0 == C_META:
                P.op("act", lambda e: e.copy(out=qdec[:], in_=ps[:, NMETA:NSPEC]), r=[psk(pk)], w=["qdec"])
            P.op("dve", lambda e: e.tensor_tensor(out=qeT[:, t0:t0 + n], in0=ps[:, 0:n], in1=fbuf[:, t0:t0 + n], op=ALU.mult),
                 r=[psk(pk), "fbuf"], w=["qeT"])
        proj_F(ws, cons_q)
        chk("h2")
        ws = load_w(512 * 2 + 128 * h)

        def cons_v(ps, pk, i, rows):
            P.op("act", lambda e: e.copy(out=vtok[0:rows, i, :], in_=ps[0:rows, 0:128]), r=[psk(pk)], w=["vtok"])
            if i == 16:
                P.op("dve", lambda e: e.tensor_copy(out=vspf[:], in_=ps[0:NSPEC, 0:128]), r=[psk(pk)], w=["vspf"])
        proj_T(ws, cons_v)
        chk("h3")
        P.op("pool", lambda e: e.memset(S32[:], 0.0), w=["S32"])
        P.op("pool", lambda e: e.memset(Sbf[:], 0.0), w=["Sbf"])
        chunks = [("m", 0)] + [("r", c) for c in range(32)]
        kslot = 0
        for ci, (kind, c) in enumerate(chunks):
            if kind == "m":
                c0, n, tile, rows, par = C_META, 16, 16, NSPEC, 2
            else:
                c0, n, tile, rows, par = 64 * c, 64, c // 2, 128, c % 2
            tc0 = 128 * tile
            a = ci % 2
            P.op("pe", lambda e, c0=c0, n=n, tc0=tc0, rows=rows: e.matmul(
                PS[4][0:rows, 0:n], lhsT=nkeT[:, tc0:tc0 + rows], rhs=qeT[:, c0:c0 + n], start=True, stop=True),
                r=["nkeT", "qeT"], w=[psk(4)])
            if kind == "m":
                P.op("dve", lambda e, a=a: e.tensor_tensor(out=ATm[a][0:NSPEC, 0:16], in0=PS[4][0:NSPEC, 0:16], in1=maskAm[:], op=ALU.mult),
                     r=[psk(4), "maskAm"], w=[f"ATm{a}"])
            else:
                P.op("dve", lambda e, a=a, par=par: e.tensor_tensor(out=ATm[a][:, 0:64], in0=PS[4][:, 0:64], in1=maskA[:, par, :], op=ALU.mult),
                     r=[psk(4), "maskA"], w=[f"ATm{a}"])
            if kind == "m" or par == 0:
                pTk = PS[5][:].bitcast(BF16)
                P.op("pe", lambda e, tc0=tc0, rows=rows, pTk=pTk: e.transpose(out=pTk[0:rows, 0:128], in_=nkeT[:, tc0:tc0 + rows],
                                                                              identity=identb[:]), r=["nkeT", "identb"], w=[psk(5)])
                if kind == "m":
                    ks_m = kslot % 4
                    kslot += 1
                    P.op("act", lambda e, ks_m=ks_m, pTk=pTk: e.activation(out=ktok[ks_m][0:NSPEC, :], in_=pTk[0:NSPEC, 0:128], func=AF.Copy,
                                                                           scale=mcol[0:NSPEC, 2:3]), r=[psk(5), "mcol"], w=[f"ktok{ks_m}"])
                    kt_cur = {2: ks_m}
                else:
                    kt_cur = {}
                    for pp in range(2):
                        ks_p = kslot % 4
                        kslot += 1
                        P.op("act", lambda e, ks_p=ks_p, pp=pp, pTk=pTk: e.activation(out=ktok[ks_p][:], in_=pTk[:, 0:128], func=AF.Copy,
                                                                                      scale=mcol[:, pp:pp + 1]), r=[psk(5), "mcol"], w=[f"ktok{ks_p}"])
                        kt_cur[pp] = ks_p
            ksl = kt_cur[par]
            P.op("pe", lambda e, c0=c0, n=n: e.matmul(PS[6][:, 0:n], lhsT=Sbf[:], rhs=qeT[:, c0:c0 + n], start=True, stop=False),
                 r=["Sbf", "qeT"], w=[psk(6)])
            P.op("pe", lambda e, a=a, n=n, tile=tile, rows=rows: e.matmul(PS[6][:, 0:n], lhsT=vtok[0:rows, tile, :], rhs=ATm[a][0:rows, 0:n],
                                                                          start=False, stop=True), r=["vtok", f"ATm{a}"], w=[psk(6)])
            P.op("act", lambda e, c0=c0, n=n: e.copy(out=L1[:, c0:c0 + n], in_=PS[6][:, 0:n]), r=[psk(6)], w=["L1o"])
            P.op("pe", lambda e, ksl=ksl, tile=tile, rows=rows: e.matmul(PS[7][:, 0:128], lhsT=ktok[ksl][0:rows, :], rhs=vtok[0:rows, tile, :],
                                                                         start=True, stop=True), r=[f"ktok{ksl}", "vtok"], w=[psk(7)])
            ebc = fbuf[:, c0 + n - 1:c0 + n]
            P.op("dve", lambda e, ebc=ebc: e.tensor_scalar(out=Stmp[:], in0=PS[7][:, 0:128], scalar1=ebc, scalar2=None, op0=ALU.mult),
                 r=[psk(7), "fbuf"], w=["Stmp"])
            P.op("dve", lambda e, ebc=ebc: e.scalar_tensor_tensor(out=S32[:], in0=S32[:], scalar=ebc, in1=Stmp[:], op0=ALU.mult, op1=ALU.add),
                 r=["S32", "Stmp", "fbuf"], w=["S32"])
            P.op("act", lambda e: e.copy(out=Sbf[:], in_=S32[:]), r=["S32"], w=["Sbf"])
            pump(2)
        P.dma("sp", lambda e, h=h: e.dma_start(out=sp_d[h], in_=S32[:]), r=["S32"], w=["sp_out"], chan="spout")
        chk("h4")
        for s in range(NDEC):
            sl = s % 2
            P.dma("sp", lambda e, s=s, h=h, sl=sl: e.dma_start(out=S0t[sl][:], in_=sh_d[s, h]), w=[f"S0t{sl}"], chan=f"s0t{sl}")
            P.op("pe", lambda e, s=s: e.matmul(PS[4][:, 0:128], lhsT=selv[:, s, :], rhs=vspf[:], start=True, stop=True),
                 r=["selv", "vspf"], w=[psk(4)])
            P.op("dve", lambda e, s=s, sl=sl: e.tensor_scalar(out=Stmp[:], in0=S0t[sl][:], scalar1=fbuf[:, C_DEC + s:C_DEC + s + 1], scalar2=None,
                                                              op0=ALU.mult), r=[f"S0t{sl}", "fbuf"], w=["Stmp"])
            P.op("dve", lambda e, s=s, sl=sl: e.scalar_tensor_tensor(out=Snt[sl][:], in0=PS[4][:, 0:128], scalar=kdec[:, s:s + 1], in1=Stmp[:],
                                                                     op0=ALU.mult, op1=ALU.add), r=[psk(4), "kdec", "Stmp"], w=[f"Snt{sl}"])
            P.dma("sp", lambda e, s=s, h=h, sl=sl: e.dma_start(out=ss_d[s, h], in_=Snt[sl][:]), r=[f"Snt{sl}"], w=["ss_out"], chan=f"snt{sl}")
            P.op("pe", lambda e, s=s, sl=sl: e.matmul(PS[5][:, 0:1], lhsT=Snt[sl][:], rhs=qdec[:, s:s + 1], start=True, stop=True),
                 r=[f"Snt{sl}", "qdec"], w=[psk(5)])
            P.op("act", lambda e, s=s: e.copy(out=L1[:, C_DEC + s:C_DEC + s + 1], in_=PS[5][:, 0:1]), r=[psk(5)], w=["L1o"])
            pump(1)
        chk("h5")
        for (t0, n) in RANGES[:4]:
            norm_gate_cols(L1[:, t0:t0 + n], "L1o", n, hgng[:, 0:1], "hgng", mixedT[:, h, t0:t0 + n], ("mixedT", h), 4)
        norm_gate_cols(L1[:, C_DEC:C_DEC + NDEC], "L1o", NDEC, hgng[:, 0:1], "hgng", mixedT[:, h, NR:NR + NDEC], ("mixedT", h), 4)
        chk("h6")
        ws = load_w(512 * 3 + 128 * h)

        def cons_g(ps, pk, t0, n, h=h):
            gs = 0
            if t0 == C_META:
                P.op("act", lambda e: e.activation(out=gtmp[gs][:, 0:NDEC], in_=ps[:, NMETA:NSPEC], func=AF.Silu), r=[psk(pk)], w=[f"gtmp{gs}"])
                P.op("dve", lambda e: e.tensor_tensor(out=mixedT[:, h, NR:NR + NDEC], in0=mixedT[:, h, NR:NR + NDEC], in1=gtmp[gs][:, 0:NDEC],
                                                      op=ALU.mult), r=[f"gtmp{gs}", ("mixedT", h)], w=[("mixedT", h)])
            else:
                P.op("act", lambda e: e.activation(out=gtmp[gs][:, 0:n], in_=ps[:, 0:n], func=AF.Silu), r=[psk(pk)], w=[f"gtmp{gs}"])
                P.op("dve", lambda e: e.tensor_tensor(out=mixedT[:, h, t0:t0 + n], in0=mixedT[:, h, t0:t0 + n], in1=gtmp[gs][:, 0:n],
                                                      op=ALU.mult), r=[f"gtmp{gs}", ("mixedT", h)], w=[("mixedT", h)])
        proj_F(ws, cons_g)
    P.flush(barrier=True)
    es_a2.close()
    chk("a2")
    es_a3 = ExitStackT()
    sb = mk(es_a3)

    QT = sb("QT", [128, NR], BF16)
    K0T = sb("K0T", [128, NR + NMETA], BF16)
    K1T = sb("K1T", [128, NR + NMETA], BF16)
    Vtk = sb("Vtk", [128, 17, 128], BF16)
    gate = sb("gate", [128, NR], F32)
    kstage = [sb(f"kstage{s}", [128, 128], F32) for s in range(2)]
    vstage = [sb(f"vstage{s}", [128, 128], F32) for s in range(2)]
    Et = [[sb(f"Et{c}{s}", [128, 512], BF16) for s in range(2)] for c in range(2)]
    rz = [sb(f"rz{c}", [128, 512], F32) for c in range(2)]
    tat = [sb(f"tat{c}", [128, 512], F32) for c in range(2)]
    oat = sb("oat", [128, 512], F32)

    P.op("pool", lambda e: e.memset(K0T[:], 0.0), w=["K0T"])
    P.op("pool", lambda e: e.memset(K1T[:], 0.0), w=["K1T"])

    def out_rows(i):
        return (NMETA + 128 * i, 128)

    for h in range(4):
        ws = load_w(512 * 5 + 128 * h)

        def cons_kF(ps, pk, t0, n):
            nn = n if t0 != C_META else NMETA
            P.op("act", lambda e: e.copy(out=K0T[0:64, t0:t0 + nn], in_=ps[0:64, 0:nn]), r=[psk(pk)], w=["K0T"])
            P.op("dve", lambda e: e.tensor_copy(out=K1T[64:128, t0:t0 + nn], in_=ps[64:128, 0:nn]), r=[psk(pk)], w=["K1T"])
        proj_F(ws, cons_kF)
        sctr = [0]

        def cons_kT(ps, pk, i, rows, h=h):
            s = sctr[0] % 2
            sctr[0] += 1
            P.op("act", lambda e: e.copy(out=kstage[s][0:rows, :], in_=ps[0:rows, 0:128]), r=[psk(pk)], w=[f"kstage{s}"])
            if i < 16:
                r0, _ = out_rows(i)
                P.dma("sp", lambda e: e.dma_start(out=kp_d[r0:r0 + 128, 128 * h:128 * h + 128], in_=kstage[s][:]),
                      r=[f"kstage{s}"], w=["kp_out"], chan=f"kstage{s}")
            else:
                P.dma("sp", lambda e: e.dma_start(out=kp_d[0:NMETA, 128 * h:128 * h + 128], in_=kstage[s][0:NMETA, :]),
                      r=[f"kstage{s}"], w=["kp_out"], chan=f"kstage{s}")
        proj_T(ws, cons_kT)
        ws = load_w(512 * 6 + 128 * h)
        sctr2 = [0]

        def cons_vT(ps, pk, i, rows, h=h):
            s = sctr2[0] % 2
            sctr2[0] += 1
            P.op("act", lambda e: e.copy(out=vstage[s][0:rows, :], in_=ps[0:rows, 0:128]), r=[psk(pk)], w=[f"vstage{s}"])
            P.op("dve", lambda e: e.tensor_copy(out=Vtk[0:rows, i, :], in_=ps[0:rows, 0:128]), r=[psk(pk)], w=["Vtk"])
            if i < 16:
                r0, _ = out_rows(i)
                P.dma("sp", lambda e: e.dma_start(out=vp_d[r0:r0 + 128, 128 * h:128 * h + 128], in_=vstage[s][:]),
                      r=[f"vstage{s}"], w=["vp_out"], chan=f"vstage{s}")
            else:
                P.dma("sp", lambda e: e.dma_start(out=vp_d[0:NMETA, 128 * h:128 * h + 128], in_=vstage[s][0:NMETA, :]),
                      r=[f"vstage{s}"], w=["vp_out"], chan=f"vstage{s}")
        proj_T(ws, cons_vT)
        ws = load_w(512 * 4 + 128 * h)

        def cons_qF(ps, pk, t0, n):
            P.op("act", lambda e: e.mul(out=QT[:, t0:t0 + n], in_=ps[:, 0:n], mul=0.125), r=[psk(pk)], w=["QT"])
        proj_F(ws, cons_qF, ranges=RANGES[:4])

        ws = load_w(512 * 7 + 128 * h)

        def cons_gF(ps, pk, t0, n, h=h):
            if t0 == C_META:
                P.op("act", lambda e: e.activation(out=gdec[:, h, :], in_=ps[:, NMETA:NSPEC], func=AF.Silu), r=[psk(pk)], w=["gdec"])
            else:
                P.op("act", lambda e: e.activation(out=gate[:, t0:t0 + n], in_=ps[:, 0:n], func=AF.Silu), r=[psk(pk)], w=["gate"])
        proj_F(ws, cons_gF)

        ectr = 0
        for r in range(4):
            q0 = 512 * r
            ktiles = [("m", 0)] + [("r", j) for j in range(4 * r + 4)]
            for ti, (kind, j) in enumerate(ktiles):
                if kind == "m":
                    kc0, kk, vt, lo = C_META, NMETA, 16, 0
                else:
                    kc0, kk, vt = 128 * j, 128, j
                    lo = 128 * max(0, j - 4 * r)
                n = 512 - lo
                sl = ectr % 2
                ectr += 1
                first = (ti == 0)
                last = (ti == len(ktiles) - 1)
                for c in range(2):
                    KcT = K0T if c == 0 else K1T
                    kname = "K0T" if c == 0 else "K1T"
                    pk = 2 + c
                    extra = []
                    if kind == "m":
                        if r == 0:
                            extra.append((0, 1))
                    else:
                        if j >= 4 * r:
                            extra.append((0, 0))
                            if n >= 256:
                                extra.append((128, 1))
                        elif j == 4 * r - 1:
                            extra.append((0, 1))
                    P.op("pe", lambda e, pk=pk, KcT=KcT, kc0=kc0, kk=kk, q0=q0, lo=lo, n=n, ne=len(extra): e.matmul(
                        PS[pk][0:kk, 0:n], lhsT=KcT[:, kc0:kc0 + kk], rhs=QT[:, q0 + lo:q0 + 512], start=True, stop=(ne == 0)),
                        r=[kname, "QT"], w=[psk(pk)])
                    for xi, (co, oi) in enumerate(extra):
                        if kind == "m":
                            P.op("pe", lambda e, pk=pk, co=co, oi=oi, xi=xi, ne=len(extra), h=h: e.matmul(
                                PS[pk][0:NMETA, co:co + 128], lhsT=J16b[:], rhs=Hm[0:NMETA, h, oi, :], start=False, stop=(xi == ne - 1)),
                                r=["J16b", "Hm"], w=[psk(pk)])
                        else:
                            P.op("pe", lambda e, pk=pk, co=co, oi=oi, xi=xi, ne=len(extra), h=h: e.matmul(
                                PS[pk][:, co:co + 128], lhsT=Jb[:], rhs=Hm[:, h, oi, :], start=False, stop=(xi == ne - 1)),
                                r=["Jb", "Hm"], w=[psk(pk)])
                    P.op("act", lambda e, pk=pk, c=c, sl=sl, kk=kk, n=n, h=h: e.activation(
                        out=Et[c][sl][0:kk, 0:n], in_=PS[pk][0:kk, 0:n], func=AF.Exp, bias=b31t[0:kk, h:h + 1]),
                        r=[psk(pk), "b31t"], w=[f"Et{c}{sl}"])
                for c in range(2):
                    P.op("pe", lambda e, c=c, sl=sl, kk=kk, n=n, lo=lo, vt=vt, first=first, last=last: e.matmul(
                        PS[4 + c][:, lo:512], lhsT=Vtk[0:kk, vt, :], rhs=Et[c][sl][0:kk, 0:n], start=first, stop=last),
                        r=["Vtk", f"Et{c}{sl}"], w=[psk(4 + c)])
                    P.op("pe", lambda e, c=c, sl=sl, kk=kk, n=n, lo=lo, first=first, last=last: e.matmul(
                        PS[6 + c][:, lo:512], lhsT=onesb[0:kk, :], rhs=Et[c][sl][0:kk, 0:n], start=first, stop=last),
                        r=["onesb", f"Et{c}{sl}"], w=[psk(6 + c)])
                pump(2)
            for c in range(2):
                P.op("dve", lambda e, c=c: e.reciprocal(out=rz[c][:], in_=PS[6 + c][:]), r=[psk(6 + c)], w=[f"rz{c}"])
                P.op("dve", lambda e, c=c: e.tensor_tensor(out=tat[c][:], in0=PS[4 + c][:], in1=rz[c][:], op=ALU.mult),
                     r=[psk(4 + c), f"rz{c}"], w=[f"tat{c}"])
            P.op("dve", lambda e: e.scalar_tensor_tensor(out=oat[:], in0=tat[1][:], scalar=lamt[:, 1:2], in1=tat[0][:], op0=ALU.mult, op1=ALU.add),
                 r=["tat0", "tat1", "lamt"], w=["oat"])
            norm_gate_cols(oat[:], "oat", 512, cn[:, 0:1], "cn", tat[0][:], "tat0", 2)
            P.op("dve", lambda e, q0=q0, h=h: e.tensor_tensor(out=mixedT[:, 4 + h, q0:q0 + 512], in0=tat[0][:], in1=gate[:, q0:q0 + 512], op=ALU.mult),
                 r=["tat0", "gate"], w=[("mixedT", 4 + h)])
    pump(100000)
    od2 = odec[:].rearrange("p h s -> p (h s)")
    norm_gate_cols(od2, "odec", 32, cn[:, 0:1], "cn", prod[:, 0:32], "prodb0", 4)
    P.op("dve", lambda e: e.tensor_tensor(out=mixedT[:, 4:8, NR:NR + NDEC], in0=prod[:, 0:32].rearrange("p (h s) -> p h s", h=4),
                                          in1=gdec[:], op=ALU.mult), r=["prodb0", "gdec"], w=[("mixedT", k) for k in range(4, 8)])
    P.flush(barrier=True)
    es_a3.close()
    es_a.close()
    chk("a3")
    es_b.close()

    snd3 = snd_d.rearrange("(c p) n -> p c n", p=128)
    MK = [("mixedT", k) for k in range(8)]
    for hf in range(2):
        P.dma("sp", lambda e, hf=hf: e.dma_start(out=snd3[:, :, 1028 * hf:1028 * hf + 1024], in_=mixedT[:, :, 1024 * hf:1024 * hf + 1024]),
              r=MK, w=["snd"], chan="snd", wait_all=True)
        P.dma("sp", lambda e, hf=hf: e.dma_start(out=snd3[:, :, 1028 * hf + 1024:1028 * hf + 1028], in_=mixedT[:, :, NR + 4 * hf:NR + 4 * hf + 4]),
              r=MK, w=["snd"], chan="snd", wait_all=True)
    P.flush(barrier=True)
    chk("c1")
    es_m.close()
    es_c = ExitStackT()
    sb = mk(es_c)
    mxin = sb("mxin", [128, 16, 1028], BF16)
    ypre = sb("ypre", [128, 9, D], F32)
    fng_bc = sb("fng_bc", [128, D], F32)
    wob = [sb(f"wob{s}", [128, 16, 512], BF16) for s in range(2)]
    st2 = sb("st2", [128, 9, 4], F32)
    junk2 = sb("junk2", [128, D], BF16)
    ccsem = nc.alloc_semaphore(name="ccsem")
    with nc.Block() as block:
        @block.gpsimd
        def _(g):
            for i in range(8):
                g.collective_compute("AllGather", ALU.bypass, replica_groups=[[0, 1], [2, 3], [4, 5], [6, 7]],
                                     ins=[snd_d[128 * i:128 * i + 128, :]], outs=[rcv_ds[i]]).then_inc(ccsem, 1)
            g.wait_ge(ccsem, 8)
    chk("c2")
    P.dma("sp", lambda e: e.dma_start(out=fng_bc[:], in_=fng_d.partition_broadcast(128)), w=["fng_bc"], chan="fng")
    for cc in range(16):
        rk, ch = cc // 8, cc % 8
        rcv_rows = rcv_ds[ch].rearrange("r (t n) -> (r t) n", t=2)
        P.dma("pool", lambda e, cc=cc, rk=rk, rcv_rows=rcv_rows: e.indirect_dma_start(
            out=mxin[:, cc, :], out_offset=None, in_=rcv_rows,
            in_offset=bass.IndirectOffsetOnAxis(ap=idxm[:, rk:rk + 1], axis=0)), r=["idxm"], w=["mxin"], chan="mxin", wait_all=True)
    for tt in range(9):
        rows = 128 if tt < 8 else 4
        P.dma("sp", lambda e, tt=tt, rows=rows: e.dma_start(out=ypre[0:rows, tt, :], in_=Xres[128 * tt:128 * tt + rows, :]),
              w=[("ypre", tt)], chan="xres", wait_all=True)
    chk("c3")
    for g in range(4):
        s = g % 2
        src = Wout[:, 512 * g:512 * g + 512].rearrange("(cc p) n -> p cc n", p=128)
        P.dma("pool", lambda e, s=s, src=src: e.dma_start(out=wob[s][:], in_=src), w=[f"wob{s}"], chan=f"wob{s}")
        for tt in range(9):
            rows = 128 if tt < 8 else 4
            t0 = 128 * tt
            pk = tt % 2
            for cc in range(16):
                P.op("pe", lambda e, pk=pk, cc=cc, t0=t0, rows=rows, s=s: e.matmul(
                    PS[pk][0:rows, 0:512], lhsT=mxin[:, cc, t0:t0 + rows], rhs=wob[s][:, cc, :], start=(cc == 0), stop=(cc == 15)),
                    r=["mxin", f"wob{s}"], w=[psk(pk)])
            P.op("dve", lambda e, pk=pk, tt=tt, rows=rows, g=g: e.tensor_tensor(
                out=ypre[0:rows, tt, 512 * g:512 * g + 512], in0=ypre[0:rows, tt, 512 * g:512 * g + 512], in1=PS[pk][0:rows, 0:512], op=ALU.add),
                r=[psk(pk), ("ypre", tt)], w=[("ypre", tt)])
    for tt in range(9):
        rows = 128 if tt < 8 else 4
        P.op("act", lambda e, tt=tt, rows=rows: e.activation(out=junk2[0:rows, :], in_=ypre[0:rows, tt, :], func=AF.Square,
                                                             accum_out=st2[0:rows, tt, 0:1]), r=[("ypre", tt)], w=["junk2", ("st2", tt)])
        P.op("dve", lambda e, tt=tt, rows=rows: e.tensor_scalar(out=st2[0:rows, tt, 1:2], in0=st2[0:rows, tt, 0:1], scalar1=1.0 / D, scalar2=EPS,
                                                                op0=ALU.mult, op1=ALU.add), r=[("st2", tt)], w=[("st2", tt)])
        P.op("act", lambda e, tt=tt, rows=rows: e.activation(out=st2[0:rows, tt, 2:3], in_=st2[0:rows, tt, 1:2], func=AF.Sqrt),
             r=[("st2", tt)], w=[("st2", tt)])
        P.op("dve", lambda e, tt=tt, rows=rows: e.reciprocal(out=st2[0:rows, tt, 3:4], in_=st2[0:rows, tt, 2:3]), r=[("st2", tt)], w=[("st2", tt)])
        P.op("dve", lambda e, tt=tt, rows=rows: e.scalar_tensor_tensor(out=ypre[0:rows, tt, :], in0=ypre[0:rows, tt, :], scalar=st2[0:rows, tt, 3:4],
                                                                       in1=fng_bc[0:rows, :], op0=ALU.mult, op1=ALU.mult),
             r=[("ypre", tt), ("st2", tt), "fng_bc"], w=[("ypre", tt)])
        if tt < 8:
            P.dma("sp", lambda e, tt=tt: e.dma_start(out=yp_d[128 * tt:128 * tt + 128, :], in_=ypre[:, tt, :]), r=[("ypre", tt)], w=["yp_out"],
                  chan="yout", wait_all=True)
        else:
            P.dma("sp", lambda e, tt=tt: e.dma_start(out=ys_d[:, :], in_=ypre[0:4, tt, :]), r=[("ypre", tt)], w=["yp_out"],
                  chan="yout", wait_all=True)
    P.flush(barrier=True)
    es_c.close()
    es_p.close()


def _rel_bucket(n):
    n = np.asarray(n, dtype=np.int64)
    nl = np.maximum(n, 16).astype(np.float32)
    large = 16 + (np.log(nl / np.float32(16)) / np.float32(math.log(128 / 16)) * np.float32(16)).astype(np.int32)
    large = np.minimum(large, 31)
    return np.where(n < 16, n, large)


def _constants():
    c = {}
    sq = np.zeros((128, 3, 128), np.float32)
    sq[:, 0, :] = np.eye(128)
    sq[:, 1, :] = np.eye(128)[::-1]
    c["c_sq"] = sq
    c["c_j16"] = np.eye(16, dtype=np.float32)[::-1].copy()
    tri = (np.arange(64)[None, :] >= np.arange(64)[:, None]).astype(np.float32)
    mA = np.zeros((128, 2, 64), np.float32)
    mA[0:64, 0, :] = -tri
    mA[64:128, 1, :] = -tri
    c["c_maskA"] = mA
    mAm = np.zeros((NSPEC, 16), np.float32)
    mAm[0:16, :] = -(np.arange(16)[None, :] >= np.arange(16)[:, None]).astype(np.float32)
    c["c_maskAm"] = mAm
    mc = np.zeros((128, 4), np.float32)
    mc[0:64, 0] = -1.0
    mc[64:128, 1] = -1.0
    mc[0:16, 2] = -1.0
    mc[:, 3] = np.arange(128)
    c["c_mcol"] = mc
    rm = np.ones((1, NT), np.float32)
    rm[0, 0:NR:64] = 0.0
    rm[0, C_META] = 0.0
    rm[0, C_DEC:] = 0.0
    c["c_resetm"] = rm
    oh2 = np.zeros((33, 384), np.float32)
    for j in range(384):
        dist = j - 127
        if dist < 0:
            oh2[32, j] = NEG
        else:
            oh2[int(_rel_bucket(dist)), j] += 1.0
            oh2[31, j] -= 1.0
    c["c_oh2"] = oh2
    ohd = np.zeros((33, 3, 128), np.float32)
    ohd[31, 0, :] = 1.0
    for tok in range(128):
        ohd[int(_rel_bucket(128 - tok)), 1, tok] = 1.0
    ohd[0, 2, 0] = 1.0
    ohd[32, 2, 1:] = NEG
    c["c_ohd"] = ohd
    selv = np.zeros((NSPEC, NDEC, 128), np.float32)
    for s in range(NDEC):
        selv[NMETA + s, s, :] = 1.0
    c["c_selv"] = selv
    p01 = np.zeros((8, 8), np.float32)
    for h in range(4):
        p01[2 * h, h] = 1.0
        p01[2 * h + 1, 4 + h] = 1.0
    c["c_p01"] = p01
    return c


_CACHE = {}


def kernel(x_prompt, x_sample, cache_k, cache_v, state_hgrn, page_table, meta_tokens, rel_bias,
           hgrn_lb, norm_g, w_in, hgrn_norm_g, diff_norm_g, diff_lambda, w_out, final_norm_g):
    f32 = np.float32
    x_prompt = np.asarray(x_prompt, f32)
    x_sample = np.asarray(x_sample, f32)
    cache_k = np.asarray(cache_k, f32)
    cache_v = np.asarray(cache_v, f32)
    state_hgrn = np.asarray(state_hgrn, f32)
    page_table = np.asarray(page_table, np.int32)
    meta_tokens = np.asarray(meta_tokens, f32)
    rel_bias = np.asarray(rel_bias, f32)
    hgrn_lb = np.asarray(hgrn_lb, f32)
    norm_g = np.asarray(norm_g, f32)
    w_in = np.asarray(w_in, f32)
    w_out = np.asarray(w_out, f32)
    hgrn_norm_g = np.asarray(hgrn_norm_g, f32)
    diff_norm_g = np.asarray(diff_norm_g, f32)
    diff_lambda = np.asarray(diff_lambda, f32)
    final_norm_g = np.asarray(final_norm_g, f32)

    if "nc" not in _CACHE:
        _CACHE["nc"] = build_program()
    nc = _CACHE["nc"]
    consts = _constants()
    perm = np.concatenate([np.arange(0, 512), np.arange(1024, 1536), np.arange(512, 1024), np.arange(1536, 2048)])
    wout_p = np.ascontiguousarray(w_out[0][perm, :])
    ck_half = [np.ascontiguousarray(cache_k[0][:, :, 4 * j:4 * j + 4, :]).reshape(NPHYS * 128, 512) for j in range(2)]
    cv_half = [np.ascontiguousarray(cache_v[0][:, :, 4 * j:4 * j + 4, :]).reshape(NPHYS * 128, 512) for j in range(2)]
    win3 = w_in[0].reshape(D, 8, 8, 128)
    win_half = [np.ascontiguousarray(win3[:, :, 4 * j:4 * j + 4, :]).reshape(D, 4096) for j in range(2)]
    in_maps = []
    for c in range(8):
        b, j = c // 2, c % 2
        m = dict(consts)
        m["X"] = np.ascontiguousarray(np.concatenate([x_prompt[b], meta_tokens, x_sample[8 * b:8 * b + 8, 0, :]], 0))
        m["Win"] = win_half[j]
        m["Wout"] = wout_p
        m["Xres"] = np.ascontiguousarray(np.concatenate([x_prompt[b, 1024 * j:1024 * j + 1024],
                                                        x_sample[8 * b + 4 * j:8 * b + 4 * j + 4, 0, :]], 0))
        m["normg"] = norm_g[0].reshape(1, D).copy()
        lb2 = hgrn_lb[:, 512 * j:512 * j + 512].reshape(2, 4, 128)
        m["lbraw"] = np.ascontiguousarray(lb2.transpose(2, 0, 1)).reshape(128, 8)
        m["hgng"] = hgrn_norm_g[0].reshape(128, 1).copy()
        m["dfng"] = diff_norm_g[0].reshape(128, 1).copy()
        m["fng"] = final_norm_g.reshape(1, D).copy()
        m["dlam"] = diff_lambda[0].reshape(1, 256).copy()
        m["rbx"] = np.ascontiguousarray(np.concatenate([rel_bias[:, 4 * j:4 * j + 4], np.ones((1, 4), f32)], 0))
        m["pt"] = np.ascontiguousarray(page_table[8 * b:8 * b + 8].reshape(1, 512))
        m["ck"] = ck_half[j]
        m["cv"] = cv_half[j]
        m["sh"] = np.ascontiguousarray(state_hgrn[0, 8 * b:8 * b + 8, 4 * j:4 * j + 4])
        m["c_idxm"] = ((np.arange(2)[None, :] * 128 + np.arange(128)[:, None]) * 2 + j).astype(np.int32)
        in_maps.append(m)
    res = run_bass_kernel_spmd(nc, in_maps, core_ids=list(range(8)))
    R = res.results
    y_prompt = np.zeros((4, NR, D), f32)
    y_sample = np.zeros((32, 1, D), f32)
    k_prompt = np.zeros((1, 4, NR + NMETA, 8, 128), f32)
    v_prompt = np.zeros((1, 4, NR + NMETA, 8, 128), f32)
    s_prompt = np.zeros((1, 4, 8, 128, 128), f32)
    k_sample = np.zeros((1, 32, 1, 8, 128), f32)
    v_sample = np.zeros((1, 32, 1, 8, 128), f32)
    s_sample = np.zeros((1, 32, 8, 128, 128), f32)
    for c in range(8):
        b, j = c // 2, c % 2
        r = R[c]
        y_prompt[b, 1024 * j:1024 * j + 1024] = r["yp"]
        y_sample[8 * b + 4 * j:8 * b + 4 * j + 4, 0] = r["ys"]
        k_prompt[0, b, :, 4 * j:4 * j + 4, :] = r["kp"].reshape(NR + NMETA, 4, 128)
        v_prompt[0, b, :, 4 * j:4 * j + 4, :] = r["vp"].reshape(NR + NMETA, 4, 128)
        s_prompt[0, b, 4 * j:4 * j + 4] = r["sp"]
        k_sample[0, 8 * b:8 * b + 8, 0, 4 * j:4 * j + 4, :] = r["ks"].reshape(8, 4, 128)
        v_sample[0, 8 * b:8 * b + 8, 0, 4 * j:4 * j + 4, :] = r["vs"].reshape(8, 4, 128)
        s_sample[0, 8 * b:8 * b + 8, 4 * j:4 * j + 4] = r["ss"]
    return (y_prompt, y_sample, k_prompt, v_prompt, s_prompt, k_sample, v_sample, s_sample)
```
